# Optimizing a Trainium2 kernel written in Bass

```python
import jax, jax.numpy as jnp
from jax import lax
import numpy as np


D_MODEL = 1024
BATCH = 16
SEQ = 2048
DEPTH = 4
DEC_BATCH = 2
DEC_SEQ = 16384
PAST_LEN = 128

N_MIXERS = 3
N_MLA_LAYERS = (DEPTH + 2) // 3
N_RET_LAYERS = (DEPTH + 1) // 3
N_POOL_LAYERS = DEPTH // 3
NORM_EPS = 1e-6
ROPE_THETA = 10000.0

MLA_HEADS = 8
MLA_Q_LORA = 384
MLA_KV_LORA = 256
MLA_NOPE = 128
MLA_ROPE = 64
MLA_V = 128
MLA_QK = MLA_NOPE + MLA_ROPE
MLA_SCALE = MLA_QK ** -0.5
Q_BLOCK = 128

RET_HEADS = 4
RET_DK = D_MODEL // RET_HEADS
RET_DV = 2 * D_MODEL // RET_HEADS
RET_CHUNK = 128

POOL_WINDOWS = (2, 4, 8, 16)
N_POOL_GROUPS = len(POOL_WINDOWS)
POOL_GROUP = D_MODEL // N_POOL_GROUPS

D_FF = 4 * D_MODEL

kernel_name = 'hybrid_mla_retention_pool_encoder'


def rmsnorm(x, g):
    x32 = x.astype(jnp.float32)
    r = lax.rsqrt(jnp.mean(x32 * x32, axis=-1, keepdims=True) + NORM_EPS)
    return (x32 * r).astype(x.dtype) * g


def rope(x, theta):
    S, d = x.shape[1], x.shape[-1]
    half = d // 2
    inv = 1.0 / (theta ** (jnp.arange(half, dtype=jnp.float32) * 2.0 / d))
    ang = jnp.arange(S, dtype=jnp.float32)[:, None] * inv[None, :]
    cos = jnp.cos(ang)[:, None, :].astype(x.dtype)
    sin = jnp.sin(ang)[:, None, :].astype(x.dtype)
    x1, x2 = x[..., :half], x[..., half:]
    return jnp.concatenate([x1 * cos - x2 * sin, x1 * sin + x2 * cos], axis=-1)


def mla_mixer(h, w_dq, q_norm, w_uq, w_dkv, kv_norm, w_ukv, w_o):
    B, S, _ = h.shape
    cq = rmsnorm(h @ w_dq, q_norm)
    q = (cq @ w_uq).reshape(B, S, MLA_HEADS, MLA_QK)
    q = jnp.concatenate([q[..., :MLA_NOPE], rope(q[..., MLA_NOPE:], ROPE_THETA)], axis=-1)
    kv_a = h @ w_dkv
    c_kv = rmsnorm(kv_a[..., :MLA_KV_LORA], kv_norm)
    k_pe = rope(kv_a[..., None, MLA_KV_LORA:], ROPE_THETA)
    kv = (c_kv @ w_ukv).reshape(B, S, MLA_HEADS, MLA_NOPE + MLA_V)
    k = jnp.concatenate([kv[..., :MLA_NOPE],
                         jnp.broadcast_to(k_pe, (B, S, MLA_HEADS, MLA_ROPE))], axis=-1)
    v = kv[..., MLA_NOPE:]
    n_blk = S // Q_BLOCK
    qb = q.reshape(B, n_blk, Q_BLOCK, MLA_HEADS, MLA_QK).transpose(1, 0, 2, 3, 4)

    def attend(q_blk):
        s = jnp.einsum('bqhd,bkhd->bhqk', q_blk, k).astype(jnp.float32) * MLA_SCALE
        p = jax.nn.softmax(s, axis=-1).astype(v.dtype)
        return jnp.einsum('bhqk,bkhd->bqhd', p, v)

    o = lax.map(attend, qb)
    o = o.transpose(1, 0, 2, 3, 4).reshape(B, S, MLA_HEADS * MLA_V)
    return o @ w_o


def retention_scan(q, k, v, log_gamma):
    B, S, H, DK = q.shape
    DV = v.shape[-1]
    C = RET_CHUNK
    n = S // C

    def chunks(t):
        return t.reshape(B, n, C, H, t.shape[-1]).transpose(1, 0, 3, 2, 4)

    idx = jnp.arange(C, dtype=jnp.float32)
    diff = idx[:, None] - idx[None, :]
    lg = log_gamma[:, None, None]
    intra = jnp.where(diff >= 0, jnp.exp(jnp.maximum(diff, 0.0) * lg), 0.0).astype(q.dtype)
    q_dec = jnp.exp((idx + 1.0)[None, :] * log_gamma[:, None]).astype(q.dtype)[:, :, None]
    k_dec = jnp.exp((C - 1.0 - idx)[None, :] * log_gamma[:, None]).astype(q.dtype)[:, :, None]
    c_dec = jnp.exp(C * log_gamma).astype(q.dtype)[:, None, None]

    def step(state, blk):
        qc, kc, vc = blk
        att = jnp.einsum('bhid,bhjd->bhij', qc, kc) * intra
        out = (jnp.einsum('bhij,bhjv->bhiv', att, vc)
               + jnp.einsum('bhid,bhdv->bhiv', qc * q_dec, state))
        state = state * c_dec + jnp.einsum('bhjd,bhjv->bhdv', kc * k_dec, vc)
        return state, out

    init = jnp.zeros((B, H, DK, DV), q.dtype)
    _, out = lax.scan(step, init, (chunks(q), chunks(k), chunks(v)))
    return out.transpose(1, 0, 3, 2, 4).reshape(B, S, H, DV)


def retention_mixer(h, w_q, w_k, w_v, w_g, w_o, decay_fwd, decay_bwd):
    B, S, _ = h.shape
    q = rope((h @ w_q).reshape(B, S, RET_HEADS, RET_DK), ROPE_THETA)
    k = rope((h @ w_k).reshape(B, S, RET_HEADS, RET_DK), ROPE_THETA) * (RET_DK ** -0.5)
    v = (h @ w_v).reshape(B, S, RET_HEADS, RET_DV)
    lg_f = -jax.nn.softplus(decay_fwd.astype(jnp.float32))
    lg_b = -jax.nn.softplus(decay_bwd.astype(jnp.float32))
    o_f = retention_scan(q, k, v, lg_f)
    o_b = jnp.flip(retention_scan(jnp.flip(q, 1), jnp.flip(k, 1), jnp.flip(v, 1), lg_b), 1)
    o32 = (o_f + o_b).astype(jnp.float32)
    mu = jnp.mean(o32, axis=-1, keepdims=True)
    var = jnp.mean(jnp.square(o32 - mu), axis=-1, keepdims=True)
    o = ((o32 - mu) * lax.rsqrt(var + NORM_EPS)).astype(h.dtype).reshape(B, S, RET_HEADS * RET_DV)
    gate = jax.nn.silu(h @ w_g)
    return (gate * o) @ w_o


def pool_mixer(h, w_pool, scale):
    B, S, D = h.shape
    hg = h.reshape(B, S, N_POOL_GROUPS, POOL_GROUP)
    cs = jnp.cumsum(hg.astype(jnp.float32), axis=1)
    cs = jnp.concatenate([jnp.zeros((B, 1, N_POOL_GROUPS, POOL_GROUP), jnp.float32), cs], axis=1)
    t = jnp.arange(S)
    outs = []
    for g, w in enumerate(POOL_WINDOWS):
        lo = jnp.clip(t - w // 2, 0, S - 1)
        hi = jnp.clip(t + w // 2 - 1, 0, S - 1)
        csg = cs[:, :, g]
        total = jnp.take(csg, hi + 1, axis=1) - jnp.take(csg, lo, axis=1)
        cnt = (hi - lo + 1).astype(jnp.float32)
        outs.append(total / cnt[None, :, None])
    pooled = jnp.stack(outs, axis=2).astype(h.dtype) - hg
    mixed = jnp.einsum('bsgc,gcd->bsgd', pooled, w_pool).reshape(B, S, D)
    return mixed * scale


def sq_relu_mlp(h, w1, w2):
    return jnp.square(jax.nn.relu(h @ w1)) @ w2


def trunk(x, norm_mix, norm_ffn, mla_w_dq, mla_q_norm, mla_w_uq, mla_w_dkv, mla_kv_norm,
          mla_w_ukv, mla_w_o, ret_w_q, ret_w_k, ret_w_v, ret_w_g, ret_w_o, ret_decay_fwd,
          ret_decay_bwd, pool_w, pool_scale, mlp_w1, mlp_w2, final_norm):
    for i in range(DEPTH):
        kind, j = i % N_MIXERS, i // N_MIXERS
        h = rmsnorm(x, norm_mix[i])
        if kind == 0:
            x = x + mla_mixer(h, mla_w_dq[j], mla_q_norm[j], mla_w_uq[j], mla_w_dkv[j],
                              mla_kv_norm[j], mla_w_ukv[j], mla_w_o[j])
        elif kind == 1:
            x = x + retention_mixer(h, ret_w_q[j], ret_w_k[j], ret_w_v[j], ret_w_g[j],
                                    ret_w_o[j], ret_decay_fwd[j], ret_decay_bwd[j])
        else:
            x = x + pool_mixer(h, pool_w[j], pool_scale[j])
        x = x + sq_relu_mlp(rmsnorm(x, norm_ffn[i]), mlp_w1[i], mlp_w2[i])
    return rmsnorm(x, final_norm)


def setup_inputs(seed: int = 0) -> dict:
    key = jax.random.key(seed)
    ks = jax.random.split(key, 24)
    f32 = jnp.float32

    def nrm(k, shape, fan_in):
        return jax.random.normal(k, shape, f32) * (fan_in ** -0.5)

    def gain(k, shape):
        return 1.0 + 0.02 * jax.random.normal(k, shape, f32)

    target = -np.log(1.0 - 2.0 ** (-5.0 - np.arange(RET_HEADS)))
    raw = jnp.asarray(np.log(np.expm1(target)), dtype=f32)
    return {
        'x_prompt': jax.random.normal(ks[0], (BATCH, SEQ, D_MODEL), f32),
        'x_sample': jax.random.normal(ks[1], (DEC_BATCH, DEC_SEQ, D_MODEL), f32),
        'norm_mix': gain(ks[2], (DEPTH, D_MODEL)),
        'norm_ffn': gain(ks[3], (DEPTH, D_MODEL)),
        'mla_w_dq': nrm(ks[4], (N_MLA_LAYERS, D_MODEL, MLA_Q_LORA), D_MODEL),
        'mla_q_norm': gain(ks[5], (N_MLA_LAYERS, MLA_Q_LORA)),
        'mla_w_uq': nrm(ks[6], (N_MLA_LAYERS, MLA_Q_LORA, MLA_HEADS * MLA_QK), MLA_Q_LORA),
        'mla_w_dkv': nrm(ks[7], (N_MLA_LAYERS, D_MODEL, MLA_KV_LORA + MLA_ROPE), D_MODEL),
        'mla_kv_norm': gain(ks[8], (N_MLA_LAYERS, MLA_KV_LORA)),
        'mla_w_ukv': nrm(ks[9], (N_MLA_LAYERS, MLA_KV_LORA, MLA_HEADS * (MLA_NOPE + MLA_V)), MLA_KV_LORA),
        'mla_w_o': nrm(ks[10], (N_MLA_LAYERS, MLA_HEADS * MLA_V, D_MODEL), MLA_HEADS * MLA_V),
        'ret_w_q': nrm(ks[11], (N_RET_LAYERS, D_MODEL, D_MODEL), D_MODEL),
        'ret_w_k': nrm(ks[12], (N_RET_LAYERS, D_MODEL, D_MODEL), D_MODEL),
        'ret_w_v': nrm(ks[13], (N_RET_LAYERS, D_MODEL, 2 * D_MODEL), D_MODEL),
        'ret_w_g': nrm(ks[14], (N_RET_LAYERS, D_MODEL, 2 * D_MODEL), D_MODEL),
        'ret_w_o': nrm(ks[15], (N_RET_LAYERS, 2 * D_MODEL, D_MODEL), 2 * D_MODEL),
        'ret_decay_fwd': raw[None, :] + 0.05 * jax.random.normal(ks[16], (N_RET_LAYERS, RET_HEADS), f32),
        'ret_decay_bwd': raw[None, :] + 0.05 * jax.random.normal(ks[17], (N_RET_LAYERS, RET_HEADS), f32),
        'pool_w': nrm(ks[18], (N_POOL_LAYERS, N_POOL_GROUPS, POOL_GROUP, POOL_GROUP), POOL_GROUP),
        'pool_scale': gain(ks[19], (N_POOL_LAYERS, D_MODEL)),
        'mlp_w1': nrm(ks[20], (DEPTH, D_MODEL, D_FF), D_MODEL),
        'mlp_w2': nrm(ks[21], (DEPTH, D_FF, D_MODEL), D_FF),
        'final_norm': gain(ks[22], (D_MODEL,)),
    }


def reference(x_prompt, x_sample, norm_mix, norm_ffn, mla_w_dq, mla_q_norm, mla_w_uq,
              mla_w_dkv, mla_kv_norm, mla_w_ukv, mla_w_o, ret_w_q, ret_w_k, ret_w_v, ret_w_g,
              ret_w_o, ret_decay_fwd, ret_decay_bwd, pool_w, pool_scale, mlp_w1, mlp_w2,
              final_norm):
    y_prompt = trunk(x_prompt, norm_mix, norm_ffn, mla_w_dq, mla_q_norm, mla_w_uq, mla_w_dkv,
                     mla_kv_norm, mla_w_ukv, mla_w_o, ret_w_q, ret_w_k, ret_w_v, ret_w_g, ret_w_o,
                     ret_decay_fwd, ret_decay_bwd, pool_w, pool_scale, mlp_w1, mlp_w2, final_norm)
    y_sample = trunk(x_sample, norm_mix, norm_ffn, mla_w_dq, mla_q_norm, mla_w_uq, mla_w_dkv,
                     mla_kv_norm, mla_w_ukv, mla_w_o, ret_w_q, ret_w_k, ret_w_v, ret_w_g, ret_w_o,
                     ret_decay_fwd, ret_decay_bwd, pool_w, pool_scale, mlp_w1, mlp_w2, final_norm)
    return (y_prompt, y_sample)
```

```python
import contextlib
import numpy as np
import ml_dtypes
import concourse.bass as bass
import concourse.mybir as mybir
from concourse.bass_utils import run_bass_kernel_spmd

F32 = mybir.dt.float32
BF16 = mybir.dt.bfloat16
AF = mybir.ActivationFunctionType
ALU = mybir.AluOpType

D = 1024
T = 512
EPS = 1e-6
NH = 8
MLA_SCALE = 192 ** -0.5
KCH = 2048


class Sched:
    ENG = ("sp", "act", "dve", "pool", "pe")

    def __init__(self, nc, stack):
        self.nc = nc
        self.eobj = dict(pe=nc.tensor, act=nc.scalar, dve=nc.vector, pool=nc.gpsimd, sp=nc.sync)
        self.lanes = {"sp": ["sp%d" % i for i in range(8)], "pool": ["pl%d" % i for i in range(6)],
                      "act": ["aq%d" % i for i in range(4)]}
        self.sems = {}
        self.count = {}
        for n in list(self.ENG) + sum(self.lanes.values(), []):
            self.sems[n] = stack.enter_context(nc.semaphore("s_" + n))
            self.count[n] = 0
        self.lane_rr = {k: 0 for k in self.lanes}
        self.pending = {e: [] for e in self.ENG}
        self.seen = {e: {} for e in self.ENG}
        self.lw = {}
        self.rd = {}
        self.nops = 0

    def op(self, e, fn, reads=(), writes=(), dma=False):
        deps = {}

        def add(ev):
            s, v = ev
            if deps.get(s, 0) < v:
                deps[s] = v

        for k in reads:
            ev = self.lw.get(k)
            if ev is not None:
                add(ev)
        for k in writes:
            ev = self.lw.get(k)
            if ev is not None:
                add(ev)
            r = self.rd.get(k)
            if r:
                for s, v in r.items():
                    add((s, v))
        if dma:
            lanes = self.lanes[e]
            ln = lanes[self.lane_rr[e] % len(lanes)]
            self.lane_rr[e] += 1
            if self.count[ln] > 0:
                add((ln, self.count[ln]))
            self.count[ln] += 16
            ev = (ln, self.count[ln])
            inc = (ln, 16)
        else:
            self.count[e] += 1
            ev = (e, self.count[e])
            inc = (e, 1)
        waits = []
        seen = self.seen[e]
        for s, v in deps.items():
            if s == e and e == "pe":
                continue
            if seen.get(s, 0) >= v:
                continue
            seen[s] = v
            waits.append((s, v))
        self.pending[e].append((fn, waits, inc))
        for k in reads:
            r = self.rd.setdefault(k, {})
            if r.get(ev[0], 0) < ev[1]:
                r[ev[0]] = ev[1]
        for k in writes:
            self.lw[k] = ev
            self.rd[k] = {}
        self.nops += 1
        return ev

    def barrier(self):
        for e in self.ENG:
            waits = []
            for s, v in self.count.items():
                if v > 0 and self.seen[e].get(s, 0) < v:
                    self.seen[e][s] = v
                    waits.append((s, v))
            if waits:
                self.pending[e].append((None, waits, None))

    def flush(self):
        with self.nc.Block() as blk:
            for e, deco in (("sp", blk.sync), ("act", blk.scalar), ("dve", blk.vector),
                            ("pool", blk.gpsimd), ("pe", blk.tensor)):
                ops = self.pending[e]
                self.pending[e] = []

                def body(eng, ops=ops):
                    for fn, waits, inc in ops:
                        for s, v in waits:
                            eng.wait_ge(self.sems[s], v)
                        if fn is not None:
                            ins = fn(eng)
                            ins.then_inc(self.sems[inc[0]], inc[1])

                deco(body)


class Ring:
    def __init__(self, name, aps):
        self.name = name
        self.aps = aps
        self.i = 0

    def next(self):
        j = self.i % len(self.aps)
        self.i += 1
        return self.aps[j], (self.name, j)


class Prog:
    def __init__(self, piece_lens, layers=(0, 1, 2, 3), final=True, lmax=16384):
        self.piece_lens = list(piece_lens)
        self.layers = list(layers)
        self.final = final
        self.lmax = lmax
        self.inputs = {}
        self.nc = bass.Bass("TRN2", target_bir_lowering=False)
        self.stack = contextlib.ExitStack()
        self.S = None

    def din(self, name, shape, dt=F32):
        self.inputs[name] = (tuple(shape), np.float32 if dt == F32 else ml_dtypes.bfloat16)
        return self.nc.dram_tensor(name, list(shape), dt, kind="ExternalInput").ap()

    def dout(self, name, shape):
        return self.nc.dram_tensor(name, list(shape), F32, kind="ExternalOutput").ap()

    def dscr(self, name, shape, dt=BF16):
        return self.nc.dram_tensor(name, list(shape), dt, kind="Internal").ap()

    def uniq(self, name):
        self._u = getattr(self, "_u", 0) + 1
        return "%s_u%d" % (name, self._u)

    def sb(self, st, name, shape, dt):
        return st.enter_context(self.nc.sbuf_tensor(self.uniq(name), list(shape), dt))

    def ps(self, st, name, shape, dt):
        return st.enter_context(self.nc.psum_tensor(self.uniq(name), list(shape), dt))

    def dma(self, q, out, in_, reads, writes, **kw):
        self.S.op(q, lambda e: e.dma_start(out=out, in_=in_, **kw), reads, writes, dma=True)

    def mm(self, out, lhsT, rhs, start, stop, reads, writes):
        self.S.op("pe", lambda e: e.matmul(out, lhsT=lhsT, rhs=rhs, start=start, stop=stop), reads, writes)

    def tr(self, out, in_, reads, writes):
        ident = self.ident
        self.S.op("pe", lambda e: e.transpose(out, in_, ident), list(reads) + ["ident"], writes)

    def act(self, out, in_, func, reads, writes, bias=None, scale=None, accum=None):
        kw = {}
        if bias is not None:
            kw["bias"] = bias
        if scale is not None:
            kw["scale"] = scale
        if accum is not None:
            kw["accum_out"] = accum
        self.S.op("act", lambda e: e.activation(out=out, in_=in_, func=func, **kw), reads, writes)

    def tt(self, eng, out, a, b, op, reads, writes):
        self.S.op(eng, lambda e: e.tensor_tensor(out=out, in0=a, in1=b, op=op), reads, writes)

    def ts(self, eng, out, a, s1, s2, op0, op1, reads, writes):
        if op1 is None:
            self.S.op(eng, lambda e: e.tensor_scalar(out=out, in0=a, scalar1=s1, scalar2=None, op0=op0),
                      reads, writes)
        else:
            self.S.op(eng, lambda e: e.tensor_scalar(out=out, in0=a, scalar1=s1, scalar2=s2, op0=op0, op1=op1),
                      reads, writes)

    def stt(self, out, a, s, b, op0, op1, reads, writes):
        self.S.op("dve", lambda e: e.scalar_tensor_tensor(out=out, in0=a, scalar=s, in1=b, op0=op0, op1=op1),
                  reads, writes)

    def cp(self, eng, out, in_, reads, writes):
        if eng == "act":
            self.act(out, in_, AF.Copy, reads, writes)
        else:
            self.S.op(eng, lambda e: e.tensor_copy(out=out, in_=in_), reads, writes)

    def declare(self):
        nc = self.nc
        LM = self.lmax
        self.xin = []
        self.yout = []
        self.X = []
        for pi, L in enumerate(self.piece_lens):
            self.xin.append(self.din("x%d" % pi, [L, D]))
            self.yout.append(self.dout("y%d" % pi, [L, D]))
            self.X.append(self.dscr("X%d" % pi, [L, D], F32))
        self.wspec = {}
        for j in range(2):
            self.wspec["m%d_wdq" % j] = (1024, 384)
            self.wspec["m%d_wdkvc" % j] = (1024, 256)
            self.wspec["m%d_wdkvr" % j] = (1024, 128)
            self.wspec["m%d_wq" % j] = (384, 2048)
            self.wspec["m%d_wukv" % j] = (256, 2048)
            self.wspec["m%d_wo" % j] = (1024, 1024)
        self.wspec["r_wq"] = (1024, 1024)
        self.wspec["r_wk"] = (1024, 1024)
        self.wspec["r_wv"] = (1024, 2048)
        self.wspec["r_wg"] = (1024, 2048)
        self.wspec["r_wo"] = (2048, 1024)
        self.wspec["p_w"] = (1024, 256)
        for i in range(4):
            self.wspec["w1_%d" % i] = (1024, 4096)
            self.wspec["w2_%d" % i] = (4096, 1024)
        self.wspec["ident"] = (128, 128)
        self.wspec["ones"] = (128, 128)
        self.wspec["pool_bt"] = (3 * 4 * 6 * 128, 512)
        self.wf = {}
        self.wb = {}
        for n, shp in self.wspec.items():
            self.wf[n] = self.din(n, shp)
            self.wb[n] = self.dscr("b_" + n, shp, BF16)
        self.c = {}
        for n, shp in [("nmix_col", (128, 32)), ("nffn_col", (128, 32)), ("nmix2_rep", (128, 1024)),
                       ("final_rep", (128, 1024)), ("pscale_rep", (128, 1024)),
                       ("qn_col", (128, 6)), ("kvn_col", (128, 4)), ("decay_rep", (128, 8)),
                       ("ccm", (128, LM)), ("ssm", (128, LM)), ("cosr", (128, LM)), ("sinr", (128, LM)),
                       ("dmask", (128, 256)), ("ind", (128, 256)), ("e12", (128, 1024)), ("pcol", (128, 2)),
                       ("pool_rc", (3 * 4 * 128, 512))]:
            self.c[n] = self.din(n, shp)

    def build(self):
        self.declare()
        with self.stack:
            self.S = Sched(self.nc, self.stack)
            self.identt = self.sb(self.stack, "identt", [128, 128], BF16)
            self.ident = self.identt[:]
            self.onest = self.sb(self.stack, "onest", [128, 128], BF16)
            self.prologue()
            for li in self.layers:
                kind, j = li % 3, li // 3
                first = (li == self.layers[0])
                for pi in range(len(self.piece_lens)):
                    src = self.xin[pi] if first else self.X[pi]
                    sname = ("xin%d" % pi) if first else ("X%d" % pi)
                    if kind == 0:
                        self.mla_layer(li, j, pi, src, sname)
                    elif kind == 1:
                        self.ret_layer(li, pi, src, sname)
                    else:
                        self.pool_layer(li, pi, src, sname)
                for pi in range(len(self.piece_lens)):
                    self.mlp_layer(li, pi)
            if self.final:
                for pi in range(len(self.piece_lens)):
                    self.final_norm(pi)
            else:
                for pi in range(len(self.piece_lens)):
                    for t in range(self.piece_lens[pi] // T):
                        self.dma("sp", self.yout[pi][t * T:(t + 1) * T, :], self.X[pi][t * T:(t + 1) * T, :],
                                 reads=[("X%d" % pi, pi, t)], writes=[("yout", pi, t)])
            self.S.barrier()
            self.S.flush()
        return self.nc

    def prologue(self):
        for n, shp in self.wspec.items():
            K, N = shp
            rows = 512 if N * 512 * 4 <= (8 << 20) else 128
            rows = min(rows, K)
            for r0 in range(0, K, rows):
                self.dma("pool", self.wb[n][r0:r0 + rows, :], self.wf[n][r0:r0 + rows, :],
                         reads=[], writes=[("wb", n)], max_dma_last_dim=4096)
        self.dma("sp", self.identt[:], self.wb["ident"][:, :], reads=[("wb", "ident")], writes=["ident"])
        self.dma("sp", self.onest[:], self.wb["ones"][:, :], reads=[("wb", "ones")], writes=["ones"])

    def phase_begin(self):
        self.S.barrier()
        return contextlib.ExitStack()

    def phase_end(self):
        self.S.barrier()
        self.S.flush()

    def load_w(self, st, name, wname, kchunks, ncols, col0=0, q="sp"):
        t = self.sb(st, name, [128, kchunks, ncols], BF16)
        src = self.wb[wname][0:kchunks * 128, col0:col0 + ncols].rearrange("(c p) n -> p c n", p=128)
        self.dma(q, t[:], src, reads=[("wb", wname)], writes=[name])
        return t

    def load_c(self, st, name, cname, shape, src_ap, q="sp"):
        t = self.sb(st, name, list(shape), F32)
        self.dma(q, t[:], src_ap, reads=[], writes=[name])
        return t

    def front_alloc(self, st, nbuf=2):
        R = {}
        R["xin"] = Ring("f_xin", [self.sb(st, "f_xin%d" % i, [128, 4, D], F32) for i in range(nbuf)])
        R["xn"] = Ring("f_xn", [self.sb(st, "f_xn%d" % i, [128, 4, D], BF16) for i in range(nbuf)])
        R["ss"] = Ring("f_ss", [self.sb(st, "f_ss%d" % i, [128, 4], F32) for i in range(nbuf)])
        R["rs"] = Ring("f_rs", [self.sb(st, "f_rs%d" % i, [128, 4], F32) for i in range(nbuf)])
        R["junk"] = self.sb(st, "f_junk", [128, D], BF16)
        pts = [self.ps(st, "f_pt%d" % i, [128, 1024], BF16) for i in range(2)]
        R["pt"] = Ring("f_pt", [pts[0], pts[1]])
        R["flip"] = 0
        return R

    def front_stats(self, R, src, sname, pi, t, want_xn=True):
        xin, kx = R["xin"].next()
        self.dma("sp", xin[:], src[t * T:(t + 1) * T, :].rearrange("(b p) d -> p b d", p=128),
                 reads=[(sname, pi, t)], writes=[kx])
        ss, kss = R["ss"].next()
        rs, krs = R["rs"].next()
        junk = R["junk"]
        for b in range(4):
            self.act(junk[:], xin[:, b, :], AF.Square, reads=[kx], writes=[(kss, b)], accum=ss[:, b:b + 1])
        self.act(rs[:], ss[:], AF.Ln, reads=[(kss, b) for b in range(4)], writes=[krs], scale=1.0 / D, bias=EPS)
        self.act(rs[:], rs[:], AF.Exp, reads=[krs], writes=[krs], scale=-0.5)
        xn = kxn = None
        if want_xn:
            xn, kxn = R["xn"].next()
            for b in range(4):
                self.ts("dve", xn[:, b, :], xin[:, b, :], rs[:, b:b + 1], None, ALU.mult, None,
                        reads=[kx, krs], writes=[(kxn, b)])
        return xin, kx, rs, krs, xn, kxn

    def front(self, R, src, sname, pi, t, gcol, gkey, hT, hkey):
        xin, kx, rs, krs, xn, kxn = self.front_stats(R, src, sname, pi, t)
        import os
        FD = int(os.environ.get("FDBG", "9"))
        for cpair in range(4):
            if FD < 2:
                break
            pt, kpt = R["pt"].next()
            for cc in range(2):
                c = cpair * 2 + cc
                for b in range(4):
                    self.tr(pt[:, cc * 512 + b * 128:cc * 512 + (b + 1) * 128], xn[:, b, c * 128:(c + 1) * 128],
                            reads=[(kxn, b)], writes=[kpt])
            for cc in range(2):
                if FD < 3:
                    break
                c = cpair * 2 + cc
                if True:
                    self.act(hT[:, c, :], pt[:, cc * 512:(cc + 1) * 512], AF.Copy, reads=[kpt, gkey],
                             writes=[(hkey, c)], scale=gcol[:, c:c + 1])
                else:
                    self.ts("dve", hT[:, c, :], pt[:, cc * 512:(cc + 1) * 512], gcol[:, c:c + 1], None, ALU.mult, None,
                            reads=[kpt, gkey], writes=[(hkey, c)])
        return xin, kx

    def scratch(self, name, shape, dt=BF16):
        if not hasattr(self, "_scr"):
            self._scr = {}
        if name not in self._scr:
            self._scr[name] = self.dscr(name, shape, dt)
        return self._scr[name]

    def mla_layer(self, li, j, pi, src, sname):
        L = self.piece_lens[pi]
        nt = L // T
        nblk = L // 128
        QTn = self.scratch("a_qtn%d" % pi, [NH, 128, L])
        QTr = self.scratch("a_qtr%d" % pi, [NH * 64, L])
        KTn = self.scratch("a_ktn%d" % pi, [NH, 128, L])
        KPE = self.scratch("a_kpe%d" % pi, [64, L])
        VP = self.scratch("a_vp%d" % pi, [NH, 128, nblk, 128])
        OT = self.scratch("a_ot%d" % pi, [NH, 128, L])
        pre = "m%d_" % j

        st = self.phase_begin()
        with st:
            R = self.front_alloc(st)
            wdq = self.load_w(st, "a_wdq", pre + "wdq", 8, 384)
            wdkvc = self.load_w(st, "a_wdkvc", pre + "wdkvc", 8, 256)
            wdkvr = self.load_w(st, "a_wdkvr", pre + "wdkvr", 8, 128)
            wq = self.load_w(st, "a_wq", pre + "wq", 3, 2048)
            wukv = self.load_w(st, "a_wukv", pre + "wukv", 2, 2048)
            gcol = self.load_c(st, "a_gcol", "nmix_col", [128, 8], self.c["nmix_col"][:, li * 8:(li + 1) * 8])
            qncol = self.load_c(st, "a_qn", "qn_col", [128, 3], self.c["qn_col"][:, j * 3:(j + 1) * 3])
            kvncol = self.load_c(st, "a_kvn", "kvn_col", [128, 2], self.c["kvn_col"][:, j * 2:(j + 1) * 2])
            hT = Ring("a_hT", [self.sb(st, "a_hT%d" % i, [128, 8, T], BF16) for i in range(2)])
            cqf = self.sb(st, "a_cqf", [128, 4, 640], F32)
            cqn = self.sb(st, "a_cqn", [128, 4, 640], BF16)
            ss2 = self.sb(st, "a_ss2", [128, 8], F32)
            rs2 = self.sb(st, "a_rs2", [128, 8], F32)
            cT = Ring("a_cT", [self.sb(st, "a_cT%d" % i, [128, 5, T], BF16) for i in range(2)])
            cc = Ring("a_cc", [self.sb(st, "a_cc%d" % i, [128, T], F32) for i in range(2)])
            sn = Ring("a_sn", [self.sb(st, "a_sn%d" % i, [128, T], F32) for i in range(2)])
            ob = Ring("a_ob", [self.sb(st, "a_ob%d" % i, [128, T], BF16) for i in range(6)])
            t1 = Ring("a_t1", [self.sb(st, "a_t1%d" % i, [128, T], F32) for i in range(2)])
            t2 = Ring("a_t2", [self.sb(st, "a_t2%d" % i, [128, T], F32) for i in range(2)])
            vt = Ring("a_vt", [self.sb(st, "a_vt%d" % i, [128, 4, 1024], BF16) for i in range(2)])
            pp = Ring("a_pp", [self.ps(st, "a_pp%d" % i, [128, 512], F32) for i in range(6)])
            junk = R["junk"]
            flip = 0
            for t in range(nt):
                h, hk = hT.next()
                self.front(R, src, sname, pi, t, gcol, "a_gcol", h, hk)
                hreads = [(hk, c) for c in range(8)]
                for b in range(4):
                    pa, kpa = pp.next()
                    for k in range(8):
                        self.mm(pa[:, 0:384], h[:, k, b * 128:(b + 1) * 128], wdq[:, k, :], k == 0, k == 7,
                                reads=hreads + ["a_wdq"], writes=[kpa])
                    self.cp("dve", cqf[:, b, 0:384], pa[:, 0:384], reads=[kpa], writes=[("a_cqf", b, 0)])
                    pb, kpb = pp.next()
                    for k in range(8):
                        self.mm(pb[:, 0:256], h[:, k, b * 128:(b + 1) * 128], wdkvc[:, k, :], k == 0, k == 7,
                                reads=hreads + ["a_wdkvc"], writes=[kpb])
                    self.cp("dve", cqf[:, b, 384:640], pb[:, 0:256], reads=[kpb], writes=[("a_cqf", b, 1)])
                for b in range(4):
                    self.act(junk[:, 0:384], cqf[:, b, 0:384], AF.Square, reads=[("a_cqf", b, 0)],
                             writes=[("a_ss2", b)], accum=ss2[:, b:b + 1])
                    self.act(junk[:, 0:256], cqf[:, b, 384:640], AF.Square, reads=[("a_cqf", b, 1)],
                             writes=[("a_ss2", 4 + b)], accum=ss2[:, 4 + b:5 + b])
                self.act(rs2[:, 0:4], ss2[:, 0:4], AF.Ln, reads=[("a_ss2", b) for b in range(4)],
                         writes=[("a_rs2", 0)], scale=1.0 / 384, bias=EPS)
                self.act(rs2[:, 4:8], ss2[:, 4:8], AF.Ln, reads=[("a_ss2", 4 + b) for b in range(4)],
                         writes=[("a_rs2", 1)], scale=1.0 / 256, bias=EPS)
                self.act(rs2[:], rs2[:], AF.Exp, reads=[("a_rs2", 0), ("a_rs2", 1)], writes=[("a_rs2", 0), ("a_rs2", 1)],
                         scale=-0.5)
                for b in range(4):
                    self.ts("dve", cqn[:, b, 0:384], cqf[:, b, 0:384], rs2[:, b:b + 1], None, ALU.mult, None,
                            reads=[("a_cqf", b, 0), ("a_rs2", 0)], writes=[("a_cqn", b, 0)])
                    self.ts("dve", cqn[:, b, 384:640], cqf[:, b, 384:640], rs2[:, 4 + b:5 + b], None, ALU.mult, None,
                            reads=[("a_cqf", b, 1), ("a_rs2", 1)], writes=[("a_cqn", b, 1)])
                ct, kct = cT.next()
                for cpair in range(3):
                    pt, kpt = R["pt"].next()
                    cs = [c for c in (2 * cpair, 2 * cpair + 1) if c < 5]
                    for c in cs:
                        for b in range(4):
                            self.tr(pt[:, (c % 2) * 512 + b * 128:(c % 2) * 512 + (b + 1) * 128],
                                    cqn[:, b, c * 128:(c + 1) * 128],
                                    reads=[("a_cqn", b, 0 if c < 3 else 1)], writes=[kpt])
                    for c in cs:
                        gc = qncol[:, c:c + 1] if c < 3 else kvncol[:, c - 3:c - 2]
                        gk = "a_qn" if c < 3 else "a_kvn"
                        ptc = pt[:, (c % 2) * 512:(c % 2 + 1) * 512]
                        if True:
                            self.act(ct[:, c, :], ptc, AF.Copy, reads=[kpt, gk], writes=[(kct, c)], scale=gc)
                        else:
                            self.ts("dve", ct[:, c, :], ptc, gc, None, ALU.mult, None, reads=[kpt, gk], writes=[(kct, c)])
                creads_q = [(kct, c) for c in range(3)]
                creads_kv = [(kct, 3), (kct, 4)]
                cct, kcc = cc.next()
                snt, ksn = sn.next()
                self.dma("sp", cct[:], self.c["ccm"][:, t * T:(t + 1) * T], reads=[], writes=[kcc])
                self.dma("sp", snt[:], self.c["ssm"][:, t * T:(t + 1) * T], reads=[], writes=[ksn])
                for hh in range(NH):
                    pq, kpq = pp.next()
                    for k in range(3):
                        self.mm(pq[:], wq[:, k, hh * 128:(hh + 1) * 128], ct[:, k, :], k == 0, k == 2,
                                reads=creads_q + ["a_wq"], writes=[kpq])
                    o, ko = ob.next()
                    eng = "act" if flip % 2 == 0 else "dve"
                    flip += 1
                    self.cp(eng, o[:], pq[:], reads=[kpq], writes=[ko])
                    self.dma("pool", QTn[hh, :, t * T:(t + 1) * T], o[:], reads=[ko], writes=[("a_qtn", pi, hh, t)])
                for hp in range(4):
                    pA, kpA = pp.next()
                    for k in range(3):
                        self.mm(pA[:], wq[:, k, 1024 + hp * 128:1024 + (hp + 1) * 128], ct[:, k, :], k == 0, k == 2,
                                reads=creads_q + ["a_wq"], writes=[kpA])
                    pB, kpB = pp.next()
                    for k in range(3):
                        self.mm(pB[:], wq[:, k, 1536 + hp * 128:1536 + (hp + 1) * 128], ct[:, k, :], k == 0, k == 2,
                                reads=creads_q + ["a_wq"], writes=[kpB])
                    a1, k1 = t1.next()
                    a2, k2 = t2.next()
                    self.tt("dve", a1[:], pA[:], cct[:], ALU.mult, reads=[kpA, kcc], writes=[k1])
                    self.tt("dve", a2[:], pB[:], snt[:], ALU.mult, reads=[kpB, ksn], writes=[k2])
                    o, ko = ob.next()
                    self.tt("pool", o[:], a1[:], a2[:], ALU.add, reads=[k1, k2], writes=[ko])
                    self.dma("pool", QTr[hp * 128:(hp + 1) * 128, t * T:(t + 1) * T], o[:], reads=[ko],
                             writes=[("a_qtr", pi, hp, t)])
                pA, kpA = pp.next()
                for k in range(8):
                    self.mm(pA[0:64, :], wdkvr[:, k, 0:64], h[:, k, :], k == 0, k == 7,
                            reads=hreads + ["a_wdkvr"], writes=[kpA])
                pB, kpB = pp.next()
                for k in range(8):
                    self.mm(pB[0:64, :], wdkvr[:, k, 64:128], h[:, k, :], k == 0, k == 7,
                            reads=hreads + ["a_wdkvr"], writes=[kpB])
                a1, k1 = t1.next()
                a2, k2 = t2.next()
                self.tt("dve", a1[0:64, :], pA[0:64, :], cct[0:64, :], ALU.mult, reads=[kpA, kcc], writes=[k1])
                self.tt("dve", a2[0:64, :], pB[0:64, :], snt[0:64, :], ALU.mult, reads=[kpB, ksn], writes=[k2])
                o, ko = ob.next()
                self.tt("pool", o[0:64, :], a1[0:64, :], a2[0:64, :], ALU.add, reads=[k1, k2], writes=[ko])
                self.dma("pool", KPE[:, t * T:(t + 1) * T], o[0:64, :], reads=[ko], writes=[("a_kpe", pi, t)])
                for hh in range(NH):
                    pq, kpq = pp.next()
                    for k in range(2):
                        self.mm(pq[:], wukv[:, k, hh * 128:(hh + 1) * 128], ct[:, 3 + k, :], k == 0, k == 1,
                                reads=creads_kv + ["a_wukv"], writes=[kpq])
                    o, ko = ob.next()
                    eng = "act" if flip % 2 == 0 else "dve"
                    flip += 1
                    self.cp(eng, o[:], pq[:], reads=[kpq], writes=[ko])
                    self.dma("pool", KTn[hh, :, t * T:(t + 1) * T], o[:], reads=[ko], writes=[("a_ktn", pi, hh, t)])
                v, kv = vt.next()
                for b in range(4):
                    for half in range(2):
                        pq, kpq = pp.next()
                        for k in range(2):
                            self.mm(pq[:], ct[:, 3 + k, b * 128:(b + 1) * 128],
                                    wukv[:, k, 1024 + half * 512:1024 + (half + 1) * 512], k == 0, k == 1,
                                    reads=creads_kv + ["a_wukv"], writes=[kpq])
                        eng = "act" if flip % 2 == 0 else "dve"
                        flip += 1
                        self.cp(eng, v[:, b, half * 512:(half + 1) * 512], pq[:], reads=[kpq], writes=[(kv, b, half)])
                for hh in range(NH):
                    self.dma("pool", VP[hh, :, t * 4:(t + 1) * 4, :], v[:, :, hh * 128:(hh + 1) * 128],
                             reads=[(kv, b, hh // 4) for b in range(4)], writes=[("a_vp", pi, hh, t)])
            self.phase_end()

        st = self.phase_begin()
        with st:
            nkc = max(1, L // KCH)
            kch = min(KCH, L)
            bpc = kch // 128
            qn = Ring("b_qn", [self.sb(st, "b_qn%d" % i, [128, T], BF16) for i in range(2)])
            qr = Ring("b_qr", [self.sb(st, "b_qr%d" % i, [64, T], BF16) for i in range(2)])
            kn = Ring("b_kn", [self.sb(st, "b_kn%d" % i, [128, kch], BF16) for i in range(3)])
            kr = Ring("b_kr", [self.sb(st, "b_kr%d" % i, [64, kch], BF16) for i in range(3)])
            vv = Ring("b_vv", [self.sb(st, "b_vv%d" % i, [128, bpc, 128], BF16) for i in range(3)])
            pT = Ring("b_pT", [self.sb(st, "b_pT%d" % i, [128, T], BF16) for i in range(4)])
            rsum = Ring("b_rs", [self.sb(st, "b_rs%d" % i, [128, T], F32) for i in range(2)])
            oo = Ring("b_oo", [self.sb(st, "b_oo%d" % i, [128, T], BF16) for i in range(2)])
            pss = Ring("b_pss", [self.ps(st, "b_pss%d" % i, [128, 512], F32) for i in range(3)])
            pso = Ring("b_pso", [self.ps(st, "b_pso%d" % i, [128, 512], F32) for i in range(2)])
            psm = Ring("b_psm", [self.ps(st, "b_psm%d" % i, [128, 512], F32) for i in range(2)])
            ones = self.onest
            for hh in range(NH):
                for qt in range(nt):
                    q1, kq1 = qn.next()
                    q2, kq2 = qr.next()
                    self.dma("sp", q1[:], QTn[hh, :, qt * T:(qt + 1) * T], reads=[("a_qtn", pi, hh, qt)], writes=[kq1])
                    self.dma("sp", q2[:], QTr[hh * 64:(hh + 1) * 64, qt * T:(qt + 1) * T],
                             reads=[("a_qtr", pi, hh // 2, qt)], writes=[kq2])
                    po, kpo = pso.next()
                    pm, kpm = psm.next()
                    items = []
                    for kc in range(nkc):
                        items.append(kc)
                    pend = None
                    nblk_tot = nkc * bpc
                    bi = 0
                    for kc in range(nkc):
                        k1, kk1 = kn.next()
                        k2, kk2 = kr.next()
                        v1, kv1 = vv.next()
                        tl = [kc * (kch // T) + x for x in range(max(1, kch // T))]
                        self.dma("sp", k1[:], KTn[hh, :, kc * kch:(kc + 1) * kch],
                                 reads=[("a_ktn", pi, hh, x) for x in tl], writes=[kk1])
                        self.dma("sp", k2[:], KPE[:, kc * kch:(kc + 1) * kch],
                                 reads=[("a_kpe", pi, x) for x in tl], writes=[kk2])
                        self.dma("sp", v1[:], VP[hh, :, kc * bpc:(kc + 1) * bpc, :],
                                 reads=[("a_vp", pi, hh, x) for x in tl], writes=[kv1])
                        for b in range(bpc):
                            s, ks = pss.next()
                            self.mm(s[:], k1[:, b * 128:(b + 1) * 128], q1[:], True, False, reads=[kk1, kq1], writes=[ks])
                            self.mm(s[:], k2[0:64, b * 128:(b + 1) * 128], q2[0:64, :], False, True,
                                    reads=[kk2, kq2], writes=[ks])
                            if pend is not None:
                                self._pv(pend, po, kpo, pm, kpm, ones)
                            p, kp = pT.next()
                            self.act(p[:], s[:], AF.Exp, reads=[ks], writes=[kp], scale=MLA_SCALE)
                            pend = (p, kp, v1[:, b, :], kv1, bi == 0, bi == nblk_tot - 1)
                            bi += 1
                    self._pv(pend, po, kpo, pm, kpm, ones)
                    r, kr_ = rsum.next()
                    self.S.op("dve", lambda e, r=r, pm=pm: e.reciprocal(out=r[:], in_=pm[:]), [kpm], [kr_])
                    o, ko = oo.next()
                    self.tt("dve", o[:], po[:], r[:], ALU.mult, reads=[kpo, kr_], writes=[ko])
                    self.dma("pool", OT[hh, :, qt * T:(qt + 1) * T], o[:], reads=[ko], writes=[("a_ot", pi, hh, qt)])
            self.phase_end()

        st = self.phase_begin()
        with st:
            wo = self.load_w(st, "c_wo", pre + "wo", 8, 1024)
            ot = Ring("c_ot", [self.sb(st, "c_ot%d" % i, [128, 8, T], BF16) for i in range(2)])
            xr = Ring("c_xr", [self.sb(st, "c_xr%d" % i, [128, 4, D], F32) for i in range(2)])
            xo = Ring("c_xo", [self.sb(st, "c_xo%d" % i, [128, 4, D], F32) for i in range(2)])
            pp = Ring("c_pp", [self.ps(st, "c_pp%d" % i, [128, 512], F32) for i in range(4)])
            for t in range(nt):
                o, ko = ot.next()
                for hh in range(NH):
                    self.dma("sp", o[:, hh, :], OT[hh, :, t * T:(t + 1) * T], reads=[("a_ot", pi, hh, t)],
                             writes=[(ko, hh)])
                x, kx = xr.next()
                self.dma("sp", x[:], src[t * T:(t + 1) * T, :].rearrange("(b p) d -> p b d", p=128),
                         reads=[(sname, pi, t)], writes=[kx])
                y, ky = xo.next()
                for b in range(4):
                    for half in range(2):
                        pq, kpq = pp.next()
                        for hh in range(NH):
                            self.mm(pq[:], o[:, hh, b * 128:(b + 1) * 128], wo[:, hh, half * 512:(half + 1) * 512],
                                    hh == 0, hh == NH - 1, reads=[(ko, hh), "c_wo"], writes=[kpq])
                        self.tt("dve", y[:, b, half * 512:(half + 1) * 512], pq[:], x[:, b, half * 512:(half + 1) * 512],
                                ALU.add, reads=[kpq, kx], writes=[(ky, b, half)])
                self.dma("pool", self.X[pi][t * T:(t + 1) * T, :].rearrange("(b p) d -> p b d", p=128), y[:],
                         reads=[(ky, b, hf) for b in range(4) for hf in range(2)], writes=[("X%d" % pi, pi, t)])
            self.phase_end()

    def _pv(self, pend, po, kpo, pm, kpm, ones):
        p, kp, v, kv, first, last = pend
        self.mm(po[:], v, p[:], first, last, reads=[kv, kp], writes=[kpo])
        self.mm(pm[:], ones[:], p[:], first, last, reads=["ones", kp], writes=[kpm])

    def mlp_layer(self, li, pi):
        L = self.piece_lens[pi]
        TS = 1024 if L % 1024 == 0 else 512
        nsup = L // TS
        nsub = TS // T
        nb = TS // 128
        Xp = self.X[pi]
        sname = "X%d" % pi
        st = self.phase_begin()
        with st:
            R = self.front_alloc(st)
            gcol = self.load_c(st, "m_gcol", "nffn_col", [128, 8], self.c["nffn_col"][:, li * 8:(li + 1) * 8])
            hT = self.sb(st, "m_hT", [128, 8, TS], BF16)
            w1 = Ring("m_w1", [self.sb(st, "m_w1%d" % i, [128, 8, 512], BF16) for i in range(3)])
            w2 = Ring("m_w2", [self.sb(st, "m_w2%d" % i, [128, 4, 1024], BF16) for i in range(3)])
            aT = Ring("m_aT", [self.sb(st, "m_aT%d" % i, [128, 4, TS], BF16) for i in range(2)])
            rl = Ring("m_rl", [self.sb(st, "m_rl%d" % i, [128, T], BF16) for i in range(3)])
            yacc = self.sb(st, "m_yacc", [128, nb, D], F32)
            xr = Ring("m_xr", [self.sb(st, "m_xr%d" % i, [128, D], F32) for i in range(2)])
            pa = Ring("m_pa", [self.ps(st, "m_pa%d" % i, [128, 512], F32) for i in range(3)])
            py = Ring("m_py", [self.ps(st, "m_py%d" % i, [128, 512], F32) for i in range(3)])
            w1n, w2n = "w1_%d" % li, "w2_%d" % li
            for s in range(nsup):
                for u in range(nsub):
                    t = s * nsub + u
                    self.front(R, Xp, sname, pi, t, gcol, "m_gcol", hT[:, :, u * T:(u + 1) * T], ("m_hT", u))
                import os
                DBG = int(os.environ.get("KDBG", "9"))
                for g in range(8):
                    if DBG < 2:
                        break
                    a1, ka1 = w1.next()
                    a2, ka2 = w2.next()
                    self.dma("sp", a1[:], self.wb[w1n][:, g * 512:(g + 1) * 512].rearrange("(c p) n -> p c n", p=128),
                             reads=[("wb", w1n)], writes=[ka1])
                    self.dma("sp", a2[:], self.wb[w2n][g * 512:(g + 1) * 512, :].rearrange("(c p) n -> p c n", p=128),
                             reads=[("wb", w2n)], writes=[ka2])
                    at, kat = aT.next()
                    if DBG < 3:
                        continue
                    for jc in range(4):
                        for u in range(nsub):
                            p, kp = pa.next()
                            for k in range(8):
                                self.mm(p[:], a1[:, k, jc * 128:(jc + 1) * 128], hT[:, k, u * T:(u + 1) * T],
                                        k == 0, k == 7, reads=[ka1] + [(("m_hT", u), c) for c in range(8)], writes=[kp])
                            r, kr = rl.next()
                            self.act(r[:], p[:], AF.Relu, reads=[kp], writes=[kr])
                            self.tt("pool", at[:, jc, u * T:(u + 1) * T], r[:], r[:], ALU.mult, reads=[kr],
                                    writes=[(kat, jc, u)])
                    if DBG < 4:
                        continue
                    for b in range(nb):
                        if g == 0:
                            x, kx = xr.next()
                            self.dma("sp", x[:], Xp[s * TS + b * 128:s * TS + (b + 1) * 128, :],
                                     reads=[(sname, pi, (s * TS + b * 128) // T)], writes=[kx])
                        for half in range(2):
                            p, kp = py.next()
                            for jc in range(4):
                                self.mm(p[:], at[:, jc, b * 128:(b + 1) * 128], a2[:, jc, half * 512:(half + 1) * 512],
                                        jc == 0, jc == 3, reads=[ka2, (kat, jc, b // 4)], writes=[kp])
                            ysl = yacc[:, b, half * 512:(half + 1) * 512]
                            if g == 0:
                                self.tt("dve", ysl, p[:], x[:, half * 512:(half + 1) * 512], ALU.add,
                                        reads=[kp, kx], writes=[("m_yacc", b, half)])
                            else:
                                self.tt("dve", ysl, p[:], ysl, ALU.add, reads=[kp, ("m_yacc", b, half)],
                                        writes=[("m_yacc", b, half)])
                for u in range(nsub):
                    t = s * nsub + u
                    self.dma("pool", Xp[t * T:(t + 1) * T, :].rearrange("(b p) d -> p b d", p=128),
                             yacc[:, u * 4:(u + 1) * 4, :],
                             reads=[("m_yacc", b, hf) for b in range(u * 4, u * 4 + 4) for hf in range(2)],
                             writes=[(sname, pi, t)])
            self.phase_end()

    def final_norm(self, pi):
        L = self.piece_lens[pi]
        nt = L // T
        st = self.phase_begin()
        with st:
            R = self.front_alloc(st)
            grep = self.load_c(st, "fn_g", "final_rep", [128, D], self.c["final_rep"][:, :])
            yo = Ring("fn_y", [self.sb(st, "fn_y%d" % i, [128, 4, D], F32) for i in range(2)])
            for t in range(nt):
                xin, kx, rs, krs, _, _ = self.front_stats(R, self.X[pi], "X%d" % pi, pi, t, want_xn=False)
                y, ky = yo.next()
                for b in range(4):
                    self.stt(y[:, b, :], xin[:, b, :], rs[:, b:b + 1], grep[:], ALU.mult, ALU.mult,
                             reads=[kx, krs, "fn_g"], writes=[(ky, b)])
                self.dma("pool", self.yout[pi][t * T:(t + 1) * T, :].rearrange("(b p) d -> p b d", p=128), y[:],
                         reads=[(ky, b) for b in range(4)], writes=[("yout", pi, t)])
            self.phase_end()

    def pool_layer(self, li, pi, src, sname):
        L = self.piece_lens[pi]
        nt = L // T
        nblk = L // 128
        H = self.scratch("p_h%d" % pi, [L, D])
        st = self.phase_begin()
        with st:
            R = self.front_alloc(st)
            grep = self.load_c(st, "pa_g", "nmix2_rep", [128, D], self.c["nmix2_rep"][:, :])
            ho = Ring("pa_h", [self.sb(st, "pa_h%d" % i, [128, 4, D], BF16) for i in range(2)])
            for t in range(nt):
                xin, kx, rs, krs, _, _ = self.front_stats(R, src, sname, pi, t, want_xn=False)
                y, ky = ho.next()
                for b in range(4):
                    self.stt(y[:, b, :], xin[:, b, :], rs[:, b:b + 1], grep[:], ALU.mult, ALU.mult,
                             reads=[kx, krs, "pa_g"], writes=[(ky, b)])
                self.dma("pool", H[t * T:(t + 1) * T, :].rearrange("(b p) d -> p b d", p=128), y[:],
                         reads=[(ky, b) for b in range(4)], writes=[("p_h", pi, t)])
            self.phase_end()
        st = self.phase_begin()
        with st:
            wp = self.load_w(st, "pb_w", "p_w", 8, 256)
            srep = self.load_c(st, "pb_s", "pscale_rep", [128, D], self.c["pscale_rep"][:, :])
            hb = Ring("pb_hb", [self.sb(st, "pb_hb%d" % i, [128, 6, D], BF16) for i in range(2)])
            bt = Ring("pb_bt", [self.sb(st, "pb_bt%d" % i, [128, 24, 512], BF16) for i in range(2)])
            rc = Ring("pb_rc", [self.sb(st, "pb_rc%d" % i, [128, 4, 512], F32) for i in range(2)])
            dT = Ring("pb_dT", [self.sb(st, "pb_dT%d" % i, [128, 8, T], BF16) for i in range(2)])
            xr = Ring("pb_xr", [self.sb(st, "pb_xr%d" % i, [128, 4, D], F32) for i in range(2)])
            tm = Ring("pb_tm", [self.sb(st, "pb_tm%d" % i, [128, 512], F32) for i in range(2)])
            xo = Ring("pb_xo", [self.sb(st, "pb_xo%d" % i, [128, 4, D], F32) for i in range(2)])
            pd = Ring("pb_pd", [self.ps(st, "pb_pd%d" % i, [128, 512], F32) for i in range(3)])
            pm = Ring("pb_pm", [self.ps(st, "pb_pm%d" % i, [128, 512], F32) for i in range(3)])
            for t in range(nt):
                var = 0 if t == 0 else (2 if t == nt - 1 else 1)
                if nt == 1:
                    raise NotImplementedError
                h6, kh = hb.next()
                blks = [r for r in range(6) if 0 <= t * 4 - 1 + r < nblk]
                r0, r1 = blks[0], blks[-1] + 1
                g0 = t * 4 - 1 + r0
                tiles = sorted(set((g0 + i) // 4 for i in range(r1 - r0)))
                self.dma("sp", h6[:, r0:r1, :],
                         H[g0 * 128:(g0 + r1 - r0) * 128, :].rearrange("(b p) d -> p b d", p=128),
                         reads=[("p_h", pi, x) for x in tiles], writes=[kh])
                b_, kb = bt.next()
                self.dma("sp", b_[:], self.wb["pool_bt"][var * 3072:(var + 1) * 3072, :].rearrange("(q p) n -> p q n", p=128),
                         reads=[("wb", "pool_bt")], writes=[kb])
                rc_, krc = rc.next()
                self.dma("sp", rc_[:], self.c["pool_rc"][var * 512:(var + 1) * 512, :].rearrange("(g p) n -> p g n", p=128),
                         reads=[], writes=[krc])
                x, kx = xr.next()
                self.dma("sp", x[:], src[t * T:(t + 1) * T, :].rearrange("(b p) d -> p b d", p=128),
                         reads=[(sname, pi, t)], writes=[kx])
                d, kd = dT.next()
                for c in range(8):
                    g = c // 2
                    p, kp = pd.next()
                    for i, r in enumerate(blks):
                        self.mm(p[:], h6[:, r, c * 128:(c + 1) * 128], b_[:, g * 6 + r, :], i == 0, i == len(blks) - 1,
                                reads=[kh, kb], writes=[kp])
                    self.tt("dve", d[:, c, :], p[:], rc_[:, g, :], ALU.mult, reads=[kp, krc], writes=[(kd, c)])
                y, ky = xo.next()
                for b in range(4):
                    for half in range(2):
                        p, kp = pm.next()
                        for gg in range(2):
                            g = half * 2 + gg
                            for cc_ in range(2):
                                self.mm(p[:, gg * 256:(gg + 1) * 256], d[:, 2 * g + cc_, b * 128:(b + 1) * 128],
                                        wp[:, 2 * g + cc_, :], cc_ == 0, cc_ == 1,
                                        reads=[(kd, 2 * g + cc_), "pb_w"], writes=[kp])
                        m, km = tm.next()
                        self.tt("dve", m[:], p[:], srep[:, half * 512:(half + 1) * 512], ALU.mult,
                                reads=[kp, "pb_s"], writes=[km])
                        self.tt("pool", y[:, b, half * 512:(half + 1) * 512], m[:], x[:, b, half * 512:(half + 1) * 512],
                                ALU.add, reads=[km, kx], writes=[(ky, b, half)])
                self.dma("pool", self.X[pi][t * T:(t + 1) * T, :].rearrange("(b p) d -> p b d", p=128), y[:],
                         reads=[(ky, b, hf) for b in range(4) for hf in range(2)], writes=[("X%d" % pi, pi, t)])
            self.phase_end()

    def ret_tables(self, st, want_q=True, want_m=True):
        Tb = {}
        raw = self.load_c(st, "rt_raw", "decay_rep", [128, 8], self.c["decay_rep"][:, :])
        dmask = self.load_c(st, "rt_dm", "dmask", [128, 256], self.c["dmask"][:, :])
        ind = self.load_c(st, "rt_ind", "ind", [128, 256], self.c["ind"][:, :])
        e12 = self.load_c(st, "rt_e12", "e12", [128, 1024], self.c["e12"][:, :]) if want_q else None
        pcol = self.load_c(st, "rt_pc", "pcol", [128, 2], self.c["pcol"][:, :])
        lg = self.sb(st, "rt_lg", [128, 8], F32)
        tmp = self.sb(st, "rt_tmp", [128, 8], F32)
        self.act(tmp[:], raw[:], AF.Exp, reads=["rt_raw"], writes=["rt_tmp"])
        self.act(tmp[:], tmp[:], AF.Ln, reads=["rt_tmp"], writes=["rt_tmp"], bias=1.0)
        self.ts("dve", lg[:], tmp[:], -1.0, None, ALU.mult, None, reads=["rt_tmp"], writes=["rt_lg"])
        mask = self.sb(st, "rt_mask", [128, 2, 512], F32) if want_m else None
        mtmp = self.sb(st, "rt_mtmp", [128, 2, 512], F32) if want_m else None
        qdec = self.sb(st, "rt_qdec", [128, 8, 512], F32) if want_q else None
        kdec = self.sb(st, "rt_kdec", [128, 8], F32)
        cdec = self.sb(st, "rt_cdec", [128, 8], F32)
        for d in range(2):
            for h in range(4):
                col = d * 4 + h
                if want_m:
                    self.act(mtmp[:, d, h * 128:(h + 1) * 128], dmask[:, d * 128:(d + 1) * 128], AF.Exp,
                             reads=["rt_dm", "rt_lg"], writes=[("rt_mtmp", col)], scale=lg[:, col:col + 1])
                    self.tt("dve", mask[:, d, h * 128:(h + 1) * 128], mtmp[:, d, h * 128:(h + 1) * 128],
                            ind[:, d * 128:(d + 1) * 128], ALU.mult, reads=[("rt_mtmp", col), "rt_ind"],
                            writes=[("rt_mask", col)])
                if want_q:
                    self.act(qdec[:, col, :], e12[:, d * 512:(d + 1) * 512], AF.Exp, reads=["rt_e12", "rt_lg"],
                             writes=[("rt_qdec", col)], scale=lg[:, col:col + 1])
                self.act(kdec[:, col:col + 1], pcol[:, d:d + 1], AF.Exp, reads=["rt_pc", "rt_lg"],
                         writes=[("rt_kdec0", col)], scale=lg[:, col:col + 1])
        self.ts("dve", kdec[:], kdec[:], 1.0 / 16.0, None, ALU.mult, None,
                reads=[("rt_kdec0", c) for c in range(8)], writes=["rt_kdec"])
        self.act(cdec[:], lg[:], AF.Exp, reads=["rt_lg"], writes=["rt_cdec"], scale=128.0)
        Tb.update(mask=mask, qdec=qdec, kdec=kdec, cdec=cdec)
        Tb["mask_keys"] = [("rt_mask", c) for c in range(8)]
        return Tb

    def ret_layer(self, li, pi, src, sname):
        L = self.piece_lens[pi]
        nt = L // T
        RQ = self.scratch("r_q%d" % pi, [8, 128, L])
        RQF = self.scratch("r_qf%d" % pi, [8, 128, L])
        RQB = self.scratch("r_qb%d" % pi, [8, 128, L])
        RKT = self.scratch("r_kt%d" % pi, [8, 128, L])
        RKF = self.scratch("r_kf%d" % pi, [L, 1024])
        RKB = self.scratch("r_kb%d" % pi, [L, 1024])
        RV = self.scratch("r_v%d" % pi, [L, 2048])
        RG = self.scratch("r_g%d" % pi, [L, 2048])
        GO = self.scratch("r_go%d" % pi, [L, 2048])
        OF = self.scratch("r_of%d" % pi, [L, 2048], F32)
        tok = lambda A, t: A[t * T:(t + 1) * T, :].rearrange("(b p) d -> p b d", p=128)

        st = self.phase_begin()
        with st:
            R = self.front_alloc(st)
            Tb = self.ret_tables(st, want_m=False)
            wq = self.load_w(st, "ra_wq", "r_wq", 8, 1024)
            wk = self.load_w(st, "ra_wk", "r_wk", 8, 1024)
            gcol = self.load_c(st, "ra_gcol", "nmix_col", [128, 8], self.c["nmix_col"][:, li * 8:(li + 1) * 8])
            hT = Ring("ra_hT", [self.sb(st, "ra_hT%d" % i, [128, 8, T], BF16) for i in range(2)])
            cs = Ring("ra_cs", [self.sb(st, "ra_cs%d" % i, [128, T], F32) for i in range(2)])
            sn = Ring("ra_sn", [self.sb(st, "ra_sn%d" % i, [128, T], F32) for i in range(2)])
            tA = Ring("ra_tA", [self.sb(st, "ra_tA%d" % i, [128, T], F32) for i in range(4)])
            tB = Ring("ra_tB", [self.sb(st, "ra_tB%d" % i, [128, T], F32) for i in range(4)])
            oo = Ring("ra_oo", [self.sb(st, "ra_oo%d" % i, [128, T], F32) for i in range(4)])
            ob = Ring("ra_ob", [self.sb(st, "ra_ob%d" % i, [128, T], BF16) for i in range(8)])
            kTt = self.sb(st, "ra_kTt", [128, 8, T], BF16)
            ktm = Ring("ra_ktm", [self.sb(st, "ra_ktm%d" % i, [128, 4, 1024], BF16) for i in range(2)])
            pp = Ring("ra_pp", [self.ps(st, "ra_pp%d" % i, [128, 512], F32) for i in range(4)])
            ptk = Ring("ra_ptk", [self.ps(st, "ra_ptk%d" % i, [128, 1024], BF16) for i in range(2)])
            for t in range(nt):
                h, hk = hT.next()
                self.front(R, src, sname, pi, t, gcol, "ra_gcol", h, hk)
                hreads = [(hk, c) for c in range(8)]
                c_, kc_ = cs.next()
                s_, ks_ = sn.next()
                self.dma("sp", c_[:], self.c["cosr"][:, t * T:(t + 1) * T], reads=[], writes=[kc_])
                self.dma("sp", s_[:], self.c["sinr"][:, t * T:(t + 1) * T], reads=[], writes=[ks_])
                for which in range(2):
                    w = wq if which == 0 else wk
                    wkey = "ra_wq" if which == 0 else "ra_wk"
                    for hh in range(4):
                        P12 = []
                        for half in range(2):
                            p, kp = pp.next()
                            oc = hh * 2 + half
                            for k in range(8):
                                self.mm(p[:], w[:, k, oc * 128:(oc + 1) * 128], h[:, k, :], k == 0, k == 7,
                                        reads=hreads + [wkey], writes=[kp])
                            P12.append((p, kp))
                        (p1, k1), (p2, k2) = P12
                        a1, ka1 = tA.next()
                        b1, kb1 = tB.next()
                        a2, ka2 = tA.next()
                        b2, kb2 = tB.next()
                        self.tt("dve", a1[:], p1[:], c_[:], ALU.mult, reads=[k1, kc_], writes=[ka1])
                        self.tt("dve", b1[:], p2[:], s_[:], ALU.mult, reads=[k2, ks_], writes=[kb1])
                        self.tt("dve", a2[:], p1[:], s_[:], ALU.mult, reads=[k1, ks_], writes=[ka2])
                        self.tt("dve", b2[:], p2[:], c_[:], ALU.mult, reads=[k2, kc_], writes=[kb2])
                        o1, ko1 = oo.next()
                        o2, ko2 = oo.next()
                        self.tt("pool", o1[:], a1[:], b1[:], ALU.subtract, reads=[ka1, kb1], writes=[ko1])
                        self.tt("pool", o2[:], a2[:], b2[:], ALU.add, reads=[ka2, kb2], writes=[ko2])
                        for half, (o, ko) in enumerate(((o1, ko1), (o2, ko2))):
                            oc = hh * 2 + half
                            if which == 0:
                                pb_, kpb = ob.next()
                                self.cp("act", pb_[:], o[:], reads=[ko], writes=[kpb])
                                self.dma("pool", RQ[oc, :, t * T:(t + 1) * T], pb_[:], reads=[kpb],
                                         writes=[("r_q", pi, oc, t)])
                                for d, RQD, nm in ((0, RQF, "r_qf"), (1, RQB, "r_qb")):
                                    pd_, kpd = ob.next()
                                    self.tt("pool", pd_[:], o[:], Tb["qdec"][:, d * 4 + hh, :], ALU.mult,
                                            reads=[ko, ("rt_qdec", d * 4 + hh)], writes=[kpd])
                                    self.dma("pool", RQD[oc, :, t * T:(t + 1) * T], pd_[:], reads=[kpd],
                                             writes=[(nm, pi, oc, t)])
                            else:
                                self.cp("act", kTt[:, oc, :], o[:], reads=[ko], writes=[("ra_kTt", oc)])
                                self.dma("pool", RKT[oc, :, t * T:(t + 1) * T], kTt[:, oc, :], reads=[("ra_kTt", oc)],
                                         writes=[("r_kt", pi, oc, t)])
                kf, kkf = ktm.next()
                kb, kkb = ktm.next()
                for b in range(4):
                    pt, kpt = ptk.next()
                    for oc in range(8):
                        self.tr(pt[:, oc * 128:(oc + 1) * 128], kTt[:, oc, b * 128:(b + 1) * 128],
                                reads=[("ra_kTt", oc)], writes=[kpt])
                    for hh in range(4):
                        self.act(kf[:, b, hh * 256:(hh + 1) * 256], pt[:, hh * 256:(hh + 1) * 256], AF.Copy,
                                 reads=[kpt, "rt_kdec"], writes=[(kkf, b, hh)], scale=Tb["kdec"][:, hh:hh + 1])
                        self.act(kb[:, b, hh * 256:(hh + 1) * 256], pt[:, hh * 256:(hh + 1) * 256], AF.Copy,
                                 reads=[kpt, "rt_kdec"], writes=[(kkb, b, hh)], scale=Tb["kdec"][:, 4 + hh:5 + hh])
                self.dma("pool", tok(RKF, t), kf[:], reads=[(kkf, b, hh) for b in range(4) for hh in range(4)],
                         writes=[("r_kf", pi, t)])
                self.dma("pool", tok(RKB, t), kb[:], reads=[(kkb, b, hh) for b in range(4) for hh in range(4)],
                         writes=[("r_kb", pi, t)])
            self.phase_end()

        st = self.phase_begin()
        with st:
            R = self.front_alloc(st)
            wv = self.load_w(st, "rb_wv", "r_wv", 8, 2048)
            wg = self.load_w(st, "rb_wg", "r_wg", 8, 2048)
            gcol = self.load_c(st, "rb_gcol", "nmix_col", [128, 8], self.c["nmix_col"][:, li * 8:(li + 1) * 8])
            hT = Ring("rb_hT", [self.sb(st, "rb_hT%d" % i, [128, 8, T], BF16) for i in range(2)])
            vt = Ring("rb_vt", [self.sb(st, "rb_vt%d" % i, [128, 4, 2048], BF16) for i in range(2)])
            gt = Ring("rb_gt", [self.sb(st, "rb_gt%d" % i, [128, 4, 2048], BF16) for i in range(2)])
            pp = Ring("rb_pp", [self.ps(st, "rb_pp%d" % i, [128, 512], F32) for i in range(6)])
            flip = 0
            for t in range(nt):
                h, hk = hT.next()
                self.front(R, src, sname, pi, t, gcol, "rb_gcol", h, hk)
                hreads = [(hk, c) for c in range(8)]
                v, kv = vt.next()
                g, kg = gt.next()
                for b in range(4):
                    for cc in range(4):
                        p, kp = pp.next()
                        for k in range(8):
                            self.mm(p[:], h[:, k, b * 128:(b + 1) * 128], wv[:, k, cc * 512:(cc + 1) * 512], k == 0, k == 7,
                                    reads=hreads + ["rb_wv"], writes=[kp])
                        eng = "dve" if flip % 2 == 0 else "act"
                        flip += 1
                        self.cp(eng, v[:, b, cc * 512:(cc + 1) * 512], p[:], reads=[kp], writes=[(kv, b, cc)])
                        p, kp = pp.next()
                        for k in range(8):
                            self.mm(p[:], h[:, k, b * 128:(b + 1) * 128], wg[:, k, cc * 512:(cc + 1) * 512], k == 0, k == 7,
                                    reads=hreads + ["rb_wg"], writes=[kp])
                        self.act(g[:, b, cc * 512:(cc + 1) * 512], p[:], AF.Silu, reads=[kp], writes=[(kg, b, cc)])
                self.dma("pool", tok(RV, t), v[:], reads=[(kv, b, cc) for b in range(4) for cc in range(4)],
                         writes=[("r_v", pi, t)])
                self.dma("pool", tok(RG, t), g[:], reads=[(kg, b, cc) for b in range(4) for cc in range(4)],
                         writes=[("r_g", pi, t)])
            self.phase_end()

        for d in range(2):
            st = self.phase_begin()
            with st:
                Tb = self.ret_tables(st, want_q=False)
                RQD = RQF if d == 0 else RQB
                RKD = RKF if d == 0 else RKB
                nqd = "r_qf" if d == 0 else "r_qb"
                nkd = "r_kf" if d == 0 else "r_kb"
                qt = Ring("rs_qt", [self.sb(st, "rs_qt%d" % i, [128, 8, T], BF16) for i in range(2)])
                kt = Ring("rs_kt", [self.sb(st, "rs_kt%d" % i, [128, 8, T], BF16) for i in range(2)])
                qd = Ring("rs_qd", [self.sb(st, "rs_qd%d" % i, [128, 8, T], BF16) for i in range(2)])
                km = Ring("rs_km", [self.sb(st, "rs_km%d" % i, [128, 4, 1024], BF16) for i in range(2)])
                vv = Ring("rs_vv", [self.sb(st, "rs_vv%d" % i, [128, 4, 2048], BF16) for i in range(2)])
                Sf = self.sb(st, "rs_Sf", [128, 8, 512], F32)
                Sb = self.sb(st, "rs_Sb", [128, 8, 512], BF16)
                am = Ring("rs_am", [self.sb(st, "rs_am%d" % i, [128, 512], BF16) for i in range(2)])
                of = Ring("rs_of", [self.sb(st, "rs_of%d" % i, [128, 2048], F32) for i in range(2)])
                patt = Ring("rs_patt", [self.ps(st, "rs_patt%d" % i, [128, 512], F32) for i in range(2)])
                po = Ring("rs_po", [self.ps(st, "rs_po%d" % i, [128, 512], F32) for i in range(2)])
                pd = Ring("rs_pd", [self.ps(st, "rs_pd%d" % i, [128, 512], F32) for i in range(4)])
                if d == 1:
                    ofl = Ring("rs_ofl", [self.sb(st, "rs_ofl%d" % i, [128, 2048], F32) for i in range(2)])
                    gl = Ring("rs_gl", [self.sb(st, "rs_gl%d" % i, [128, 2048], BF16) for i in range(2)])
                    on = Ring("rs_on", [self.sb(st, "rs_on%d" % i, [128, 2048], F32) for i in range(1)])
                    go = Ring("rs_go", [self.sb(st, "rs_go%d" % i, [128, 2048], BF16) for i in range(2)])
                    s1 = Ring("rs_s1", [self.sb(st, "rs_s1%d" % i, [128, 4], F32) for i in range(2)])
                    s2 = Ring("rs_s2", [self.sb(st, "rs_s2%d" % i, [128, 4], F32) for i in range(2)])
                    mu = Ring("rs_mu", [self.sb(st, "rs_mu%d" % i, [128, 4], F32) for i in range(2)])
                    vr = Ring("rs_vr", [self.sb(st, "rs_vr%d" % i, [128, 4], F32) for i in range(2)])
                    junk = self.sb(st, "rs_junk", [128, 512], BF16)
                self.S.op("dve", lambda e, Sf=Sf: e.memset(Sf[:], 0.0), [], [("rs_Sf", c) for c in range(8)])
                self.S.op("pool", lambda e, Sb=Sb: e.memset(Sb[:], 0.0), [], [("rs_Sb", c) for c in range(8)])
                flip = 0
                trange = range(nt) if d == 0 else range(nt - 1, -1, -1)
                for t in trange:
                    q_, kq = qt.next()
                    k_, kk = kt.next()
                    qd_, kqd = qd.next()
                    km_, kkm = km.next()
                    v_, kv = vv.next()
                    self.dma("sp", q_[:], RQ[:, :, t * T:(t + 1) * T].rearrange("c p n -> p c n"),
                             reads=[("r_q", pi, oc, t) for oc in range(8)], writes=[kq])
                    self.dma("sp", k_[:], RKT[:, :, t * T:(t + 1) * T].rearrange("c p n -> p c n"),
                             reads=[("r_kt", pi, oc, t) for oc in range(8)], writes=[kk])
                    self.dma("sp", qd_[:], RQD[:, :, t * T:(t + 1) * T].rearrange("c p n -> p c n"),
                             reads=[(nqd, pi, oc, t) for oc in range(8)], writes=[kqd])
                    self.dma("sp", km_[:], tok(RKD, t), reads=[(nkd, pi, t)], writes=[kkm])
                    self.dma("sp", v_[:], tok(RV, t), reads=[("r_v", pi, t)], writes=[kv])
                    brange = range(4) if d == 0 else range(3, -1, -1)
                    for b in brange:
                        n = t * 4 + b
                        sl = slice(b * 128, (b + 1) * 128)
                        pa, kpa = patt.next()
                        for hh in range(4):
                            for dc in range(2):
                                self.mm(pa[:, hh * 128:(hh + 1) * 128], k_[:, hh * 2 + dc, sl], q_[:, hh * 2 + dc, sl],
                                        dc == 0, dc == 1, reads=[kk, kq], writes=[kpa])
                        a_, ka = am.next()
                        self.tt("dve", a_[:], pa[:], Tb["mask"][:, d, :], ALU.mult, reads=[kpa] + Tb["mask_keys"], writes=[ka])
                        if d == 0:
                            o_, ko = of.next()
                        else:
                            ol, kol = ofl.next()
                            self.dma("sp", ol[:], OF[n * 128:(n + 1) * 128, :], reads=[("r_of", pi, n)], writes=[kol])
                            g_, kg = gl.next()
                            self.dma("sp", g_[:], RG[n * 128:(n + 1) * 128, :], reads=[("r_g", pi, t)], writes=[kg])
                            o_, ko = of.next()
                            s1_, ks1 = s1.next()
                            s2_, ks2 = s2.next()
                        for hh in range(4):
                            p, kp = po.next()
                            self.mm(p[:], a_[:, hh * 128:(hh + 1) * 128], v_[:, b, hh * 512:(hh + 1) * 512], True, False,
                                    reads=[ka, kv], writes=[kp])
                            for dc in range(2):
                                self.mm(p[:], qd_[:, hh * 2 + dc, sl], Sb[:, hh * 2 + dc, :], False, dc == 1,
                                        reads=[kqd, ("rs_Sb", hh * 2 + dc)], writes=[kp])
                            if d == 0:
                                self.cp("act", o_[:, hh * 512:(hh + 1) * 512], p[:], reads=[kp], writes=[(ko, hh)])
                            else:
                                self.tt("dve", o_[:, hh * 512:(hh + 1) * 512], p[:], ol[:, hh * 512:(hh + 1) * 512], ALU.add,
                                        reads=[kp, kol], writes=[(ko, hh)])
                                self.act(junk[:], o_[:, hh * 512:(hh + 1) * 512], AF.Copy, reads=[(ko, hh)],
                                         writes=[(ks1, hh)], accum=s1_[:, hh:hh + 1])
                                self.act(junk[:], o_[:, hh * 512:(hh + 1) * 512], AF.Square, reads=[(ko, hh)],
                                         writes=[(ks2, hh)], accum=s2_[:, hh:hh + 1])
                            for dc in range(2):
                                pq, kpq = pd.next()
                                self.mm(pq[:], km_[:, b, hh * 256 + dc * 128:hh * 256 + (dc + 1) * 128],
                                        v_[:, b, hh * 512:(hh + 1) * 512], True, True, reads=[kkm, kv], writes=[kpq])
                                c8 = hh * 2 + dc
                                self.stt(Sf[:, c8, :], Sf[:, c8, :], Tb["cdec"][:, d * 4 + hh:d * 4 + hh + 1], pq[:],
                                         ALU.mult, ALU.add, reads=[kpq, ("rs_Sf", c8), "rt_cdec"], writes=[("rs_Sf", c8)])
                                self.cp("act", Sb[:, c8, :], Sf[:, c8, :], reads=[("rs_Sf", c8)], writes=[("rs_Sb", c8)])
                        if d == 0:
                            self.dma("pool", OF[n * 128:(n + 1) * 128, :], o_[:], reads=[(ko, hh) for hh in range(4)],
                                     writes=[("r_of", pi, n)])
                        else:
                            mu_, kmu = mu.next()
                            vr_, kvr = vr.next()
                            self.ts("dve", mu_[:], s1_[:], 1.0 / 512, None, ALU.mult, None,
                                    reads=[(ks1, hh) for hh in range(4)], writes=[kmu])
                            self.tt("dve", vr_[:], mu_[:], mu_[:], ALU.mult, reads=[kmu], writes=[kvr])
                            self.stt(vr_[:], s2_[:], 1.0 / 512, vr_[:], ALU.mult, ALU.subtract,
                                     reads=[(ks2, hh) for hh in range(4)] + [kvr], writes=[kvr])
                            self.act(vr_[:], vr_[:], AF.Ln, reads=[kvr], writes=[kvr], bias=EPS)
                            self.act(vr_[:], vr_[:], AF.Exp, reads=[kvr], writes=[kvr], scale=-0.5)
                            on_, kon = on.next()
                            go_, kgo = go.next()
                            for hh in range(4):
                                self.ts("dve", on_[:, hh * 512:(hh + 1) * 512], o_[:, hh * 512:(hh + 1) * 512],
                                        mu_[:, hh:hh + 1], vr_[:, hh:hh + 1], ALU.subtract, ALU.mult,
                                        reads=[(ko, hh), kmu, kvr], writes=[(kon, hh)])
                                self.tt("pool", go_[:, hh * 512:(hh + 1) * 512], on_[:, hh * 512:(hh + 1) * 512],
                                        g_[:, hh * 512:(hh + 1) * 512], ALU.mult, reads=[(kon, hh), kg], writes=[(kgo, hh)])
                            self.dma("pool", GO[n * 128:(n + 1) * 128, :], go_[:], reads=[(kgo, hh) for hh in range(4)],
                                     writes=[("r_go", pi, n)])
                self.phase_end()

        st = self.phase_begin()
        with st:
            wo = self.load_w(st, "rd_wo", "r_wo", 16, 1024)
            gin = Ring("rd_gin", [self.sb(st, "rd_gin%d" % i, [128, 4, 2048], BF16) for i in range(2)])
            goT = Ring("rd_goT", [self.sb(st, "rd_goT%d" % i, [128, 16, T], BF16) for i in range(2)])
            xr = Ring("rd_xr", [self.sb(st, "rd_xr%d" % i, [128, 4, D], F32) for i in range(2)])
            xo = Ring("rd_xo", [self.sb(st, "rd_xo%d" % i, [128, 4, D], F32) for i in range(2)])
            ptr = Ring("rd_ptr", [self.ps(st, "rd_ptr%d" % i, [128, 1024], BF16) for i in range(3)])
            pp = Ring("rd_pp", [self.ps(st, "rd_pp%d" % i, [128, 512], F32) for i in range(4)])
            for t in range(nt):
                gi, kgi = gin.next()
                self.dma("sp", gi[:], tok(GO, t), reads=[("r_go", pi, t * 4 + b) for b in range(4)], writes=[kgi])
                x, kx = xr.next()
                self.dma("sp", x[:], tok(src, t), reads=[(sname, pi, t)], writes=[kx])
                gT, kgT = goT.next()
                for cp_ in range(8):
                    pt, kpt = ptr.next()
                    for cc in range(2):
                        c = cp_ * 2 + cc
                        for b in range(4):
                            self.tr(pt[:, cc * 512 + b * 128:cc * 512 + (b + 1) * 128], gi[:, b, c * 128:(c + 1) * 128],
                                    reads=[kgi], writes=[kpt])
                    self.cp("act", gT[:, cp_ * 2:cp_ * 2 + 2, :], pt[:].rearrange("p (c n) -> p c n", c=2),
                            reads=[kpt], writes=[(kgT, cp_)])
                y, ky = xo.next()
                for b in range(4):
                    for half in range(2):
                        pq, kpq = pp.next()
                        for c in range(16):
                            self.mm(pq[:], gT[:, c, b * 128:(b + 1) * 128], wo[:, c, half * 512:(half + 1) * 512],
                                    c == 0, c == 15, reads=[(kgT, c // 2), "rd_wo"], writes=[kpq])
                        self.tt("dve", y[:, b, half * 512:(half + 1) * 512], pq[:], x[:, b, half * 512:(half + 1) * 512],
                                ALU.add, reads=[kpq, kx], writes=[(ky, b, half)])
                self.dma("pool", tok(self.X[pi], t), y[:],
                         reads=[(ky, b, hf) for b in range(4) for hf in range(2)], writes=[("X%d" % pi, pi, t)])
            self.phase_end()


def rope_tables(lmax):
    pos = np.arange(lmax, dtype=np.float32)
    inv = (1.0 / (np.float32(10000.0) ** (np.arange(32, dtype=np.float32) * np.float32(2.0) / np.float32(64)))).astype(np.float32)
    ang = (pos[:, None] * inv[None, :]).astype(np.float32)
    cos, sin = np.cos(ang).astype(np.float32), np.sin(ang).astype(np.float32)
    ccm = np.zeros((128, lmax), np.float32)
    ssm = np.zeros((128, lmax), np.float32)
    for r in range(128):
        f = r % 32
        ccm[r] = cos[:, f]
        ssm[r] = sin[:, f] * (-1.0 if (r % 64) < 32 else 1.0)
    inv2 = (1.0 / (np.float32(10000.0) ** (np.arange(128, dtype=np.float32) * np.float32(2.0) / np.float32(256)))).astype(np.float32)
    ang2 = (pos[:, None] * inv2[None, :]).astype(np.float32)
    cosr = np.ascontiguousarray(np.cos(ang2).astype(np.float32).T)
    sinr = np.ascontiguousarray(np.sin(ang2).astype(np.float32).T)
    return ccm, ssm, cosr, sinr


def pool_tables():
    wins = (2, 4, 8, 16)
    bt = np.zeros((3, 4, 6, 128, 512), np.float32)
    rc = np.zeros((3, 4, 128, 512), np.float32)
    for var in range(3):
        for g, w in enumerate(wins):
            for t in range(512):
                lo, hi = t - w // 2, t + w // 2 - 1
                if var == 0:
                    lo = max(lo, 0)
                if var == 2:
                    hi = min(hi, 511)
                cnt = hi - lo + 1
                rc[var, g, :, t] = 1.0 / cnt
                for jj in range(lo, hi + 1):
                    ja = jj + 128
                    bt[var, g, ja // 128, ja % 128, t] = 1.0
                ja = t + 128
                bt[var, g, ja // 128, ja % 128, t] = 1.0 - cnt
    return bt.reshape(3 * 4 * 6 * 128, 512), rc.reshape(3 * 4 * 128, 512)


def col_layout(v):
    v = np.asarray(v, np.float32)
    return np.ascontiguousarray(v.reshape(-1, 128).T)


def rep128(v):
    v = np.asarray(v, np.float32).reshape(1, -1)
    return np.ascontiguousarray(np.broadcast_to(v, (128, v.shape[1])))


def host_shared(inp, lmax):
    f = lambda a: np.ascontiguousarray(np.asarray(a, np.float32))
    sh = {}
    for j in range(2):
        wuq = f(inp["mla_w_uq"][j]).reshape(384, 8, 192)
        nope = wuq[:, :, :128].reshape(384, 1024)
        ra = wuq[:, :, 128:].reshape(384, 512)
        rb = np.concatenate([wuq[:, :, 160:192], wuq[:, :, 128:160]], axis=2).reshape(384, 512)
        sh["m%d_wq" % j] = np.ascontiguousarray(np.concatenate([nope, ra, rb], axis=1))
        wdkv = f(inp["mla_w_dkv"][j])
        sh["m%d_wdq" % j] = f(inp["mla_w_dq"][j])
        sh["m%d_wdkvc" % j] = np.ascontiguousarray(wdkv[:, :256])
        sh["m%d_wdkvr" % j] = np.ascontiguousarray(
            np.concatenate([wdkv[:, 256:320], wdkv[:, 288:320], wdkv[:, 256:288]], axis=1))
        wukv = f(inp["mla_w_ukv"][j]).reshape(256, 8, 256)
        sh["m%d_wukv" % j] = np.ascontiguousarray(
            np.concatenate([wukv[:, :, :128].reshape(256, 1024), wukv[:, :, 128:].reshape(256, 1024)], axis=1))
        sh["m%d_wo" % j] = f(inp["mla_w_o"][j])
    sh["r_wq"] = f(inp["ret_w_q"][0])
    sh["r_wk"] = f(inp["ret_w_k"][0])
    sh["r_wv"] = f(inp["ret_w_v"][0])
    sh["r_wg"] = f(inp["ret_w_g"][0])
    sh["r_wo"] = f(inp["ret_w_o"][0])
    sh["p_w"] = f(inp["pool_w"][0]).reshape(1024, 256)
    for i in range(4):
        sh["w1_%d" % i] = f(inp["mlp_w1"][i])
        sh["w2_%d" % i] = f(inp["mlp_w2"][i])
    sh["ident"] = np.eye(128, dtype=np.float32)
    sh["ones"] = np.ones((128, 128), np.float32)
    bt, rc = pool_tables()
    sh["pool_bt"] = bt
    sh["pool_rc"] = rc
    sh["nmix_col"] = np.concatenate([col_layout(inp["norm_mix"][i]) for i in range(4)], axis=1)
    sh["nffn_col"] = np.concatenate([col_layout(inp["norm_ffn"][i]) for i in range(4)], axis=1)
    sh["nmix2_rep"] = rep128(inp["norm_mix"][2])
    sh["final_rep"] = rep128(inp["final_norm"])
    sh["pscale_rep"] = rep128(inp["pool_scale"][0])
    sh["qn_col"] = np.concatenate([col_layout(inp["mla_q_norm"][j]) for j in range(2)], axis=1)
    sh["kvn_col"] = np.concatenate([col_layout(inp["mla_kv_norm"][j]) for j in range(2)], axis=1)
    sh["decay_rep"] = rep128(np.concatenate([np.asarray(inp["ret_decay_fwd"][0]), np.asarray(inp["ret_decay_bwd"][0])]))
    ccm, ssm, cosr, sinr = rope_tables(lmax)
    sh["ccm"], sh["ssm"], sh["cosr"], sh["sinr"] = ccm, ssm, cosr, sinr
    ii = np.arange(128, dtype=np.float32)
    df = np.maximum(ii[None, :] - ii[:, None], 0.0)
    db = np.maximum(ii[:, None] - ii[None, :], 0.0)
    sh["dmask"] = np.ascontiguousarray(np.concatenate([df, db], axis=1))
    indf = (ii[None, :] >= ii[:, None]).astype(np.float32) / 16.0
    indb = (ii[:, None] >= ii[None, :]).astype(np.float32) / 16.0
    sh["ind"] = np.ascontiguousarray(np.concatenate([indf, indb], axis=1))
    il = (np.arange(512) % 128).astype(np.float32)
    sh["e12"] = rep128(np.concatenate([il + 1.0, 128.0 - il]))
    sh["pcol"] = np.ascontiguousarray(np.stack([127.0 - ii, ii], axis=1).astype(np.float32))
    return sh


_CACHE = {}


def get_prog(piece_lens, layers, final, lmax):
    key = (tuple(piece_lens), tuple(layers), final, lmax)
    if key not in _CACHE:
        p = Prog(piece_lens, layers, final, lmax)
        p.build()
        _CACHE[key] = p
    return _CACHE[key]


def run_pieces(inp, core_pieces, layers=(0, 1, 2, 3), final=True):
    piece_lens = [a.shape[0] for a in core_pieces[0]]
    lmax = max(piece_lens)
    prog = get_prog(piece_lens, layers, final, lmax)
    sh = host_shared(inp, lmax)
    in_maps = []
    for cp in core_pieces:
        m = dict(sh)
        for pi, a in enumerate(cp):
            m["x%d" % pi] = np.ascontiguousarray(a, dtype=np.float32)
        m = {k: m[k] for k in prog.inputs}
        in_maps.append(m)
    res = run_bass_kernel_spmd(prog.nc, in_maps, core_ids=list(range(len(core_pieces))))
    return [[np.asarray(r["y%d" % pi]) for pi in range(len(piece_lens))] for r in res.results]


def kernel(**inp):
    xp = np.asarray(inp["x_prompt"], np.float32)
    xs = np.asarray(inp["x_sample"], np.float32)
    zero = np.zeros_like(xs[0])
    core_pieces = []
    for c in range(8):
        core_pieces.append([xs[c] if c < 2 else zero, xp[2 * c], xp[2 * c + 1]])
    outs = run_pieces(inp, core_pieces)
    y_prompt = np.stack([outs[c][1 + k] for c in range(8) for k in range(2)], axis=0)
    y_sample = np.stack([outs[0][0], outs[1][0]], axis=0)
    return (y_prompt.astype(np.float32), y_sample.astype(np.float32))
```

```python
import contextlib
import numpy as np
import ml_dtypes
import concourse.bass as bass
import concourse.mybir as mybir
from concourse.bass_utils import run_bass_kernel_spmd

F32 = mybir.dt.float32
BF16 = mybir.dt.bfloat16
AF = mybir.ActivationFunctionType
ALU = mybir.AluOpType

D = 1024
T = 512
EPS = 1e-6
NH = 8
MLA_SCALE = 192 ** -0.5
KCH = 2048


class Sched:
    ENG = ("sp", "act", "dve", "pool", "pe")

    def __init__(self, nc, stack):
        self.nc = nc
        self.eobj = dict(pe=nc.tensor, act=nc.scalar, dve=nc.vector, pool=nc.gpsimd, sp=nc.sync)
        self.lanes = {"sp": ["sp%d" % i for i in range(8)], "pool": ["pl%d" % i for i in range(6)],
                      "act": ["aq%d" % i for i in range(4)]}
        self.sems = {}
        self.count = {}
        for n in list(self.ENG) + sum(self.lanes.values(), []):
            self.sems[n] = stack.enter_context(nc.semaphore("s_" + n))
            self.count[n] = 0
        self.lane_rr = {k: 0 for k in self.lanes}
        self.pending = {e: [] for e in self.ENG}
        self.seen = {e: {} for e in self.ENG}
        self.lw = {}
        self.rd = {}
        self.nops = 0

    def op(self, e, fn, reads=(), writes=(), dma=False):
        deps = {}

        def add(ev):
            s, v = ev
            if deps.get(s, 0) < v:
                deps[s] = v

        for k in reads:
            ev = self.lw.get(k)
            if ev is not None:
                add(ev)
        for k in writes:
            ev = self.lw.get(k)
            if ev is not None:
                add(ev)
            r = self.rd.get(k)
            if r:
                for s, v in r.items():
                    add((s, v))
        if dma:
            lanes = self.lanes[e]
            ln = lanes[self.lane_rr[e] % len(lanes)]
            self.lane_rr[e] += 1
            if self.count[ln] > 0:
                add((ln, self.count[ln]))
            self.count[ln] += 16
            ev = (ln, self.count[ln])
            inc = (ln, 16)
        else:
            self.count[e] += 1
            ev = (e, self.count[e])
            inc = (e, 1)
        waits = []
        seen = self.seen[e]
        for s, v in deps.items():
            if s == e and e == "pe":
                continue
            if seen.get(s, 0) >= v:
                continue
            seen[s] = v
            waits.append((s, v))
        self.pending[e].append((fn, waits, inc))
        for k in reads:
            r = self.rd.setdefault(k, {})
            if r.get(ev[0], 0) < ev[1]:
                r[ev[0]] = ev[1]
        for k in writes:
            self.lw[k] = ev
            self.rd[k] = {}
        self.nops += 1
        return ev

    def barrier(self):
        for e in self.ENG:
            waits = []
            for s, v in self.count.items():
                if v > 0 and self.seen[e].get(s, 0) < v:
                    self.seen[e][s] = v
                    waits.append((s, v))
            if waits:
                self.pending[e].append((None, waits, None))

    def flush(self):
        with self.nc.Block() as blk:
            for e, deco in (("sp", blk.sync), ("act", blk.scalar), ("dve", blk.vector),
                            ("pool", blk.gpsimd), ("pe", blk.tensor)):
                ops = self.pending[e]
                self.pending[e] = []

                def body(eng, ops=ops):
                    for fn, waits, inc in ops:
                        for s, v in waits:
                            eng.wait_ge(self.sems[s], v)
                        if fn is not None:
                            ins = fn(eng)
                            ins.then_inc(self.sems[inc[0]], inc[1])

                deco(body)


class Ring:
    def __init__(self, name, aps):
        self.name = name
        self.aps = aps
        self.i = 0

    def next(self):
        j = self.i % len(self.aps)
        self.i += 1
        return self.aps[j], (self.name, j)


class Prog:
    def __init__(self, piece_lens, layers=(0, 1, 2, 3), final=True, lmax=16384):
        self.piece_lens = list(piece_lens)
        self.layers = list(layers)
        self.final = final
        self.lmax = lmax
        self.inputs = {}
        self.nc = bass.Bass("TRN2", target_bir_lowering=False)
        self.stack = contextlib.ExitStack()
        self.S = None

    def din(self, name, shape, dt=F32):
        self.inputs[name] = (tuple(shape), np.float32 if dt == F32 else ml_dtypes.bfloat16)
        return self.nc.dram_tensor(name, list(shape), dt, kind="ExternalInput").ap()

    def dout(self, name, shape):
        return self.nc.dram_tensor(name, list(shape), F32, kind="ExternalOutput").ap()

    def dscr(self, name, shape, dt=BF16):
        return self.nc.dram_tensor(name, list(shape), dt, kind="Internal").ap()

    def uniq(self, name):
        self._u = getattr(self, "_u", 0) + 1
        return "%s_u%d" % (name, self._u)

    def sb(self, st, name, shape, dt):
        return st.enter_context(self.nc.sbuf_tensor(self.uniq(name), list(shape), dt))

    def ps(self, st, name, shape, dt):
        return st.enter_context(self.nc.psum_tensor(self.uniq(name), list(shape), dt))

    def dma(self, q, out, in_, reads, writes, **kw):
        self.S.op(q, lambda e: e.dma_start(out=out, in_=in_, **kw), reads, writes, dma=True)

    def mm(self, out, lhsT, rhs, start, stop, reads, writes):
        self.S.op("pe", lambda e: e.matmul(out, lhsT=lhsT, rhs=rhs, start=start, stop=stop), reads, writes)

    def tr(self, out, in_, reads, writes):
        ident = self.ident
        self.S.op("pe", lambda e: e.transpose(out, in_, ident), list(reads) + ["ident"], writes)

    def act(self, out, in_, func, reads, writes, bias=None, scale=None, accum=None):
        kw = {}
        if bias is not None:
            kw["bias"] = bias
        if scale is not None:
            kw["scale"] = scale
        if accum is not None:
            kw["accum_out"] = accum
        self.S.op("act", lambda e: e.activation(out=out, in_=in_, func=func, **kw), reads, writes)

    def tt(self, eng, out, a, b, op, reads, writes):
        self.S.op(eng, lambda e: e.tensor_tensor(out=out, in0=a, in1=b, op=op), reads, writes)

    def ts(self, eng, out, a, s1, s2, op0, op1, reads, writes):
        if op1 is None:
            self.S.op(eng, lambda e: e.tensor_scalar(out=out, in0=a, scalar1=s1, scalar2=None, op0=op0),
                      reads, writes)
        else:
            self.S.op(eng, lambda e: e.tensor_scalar(out=out, in0=a, scalar1=s1, scalar2=s2, op0=op0, op1=op1),
                      reads, writes)

    def stt(self, out, a, s, b, op0, op1, reads, writes):
        self.S.op("dve", lambda e: e.scalar_tensor_tensor(out=out, in0=a, scalar=s, in1=b, op0=op0, op1=op1),
                  reads, writes)

    def cp(self, eng, out, in_, reads, writes):
        if eng == "act":
            self.act(out, in_, AF.Copy, reads, writes)
        else:
            self.S.op(eng, lambda e: e.tensor_copy(out=out, in_=in_), reads, writes)

    def declare(self):
        nc = self.nc
        LM = self.lmax
        self.xin = []
        self.yout = []
        self.X = []
        for pi, L in enumerate(self.piece_lens):
            self.xin.append(self.din("x%d" % pi, [L, D]))
            self.yout.append(self.dout("y%d" % pi, [L, D]))
            self.X.append(self.dscr("X%d" % pi, [L, D], F32))
        self.wspec = {}
        for j in range(2):
            self.wspec["m%d_wdq" % j] = (1024, 384)
            self.wspec["m%d_wdkvc" % j] = (1024, 256)
            self.wspec["m%d_wdkvr" % j] = (1024, 128)
            self.wspec["m%d_wq" % j] = (384, 2048)
            self.wspec["m%d_wukv" % j] = (256, 2048)
            self.wspec["m%d_wo" % j] = (1024, 1024)
        self.wspec["r_wq"] = (1024, 1024)
        self.wspec["r_wk"] = (1024, 1024)
        self.wspec["r_wv"] = (1024, 2048)
        self.wspec["r_wg"] = (1024, 2048)
        self.wspec["r_wo"] = (2048, 1024)
        self.wspec["p_w"] = (1024, 256)
        for i in range(4):
            self.wspec["w1_%d" % i] = (1024, 4096)
            self.wspec["w2_%d" % i] = (4096, 1024)
        self.wspec["ident"] = (128, 128)
        self.wspec["ones"] = (128, 128)
        self.wspec["pool_bt"] = (3 * 4 * 6 * 128, 512)
        self.wf = {}
        self.wb = {}
        for n, shp in self.wspec.items():
            self.wf[n] = self.din(n, shp)
            self.wb[n] = self.dscr("b_" + n, shp, BF16)
        self.c = {}
        for n, shp in [("nmix_col", (128, 32)), ("nffn_col", (128, 32)), ("nmix2_rep", (128, 1024)),
                       ("final_rep", (128, 1024)), ("pscale_rep", (128, 1024)),
                       ("qn_col", (128, 6)), ("kvn_col", (128, 4)), ("decay_rep", (128, 8)),
                       ("ccm", (128, LM)), ("ssm", (128, LM)), ("cosr", (128, LM)), ("sinr", (128, LM)),
                       ("dmask", (128, 256)), ("ind", (128, 256)), ("e12", (128, 1024)), ("pcol", (128, 2)),
                       ("pool_rc", (3 * 4 * 128, 512))]:
            self.c[n] = self.din(n, shp)

    def build(self):
        self.declare()
        with self.stack:
            self.S = Sched(self.nc, self.stack)
            self.identt = self.sb(self.stack, "identt", [128, 128], BF16)
            self.ident = self.identt[:]
            self.onest = self.sb(self.stack, "onest", [128, 128], BF16)
            self.prologue()
            for li in self.layers:
                kind, j = li % 3, li // 3
                first = (li == self.layers[0])
                for pi in range(len(self.piece_lens)):
                    src = self.xin[pi] if first else self.X[pi]
                    sname = ("xin%d" % pi) if first else ("X%d" % pi)
                    if kind == 0:
                        self.mla_layer(li, j, pi, src, sname)
                    elif kind == 1:
                        self.ret_layer(li, pi, src, sname)
                    else:
                        self.pool_layer(li, pi, src, sname)
                for pi in range(len(self.piece_lens)):
                    self.mlp_layer(li, pi)
            if self.final:
                for pi in range(len(self.piece_lens)):
                    self.final_norm(pi)
            else:
                for pi in range(len(self.piece_lens)):
                    for t in range(self.piece_lens[pi] // T):
                        self.dma("sp", self.yout[pi][t * T:(t + 1) * T, :], self.X[pi][t * T:(t + 1) * T, :],
                                 reads=[("X%d" % pi, pi, t)], writes=[("yout", pi, t)])
            self.S.barrier()
            self.S.flush()
        return self.nc

    def prologue(self):
        for n, shp in self.wspec.items():
            K, N = shp
            rows = 512 if N * 512 * 4 <= (8 << 20) else 128
            rows = min(rows, K)
            for r0 in range(0, K, rows):
                self.dma("pool", self.wb[n][r0:r0 + rows, :], self.wf[n][r0:r0 + rows, :],
                         reads=[], writes=[("wb", n)], max_dma_last_dim=4096)
        self.dma("sp", self.identt[:], self.wb["ident"][:, :], reads=[("wb", "ident")], writes=["ident"])
        self.dma("sp", self.onest[:], self.wb["ones"][:, :], reads=[("wb", "ones")], writes=["ones"])

    def phase_begin(self):
        self.S.barrier()
        return contextlib.ExitStack()

    def phase_end(self):
        self.S.barrier()
        self.S.flush()

    def load_w(self, st, name, wname, kchunks, ncols, col0=0, q="sp"):
        t = self.sb(st, name, [128, kchunks, ncols], BF16)
        src = self.wb[wname][0:kchunks * 128, col0:col0 + ncols].rearrange("(c p) n -> p c n", p=128)
        self.dma(q, t[:], src, reads=[("wb", wname)], writes=[name])
        return t

    def load_c(self, st, name, cname, shape, src_ap, q="sp"):
        t = self.sb(st, name, list(shape), F32)
        self.dma(q, t[:], src_ap, reads=[], writes=[name])
        return t

    def front_alloc(self, st, nbuf=2):
        R = {}
        R["xin"] = Ring("f_xin", [self.sb(st, "f_xin%d" % i, [128, 4, D], F32) for i in range(nbuf)])
        R["xn"] = Ring("f_xn", [self.sb(st, "f_xn%d" % i, [128, 4, D], BF16) for i in range(nbuf)])
        R["ss"] = Ring("f_ss", [self.sb(st, "f_ss%d" % i, [128, 4], F32) for i in range(nbuf)])
        R["rs"] = Ring("f_rs", [self.sb(st, "f_rs%d" % i, [128, 4], F32) for i in range(nbuf)])
        R["junk"] = self.sb(st, "f_junk", [128, D], BF16)
        pts = [self.ps(st, "f_pt%d" % i, [128, 1024], BF16) for i in range(2)]
        R["pt"] = Ring("f_pt", [pts[0], pts[1]])
        R["flip"] = 0
        return R

    def front_stats(self, R, src, sname, pi, t, want_xn=True):
        xin, kx = R["xin"].next()
        self.dma("sp", xin[:], src[t * T:(t + 1) * T, :].rearrange("(b p) d -> p b d", p=128),
                 reads=[(sname, pi, t)], writes=[kx])
        ss, kss = R["ss"].next()
        rs, krs = R["rs"].next()
        junk = R["junk"]
        for b in range(4):
            self.act(junk[:], xin[:, b, :], AF.Square, reads=[kx], writes=[(kss, b)], accum=ss[:, b:b + 1])
        self.act(rs[:], ss[:], AF.Ln, reads=[(kss, b) for b in range(4)], writes=[krs], scale=1.0 / D, bias=EPS)
        self.act(rs[:], rs[:], AF.Exp, reads=[krs], writes=[krs], scale=-0.5)
        xn = kxn = None
        if want_xn:
            xn, kxn = R["xn"].next()
            for b in range(4):
                self.ts("dve", xn[:, b, :], xin[:, b, :], rs[:, b:b + 1], None, ALU.mult, None,
                        reads=[kx, krs], writes=[(kxn, b)])
        return xin, kx, rs, krs, xn, kxn

    def front(self, R, src, sname, pi, t, gcol, gkey, hT, hkey):
        xin, kx, rs, krs, xn, kxn = self.front_stats(R, src, sname, pi, t)
        import os
        FD = int(os.environ.get("FDBG", "9"))
        for cpair in range(4):
            if FD < 2:
                break
            pt, kpt = R["pt"].next()
            for cc in range(2):
                c = cpair * 2 + cc
                for b in range(4):
                    self.tr(pt[:, cc * 512 + b * 128:cc * 512 + (b + 1) * 128], xn[:, b, c * 128:(c + 1) * 128],
                            reads=[(kxn, b)], writes=[kpt])
            for cc in range(2):
                if FD < 3:
                    break
                c = cpair * 2 + cc
                if True:
                    self.act(hT[:, c, :], pt[:, cc * 512:(cc + 1) * 512], AF.Copy, reads=[kpt, gkey],
                             writes=[(hkey, c)], scale=gcol[:, c:c + 1])
                else:
                    self.ts("dve", hT[:, c, :], pt[:, cc * 512:(cc + 1) * 512], gcol[:, c:c + 1], None, ALU.mult, None,
                            reads=[kpt, gkey], writes=[(hkey, c)])
        return xin, kx

    def scratch(self, name, shape, dt=BF16):
        if not hasattr(self, "_scr"):
            self._scr = {}
        if name not in self._scr:
            self._scr[name] = self.dscr(name, shape, dt)
        return self._scr[name]

    def mla_layer(self, li, j, pi, src, sname):
        L = self.piece_lens[pi]
        nt = L // T
        nblk = L // 128
        QTn = self.scratch("a_qtn%d" % pi, [NH, 128, L])
        QTr = self.scratch("a_qtr%d" % pi, [NH * 64, L])
        KTn = self.scratch("a_ktn%d" % pi, [NH, 128, L])
        KPE = self.scratch("a_kpe%d" % pi, [64, L])
        VP = self.scratch("a_vp%d" % pi, [NH, 128, nblk, 128])
        OT = self.scratch("a_ot%d" % pi, [NH, 128, L])
        pre = "m%d_" % j

        st = self.phase_begin()
        with st:
            R = self.front_alloc(st)
            wdq = self.load_w(st, "a_wdq", pre + "wdq", 8, 384)
            wdkvc = self.load_w(st, "a_wdkvc", pre + "wdkvc", 8, 256)
            wdkvr = self.load_w(st, "a_wdkvr", pre + "wdkvr", 8, 128)
            wq = self.load_w(st, "a_wq", pre + "wq", 3, 2048)
            wukv = self.load_w(st, "a_wukv", pre + "wukv", 2, 2048)
            gcol = self.load_c(st, "a_gcol", "nmix_col", [128, 8], self.c["nmix_col"][:, li * 8:(li + 1) * 8])
            qncol = self.load_c(st, "a_qn", "qn_col", [128, 3], self.c["qn_col"][:, j * 3:(j + 1) * 3])
            kvncol = self.load_c(st, "a_kvn", "kvn_col", [128, 2], self.c["kvn_col"][:, j * 2:(j + 1) * 2])
            hT = Ring("a_hT", [self.sb(st, "a_hT%d" % i, [128, 8, T], BF16) for i in range(2)])
            cqf = self.sb(st, "a_cqf", [128, 4, 640], F32)
            cqn = self.sb(st, "a_cqn", [128, 4, 640], BF16)
            ss2 = self.sb(st, "a_ss2", [128, 8], F32)
            rs2 = self.sb(st, "a_rs2", [128, 8], F32)
            cT = Ring("a_cT", [self.sb(st, "a_cT%d" % i, [128, 5, T], BF16) for i in range(2)])
            cc = Ring("a_cc", [self.sb(st, "a_cc%d" % i, [128, T], F32) for i in range(2)])
            sn = Ring("a_sn", [self.sb(st, "a_sn%d" % i, [128, T], F32) for i in range(2)])
            ob = Ring("a_ob", [self.sb(st, "a_ob%d" % i, [128, T], BF16) for i in range(6)])
            t1 = Ring("a_t1", [self.sb(st, "a_t1%d" % i, [128, T], F32) for i in range(2)])
            t2 = Ring("a_t2", [self.sb(st, "a_t2%d" % i, [128, T], F32) for i in range(2)])
            vt = Ring("a_vt", [self.sb(st, "a_vt%d" % i, [128, 4, 1024], BF16) for i in range(2)])
            pp = Ring("a_pp", [self.ps(st, "a_pp%d" % i, [128, 512], F32) for i in range(6)])
            junk = R["junk"]
            flip = 0
            for t in range(nt):
                h, hk = hT.next()
                self.front(R, src, sname, pi, t, gcol, "a_gcol", h, hk)
                hreads = [(hk, c) for c in range(8)]
                for b in range(4):
                    pa, kpa = pp.next()
                    for k in range(8):
                        self.mm(pa[:, 0:384], h[:, k, b * 128:(b + 1) * 128], wdq[:, k, :], k == 0, k == 7,
                                reads=hreads + ["a_wdq"], writes=[kpa])
                    self.cp("dve", cqf[:, b, 0:384], pa[:, 0:384], reads=[kpa], writes=[("a_cqf", b, 0)])
                    pb, kpb = pp.next()
                    for k in range(8):
                        self.mm(pb[:, 0:256], h[:, k, b * 128:(b + 1) * 128], wdkvc[:, k, :], k == 0, k == 7,
                                reads=hreads + ["a_wdkvc"], writes=[kpb])
                    self.cp("dve", cqf[:, b, 384:640], pb[:, 0:256], reads=[kpb], writes=[("a_cqf", b, 1)])
                for b in range(4):
                    self.act(junk[:, 0:384], cqf[:, b, 0:384], AF.Square, reads=[("a_cqf", b, 0)],
                             writes=[("a_ss2", b)], accum=ss2[:, b:b + 1])
                    self.act(junk[:, 0:256], cqf[:, b, 384:640], AF.Square, reads=[("a_cqf", b, 1)],
                             writes=[("a_ss2", 4 + b)], accum=ss2[:, 4 + b:5 + b])
                self.act(rs2[:, 0:4], ss2[:, 0:4], AF.Ln, reads=[("a_ss2", b) for b in range(4)],
                         writes=[("a_rs2", 0)], scale=1.0 / 384, bias=EPS)
                self.act(rs2[:, 4:8], ss2[:, 4:8], AF.Ln, reads=[("a_ss2", 4 + b) for b in range(4)],
                         writes=[("a_rs2", 1)], scale=1.0 / 256, bias=EPS)
                self.act(rs2[:], rs2[:], AF.Exp, reads=[("a_rs2", 0), ("a_rs2", 1)], writes=[("a_rs2", 0), ("a_rs2", 1)],
                         scale=-0.5)
                for b in range(4):
                    self.ts("dve", cqn[:, b, 0:384], cqf[:, b, 0:384], rs2[:, b:b + 1], None, ALU.mult, None,
                            reads=[("a_cqf", b, 0), ("a_rs2", 0)], writes=[("a_cqn", b, 0)])
                    self.ts("dve", cqn[:, b, 384:640], cqf[:, b, 384:640], rs2[:, 4 + b:5 + b], None, ALU.mult, None,
                            reads=[("a_cqf", b, 1), ("a_rs2", 1)], writes=[("a_cqn", b, 1)])
                ct, kct = cT.next()
                for cpair in range(3):
                    pt, kpt = R["pt"].next()
                    cs = [c for c in (2 * cpair, 2 * cpair + 1) if c < 5]
                    for c in cs:
                        for b in range(4):
                            self.tr(pt[:, (c % 2) * 512 + b * 128:(c % 2) * 512 + (b + 1) * 128],
                                    cqn[:, b, c * 128:(c + 1) * 128],
                                    reads=[("a_cqn", b, 0 if c < 3 else 1)], writes=[kpt])
                    for c in cs:
                        gc = qncol[:, c:c + 1] if c < 3 else kvncol[:, c - 3:c - 2]
                        gk = "a_qn" if c < 3 else "a_kvn"
                        ptc = pt[:, (c % 2) * 512:(c % 2 + 1) * 512]
                        if True:
                            self.act(ct[:, c, :], ptc, AF.Copy, reads=[kpt, gk], writes=[(kct, c)], scale=gc)
                        else:
                            self.ts("dve", ct[:, c, :], ptc, gc, None, ALU.mult, None, reads=[kpt, gk], writes=[(kct, c)])
                creads_q = [(kct, c) for c in range(3)]
                creads_kv = [(kct, 3), (kct, 4)]
                cct, kcc = cc.next()
                snt, ksn = sn.next()
                self.dma("sp", cct[:], self.c["ccm"][:, t * T:(t + 1) * T], reads=[], writes=[kcc])
                self.dma("sp", snt[:], self.c["ssm"][:, t * T:(t + 1) * T], reads=[], writes=[ksn])
                for hh in range(NH):
                    pq, kpq = pp.next()
                    for k in range(3):
                        self.mm(pq[:], wq[:, k, hh * 128:(hh + 1) * 128], ct[:, k, :], k == 0, k == 2,
                                reads=creads_q + ["a_wq"], writes=[kpq])
                    o, ko = ob.next()
                    eng = "act" if flip % 2 == 0 else "dve"
                    flip += 1
                    self.cp(eng, o[:], pq[:], reads=[kpq], writes=[ko])
                    self.dma("pool", QTn[hh, :, t * T:(t + 1) * T], o[:], reads=[ko], writes=[("a_qtn", pi, hh, t)])
                for hp in range(4):
                    pA, kpA = pp.next()
                    for k in range(3):
                        self.mm(pA[:], wq[:, k, 1024 + hp * 128:1024 + (hp + 1) * 128], ct[:, k, :], k == 0, k == 2,
                                reads=creads_q + ["a_wq"], writes=[kpA])
                    pB, kpB = pp.next()
                    for k in range(3):
                        self.mm(pB[:], wq[:, k, 1536 + hp * 128:1536 + (hp + 1) * 128], ct[:, k, :], k == 0, k == 2,
                                reads=creads_q + ["a_wq"], writes=[kpB])
                    a1, k1 = t1.next()
                    a2, k2 = t2.next()
                    self.tt("dve", a1[:], pA[:], cct[:], ALU.mult, reads=[kpA, kcc], writes=[k1])
                    self.tt("dve", a2[:], pB[:], snt[:], ALU.mult, reads=[kpB, ksn], writes=[k2])
                    o, ko = ob.next()
                    self.tt("pool", o[:], a1[:], a2[:], ALU.add, reads=[k1, k2], writes=[ko])
                    self.dma("pool", QTr[hp * 128:(hp + 1) * 128, t * T:(t + 1) * T], o[:], reads=[ko],
                             writes=[("a_qtr", pi, hp, t)])
                pA, kpA = pp.next()
                for k in range(8):
                    self.mm(pA[0:64, :], wdkvr[:, k, 0:64], h[:, k, :], k == 0, k == 7,
                            reads=hreads + ["a_wdkvr"], writes=[kpA])
                pB, kpB = pp.next()
                for k in range(8):
                    self.mm(pB[0:64, :], wdkvr[:, k, 64:128], h[:, k, :], k == 0, k == 7,
                            reads=hreads + ["a_wdkvr"], writes=[kpB])
                a1, k1 = t1.next()
                a2, k2 = t2.next()
                self.tt("dve", a1[0:64, :], pA[0:64, :], cct[0:64, :], ALU.mult, reads=[kpA, kcc], writes=[k1])
                self.tt("dve", a2[0:64, :], pB[0:64, :], snt[0:64, :], ALU.mult, reads=[kpB, ksn], writes=[k2])
                o, ko = ob.next()
                self.tt("pool", o[0:64, :], a1[0:64, :], a2[0:64, :], ALU.add, reads=[k1, k2], writes=[ko])
                self.dma("pool", KPE[:, t * T:(t + 1) * T], o[0:64, :], reads=[ko], writes=[("a_kpe", pi, t)])
                for hh in range(NH):
                    pq, kpq = pp.next()
                    for k in range(2):
                        self.mm(pq[:], wukv[:, k, hh * 128:(hh + 1) * 128], ct[:, 3 + k, :], k == 0, k == 1,
                                reads=creads_kv + ["a_wukv"], writes=[kpq])
                    o, ko = ob.next()
                    eng = "act" if flip % 2 == 0 else "dve"
                    flip += 1
                    self.cp(eng, o[:], pq[:], reads=[kpq], writes=[ko])
                    self.dma("pool", KTn[hh, :, t * T:(t + 1) * T], o[:], reads=[ko], writes=[("a_ktn", pi, hh, t)])
                v, kv = vt.next()
                for b in range(4):
                    for half in range(2):
                        pq, kpq = pp.next()
                        for k in range(2):
                            self.mm(pq[:], ct[:, 3 + k, b * 128:(b + 1) * 128],
                                    wukv[:, k, 1024 + half * 512:1024 + (half + 1) * 512], k == 0, k == 1,
                                    reads=creads_kv + ["a_wukv"], writes=[kpq])
                        eng = "act" if flip % 2 == 0 else "dve"
                        flip += 1
                        self.cp(eng, v[:, b, half * 512:(half + 1) * 512], pq[:], reads=[kpq], writes=[(kv, b, half)])
                for hh in range(NH):
                    self.dma("pool", VP[hh, :, t * 4:(t + 1) * 4, :], v[:, :, hh * 128:(hh + 1) * 128],
                             reads=[(kv, b, hh // 4) for b in range(4)], writes=[("a_vp", pi, hh, t)])
            self.phase_end()

        st = self.phase_begin()
        with st:
            nkc = max(1, L // KCH)
            kch = min(KCH, L)
            bpc = kch // 128
            qn = Ring("b_qn", [self.sb(st, "b_qn%d" % i, [128, T], BF16) for i in range(2)])
            qr = Ring("b_qr", [self.sb(st, "b_qr%d" % i, [128, T], BF16) for i in range(2)])
            kn = Ring("b_kn", [self.sb(st, "b_kn%d" % i, [128, kch], BF16) for i in range(3)])
            kr = Ring("b_kr", [self.sb(st, "b_kr%d" % i, [128, kch], BF16) for i in range(3)])
            for i_, ap_ in enumerate(qr.aps):
                self.S.op("dve", lambda e, a=ap_: e.memset(a[:], 0.0), [], [("b_qr", i_)])
            for i_, ap_ in enumerate(kr.aps):
                self.S.op("dve", lambda e, a=ap_: e.memset(a[:], 0.0), [], [("b_kr", i_)])
            vv = Ring("b_vv", [self.sb(st, "b_vv%d" % i, [128, bpc, 128], BF16) for i in range(3)])
            pT = Ring("b_pT", [self.sb(st, "b_pT%d" % i, [128, T], BF16) for i in range(5)])
            ps2 = Ring("b_ps2", [self.sb(st, "b_ps2%d" % i, [128, T], BF16) for i in range(3)])
            rsum = Ring("b_rs", [self.sb(st, "b_rs%d" % i, [128, T], F32) for i in range(2)])
            oo = Ring("b_oo", [self.sb(st, "b_oo%d" % i, [128, T], BF16) for i in range(2)])
            pss = Ring("b_pss", [self.ps(st, "b_pss%d" % i, [128, 512], F32) for i in range(3)])
            pso = Ring("b_pso", [self.ps(st, "b_pso%d" % i, [128, 512], F32) for i in range(2)])
            psm = Ring("b_psm", [self.ps(st, "b_psm%d" % i, [128, 512], F32) for i in range(2)])
            ones = self.onest
            for hh in range(NH):
                for qt in range(nt):
                    q1, kq1 = qn.next()
                    q2, kq2 = qr.next()
                    self.dma("sp", q1[:], QTn[hh, :, qt * T:(qt + 1) * T], reads=[("a_qtn", pi, hh, qt)], writes=[kq1])
                    self.dma("sp", q2[0:64, :], QTr[hh * 64:(hh + 1) * 64, qt * T:(qt + 1) * T],
                             reads=[("a_qtr", pi, hh // 2, qt)], writes=[kq2])
                    po, kpo = pso.next()
                    pm, kpm = psm.next()
                    nblk_tot = nkc * bpc
                    chunks = {}

                    def get_chunk(kc, hh=hh):
                        if kc not in chunks:
                            k1, kk1 = kn.next()
                            k2, kk2 = kr.next()
                            v1, kv1 = vv.next()
                            tl = [kc * (kch // T) + x for x in range(max(1, kch // T))]
                            self.dma("sp", k1[:], KTn[hh, :, kc * kch:(kc + 1) * kch],
                                     reads=[("a_ktn", pi, hh, x) for x in tl], writes=[kk1])
                            self.dma("sp", k2[0:64, :], KPE[:, kc * kch:(kc + 1) * kch],
                                     reads=[("a_kpe", pi, x) for x in tl], writes=[kk2])
                            self.dma("sp", v1[:], VP[hh, :, kc * bpc:(kc + 1) * bpc, :],
                                     reads=[("a_vp", pi, hh, x) for x in tl], writes=[kv1])
                            chunks[kc] = (k1, kk1, k2, kk2, v1, kv1)
                        return chunks[kc]

                    def emit_qk(i):
                        kc, b = divmod(i, bpc)
                        k1, kk1, k2, kk2, v1, kv1 = get_chunk(kc)
                        s, ks = pss.next()
                        self.mm(s[:], k1[:, b * 128:(b + 1) * 128], q1[:], True, False, reads=[kk1, kq1], writes=[ks])
                        self.mm(s[:], k2[:, b * 128:(b + 1) * 128], q2[:, :], False, True,
                                reads=[kk2, kq2], writes=[ks])
                        return s, ks, v1[:, b, :], kv1

                    LOOK = 2
                    pend_sum = None
                    prev_p = None
                    Sq = {}
                    for i in range(min(LOOK, nblk_tot)):
                        Sq[i] = emit_qk(i)
                    for jb in range(nblk_tot):
                        if jb + LOOK < nblk_tot:
                            Sq[jb + LOOK] = emit_qk(jb + LOOK)
                        s, ks, vap, kv1 = Sq.pop(jb)
                        p, kp = pT.next()
                        self.act(p[:], s[:], AF.Exp, reads=[ks], writes=[kp], scale=MLA_SCALE)
                        self.mm(po[:], vap, p[:], jb == 0, jb == nblk_tot - 1, reads=[kv1, kp], writes=[kpo])
                        if jb % 2 == 0:
                            if pend_sum is not None:
                                s2_, ks2_, first_ = pend_sum
                                self.mm(pm[:], ones[:], s2_[:], first_, False, reads=["ones", ks2_], writes=[kpm])
                                pend_sum = None
                            prev_p = (p, kp)
                        else:
                            s2, ks2 = ps2.next()
                            self.tt("dve", s2[:], prev_p[0][:], p[:], ALU.add, reads=[prev_p[1], kp], writes=[ks2])
                            pend_sum = (s2, ks2, jb == 1)
                    s2_, ks2_, first_ = pend_sum
                    self.mm(pm[:], ones[:], s2_[:], first_, True, reads=["ones", ks2_], writes=[kpm])
                    r, kr_ = rsum.next()
                    self.act(r[:], pm[:], AF.Ln, reads=[kpm], writes=[kr_])
                    self.act(r[:], r[:], AF.Exp, reads=[kr_], writes=[kr_], scale=-1.0)
                    o, ko = oo.next()
                    self.tt("dve", o[:], po[:], r[:], ALU.mult, reads=[kpo, kr_], writes=[ko])
                    self.dma("pool", OT[hh, :, qt * T:(qt + 1) * T], o[:], reads=[ko], writes=[("a_ot", pi, hh, qt)])
            self.phase_end()

        st = self.phase_begin()
        with st:
            wo = self.load_w(st, "c_wo", pre + "wo", 8, 1024)
            ot = Ring("c_ot", [self.sb(st, "c_ot%d" % i, [128, 8, T], BF16) for i in range(2)])
            xr = Ring("c_xr", [self.sb(st, "c_xr%d" % i, [128, 4, D], F32) for i in range(2)])
            xo = Ring("c_xo", [self.sb(st, "c_xo%d" % i, [128, 4, D], F32) for i in range(2)])
            pp = Ring("c_pp", [self.ps(st, "c_pp%d" % i, [128, 512], F32) for i in range(4)])
            for t in range(nt):
                o, ko = ot.next()
                for hh in range(NH):
                    self.dma("sp", o[:, hh, :], OT[hh, :, t * T:(t + 1) * T], reads=[("a_ot", pi, hh, t)],
                             writes=[(ko, hh)])
                x, kx = xr.next()
                self.dma("sp", x[:], src[t * T:(t + 1) * T, :].rearrange("(b p) d -> p b d", p=128),
                         reads=[(sname, pi, t)], writes=[kx])
                y, ky = xo.next()
                for b in range(4):
                    for half in range(2):
                        pq, kpq = pp.next()
                        for hh in range(NH):
                            self.mm(pq[:], o[:, hh, b * 128:(b + 1) * 128], wo[:, hh, half * 512:(half + 1) * 512],
                                    hh == 0, hh == NH - 1, reads=[(ko, hh), "c_wo"], writes=[kpq])
                        self.tt("dve", y[:, b, half * 512:(half + 1) * 512], pq[:], x[:, b, half * 512:(half + 1) * 512],
                                ALU.add, reads=[kpq, kx], writes=[(ky, b, half)])
                self.dma("pool", self.X[pi][t * T:(t + 1) * T, :].rearrange("(b p) d -> p b d", p=128), y[:],
                         reads=[(ky, b, hf) for b in range(4) for hf in range(2)], writes=[("X%d" % pi, pi, t)])
            self.phase_end()

    def _pv(self, pend, po, kpo, pm, kpm, ones):
        p, kp, v, kv, first, last = pend
        self.mm(po[:], v, p[:], first, last, reads=[kv, kp], writes=[kpo])
        self.mm(pm[:], ones[:], p[:], first, last, reads=["ones", kp], writes=[kpm])

    def mlp_layer(self, li, pi):
        L = self.piece_lens[pi]
        TS = 1024 if L % 1024 == 0 else 512
        nsup = L // TS
        nsub = TS // T
        nb = TS // 128
        Xp = self.X[pi]
        sname = "X%d" % pi
        st = self.phase_begin()
        with st:
            R = self.front_alloc(st)
            gcol = self.load_c(st, "m_gcol", "nffn_col", [128, 8], self.c["nffn_col"][:, li * 8:(li + 1) * 8])
            hT = self.sb(st, "m_hT", [128, 8, TS], BF16)
            w1 = Ring("m_w1", [self.sb(st, "m_w1%d" % i, [128, 8, 512], BF16) for i in range(3)])
            w2 = Ring("m_w2", [self.sb(st, "m_w2%d" % i, [128, 4, 1024], BF16) for i in range(3)])
            aT = Ring("m_aT", [self.sb(st, "m_aT%d" % i, [128, 4, TS], BF16) for i in range(2)])
            rl = Ring("m_rl", [self.sb(st, "m_rl%d" % i, [128, T], BF16) for i in range(3)])
            yacc = self.sb(st, "m_yacc", [128, nb, D], F32)
            xr = Ring("m_xr", [self.sb(st, "m_xr%d" % i, [128, D], F32) for i in range(2)])
            pa = Ring("m_pa", [self.ps(st, "m_pa%d" % i, [128, 512], F32) for i in range(3)])
            py = Ring("m_py", [self.ps(st, "m_py%d" % i, [128, 512], F32) for i in range(3)])
            w1n, w2n = "w1_%d" % li, "w2_%d" % li
            for s in range(nsup):
                for u in range(nsub):
                    t = s * nsub + u
                    self.front(R, Xp, sname, pi, t, gcol, "m_gcol", hT[:, :, u * T:(u + 1) * T], ("m_hT", u))
                import os
                DBG = int(os.environ.get("KDBG", "9"))
                for g in range(8):
                    if DBG < 2:
                        break
                    a1, ka1 = w1.next()
                    a2, ka2 = w2.next()
                    self.dma("sp", a1[:], self.wb[w1n][:, g * 512:(g + 1) * 512].rearrange("(c p) n -> p c n", p=128),
                             reads=[("wb", w1n)], writes=[ka1])
                    self.dma("sp", a2[:], self.wb[w2n][g * 512:(g + 1) * 512, :].rearrange("(c p) n -> p c n", p=128),
                             reads=[("wb", w2n)], writes=[ka2])
                    at, kat = aT.next()
                    if DBG < 3:
                        continue
                    for jc in range(4):
                        for u in range(nsub):
                            p, kp = pa.next()
                            for k in range(8):
                                self.mm(p[:], a1[:, k, jc * 128:(jc + 1) * 128], hT[:, k, u * T:(u + 1) * T],
                                        k == 0, k == 7, reads=[ka1] + [(("m_hT", u), c) for c in range(8)], writes=[kp])
                            r, kr = rl.next()
                            self.act(r[:], p[:], AF.Relu, reads=[kp], writes=[kr])
                            self.tt("pool", at[:, jc, u * T:(u + 1) * T], r[:], r[:], ALU.mult, reads=[kr],
                                    writes=[(kat, jc, u)])
                    if DBG < 4:
                        continue
                    for b in range(nb):
                        if g == 0:
                            x, kx = xr.next()
                            self.dma("sp", x[:], Xp[s * TS + b * 128:s * TS + (b + 1) * 128, :],
                                     reads=[(sname, pi, (s * TS + b * 128) // T)], writes=[kx])
                        for half in range(2):
                            p, kp = py.next()
                            for jc in range(4):
                                self.mm(p[:], at[:, jc, b * 128:(b + 1) * 128], a2[:, jc, half * 512:(half + 1) * 512],
                                        jc == 0, jc == 3, reads=[ka2, (kat, jc, b // 4)], writes=[kp])
                            ysl = yacc[:, b, half * 512:(half + 1) * 512]
                            if g == 0:
                                self.tt("dve", ysl, p[:], x[:, half * 512:(half + 1) * 512], ALU.add,
                                        reads=[kp, kx], writes=[("m_yacc", b, half)])
                            else:
                                self.tt("dve", ysl, p[:], ysl, ALU.add, reads=[kp, ("m_yacc", b, half)],
                                        writes=[("m_yacc", b, half)])
                for u in range(nsub):
                    t = s * nsub + u
                    self.dma("pool", Xp[t * T:(t + 1) * T, :].rearrange("(b p) d -> p b d", p=128),
                             yacc[:, u * 4:(u + 1) * 4, :],
                             reads=[("m_yacc", b, hf) for b in range(u * 4, u * 4 + 4) for hf in range(2)],
                             writes=[(sname, pi, t)])
            self.phase_end()

    def final_norm(self, pi):
        L = self.piece_lens[pi]
        nt = L // T
        st = self.phase_begin()
        with st:
            R = self.front_alloc(st)
            grep = self.load_c(st, "fn_g", "final_rep", [128, D], self.c["final_rep"][:, :])
            yo = Ring("fn_y", [self.sb(st, "fn_y%d" % i, [128, 4, D], F32) for i in range(2)])
            for t in range(nt):
                xin, kx, rs, krs, _, _ = self.front_stats(R, self.X[pi], "X%d" % pi, pi, t, want_xn=False)
                y, ky = yo.next()
                for b in range(4):
                    self.stt(y[:, b, :], xin[:, b, :], rs[:, b:b + 1], grep[:], ALU.mult, ALU.mult,
                             reads=[kx, krs, "fn_g"], writes=[(ky, b)])
                self.dma("pool", self.yout[pi][t * T:(t + 1) * T, :].rearrange("(b p) d -> p b d", p=128), y[:],
                         reads=[(ky, b) for b in range(4)], writes=[("yout", pi, t)])
            self.phase_end()

    def pool_layer(self, li, pi, src, sname):
        L = self.piece_lens[pi]
        nt = L // T
        nblk = L // 128
        H = self.scratch("p_h%d" % pi, [L, D])
        st = self.phase_begin()
        with st:
            R = self.front_alloc(st)
            grep = self.load_c(st, "pa_g", "nmix2_rep", [128, D], self.c["nmix2_rep"][:, :])
            ho = Ring("pa_h", [self.sb(st, "pa_h%d" % i, [128, 4, D], BF16) for i in range(2)])
            for t in range(nt):
                xin, kx, rs, krs, _, _ = self.front_stats(R, src, sname, pi, t, want_xn=False)
                y, ky = ho.next()
                for b in range(4):
                    self.stt(y[:, b, :], xin[:, b, :], rs[:, b:b + 1], grep[:], ALU.mult, ALU.mult,
                             reads=[kx, krs, "pa_g"], writes=[(ky, b)])
                self.dma("pool", H[t * T:(t + 1) * T, :].rearrange("(b p) d -> p b d", p=128), y[:],
                         reads=[(ky, b) for b in range(4)], writes=[("p_h", pi, t)])
            self.phase_end()
        st = self.phase_begin()
        with st:
            wp = self.load_w(st, "pb_w", "p_w", 8, 256)
            srep = self.load_c(st, "pb_s", "pscale_rep", [128, D], self.c["pscale_rep"][:, :])
            hb = Ring("pb_hb", [self.sb(st, "pb_hb%d" % i, [128, 6, D], BF16) for i in range(2)])
            bt = Ring("pb_bt", [self.sb(st, "pb_bt%d" % i, [128, 24, 512], BF16) for i in range(2)])
            rc = Ring("pb_rc", [self.sb(st, "pb_rc%d" % i, [128, 4, 512], F32) for i in range(2)])
            dT = Ring("pb_dT", [self.sb(st, "pb_dT%d" % i, [128, 8, T], BF16) for i in range(2)])
            xr = Ring("pb_xr", [self.sb(st, "pb_xr%d" % i, [128, 4, D], F32) for i in range(2)])
            tm = Ring("pb_tm", [self.sb(st, "pb_tm%d" % i, [128, 512], F32) for i in range(2)])
            xo = Ring("pb_xo", [self.sb(st, "pb_xo%d" % i, [128, 4, D], F32) for i in range(2)])
            pd = Ring("pb_pd", [self.ps(st, "pb_pd%d" % i, [128, 512], F32) for i in range(3)])
            pm = Ring("pb_pm", [self.ps(st, "pb_pm%d" % i, [128, 512], F32) for i in range(3)])
            for t in range(nt):
                var = 0 if t == 0 else (2 if t == nt - 1 else 1)
                if nt == 1:
                    raise NotImplementedError
                h6, kh = hb.next()
                blks = [r for r in range(6) if 0 <= t * 4 - 1 + r < nblk]
                r0, r1 = blks[0], blks[-1] + 1
                g0 = t * 4 - 1 + r0
                tiles = sorted(set((g0 + i) // 4 for i in range(r1 - r0)))
                self.dma("sp", h6[:, r0:r1, :],
                         H[g0 * 128:(g0 + r1 - r0) * 128, :].rearrange("(b p) d -> p b d", p=128),
                         reads=[("p_h", pi, x) for x in tiles], writes=[kh])
                b_, kb = bt.next()
                self.dma("sp", b_[:], self.wb["pool_bt"][var * 3072:(var + 1) * 3072, :].rearrange("(q p) n -> p q n", p=128),
                         reads=[("wb", "pool_bt")], writes=[kb])
                rc_, krc = rc.next()
                self.dma("sp", rc_[:], self.c["pool_rc"][var * 512:(var + 1) * 512, :].rearrange("(g p) n -> p g n", p=128),
                         reads=[], writes=[krc])
                x, kx = xr.next()
                self.dma("sp", x[:], src[t * T:(t + 1) * T, :].rearrange("(b p) d -> p b d", p=128),
                         reads=[(sname, pi, t)], writes=[kx])
                d, kd = dT.next()
                for c in range(8):
                    g = c // 2
                    p, kp = pd.next()
                    for i, r in enumerate(blks):
                        self.mm(p[:], h6[:, r, c * 128:(c + 1) * 128], b_[:, g * 6 + r, :], i == 0, i == len(blks) - 1,
                                reads=[kh, kb], writes=[kp])
                    self.tt("dve", d[:, c, :], p[:], rc_[:, g, :], ALU.mult, reads=[kp, krc], writes=[(kd, c)])
                y, ky = xo.next()
                for b in range(4):
                    for half in range(2):
                        p, kp = pm.next()
                        for gg in range(2):
                            g = half * 2 + gg
                            for cc_ in range(2):
                                self.mm(p[:, gg * 256:(gg + 1) * 256], d[:, 2 * g + cc_, b * 128:(b + 1) * 128],
                                        wp[:, 2 * g + cc_, :], cc_ == 0, cc_ == 1,
                                        reads=[(kd, 2 * g + cc_), "pb_w"], writes=[kp])
                        m, km = tm.next()
                        self.tt("dve", m[:], p[:], srep[:, half * 512:(half + 1) * 512], ALU.mult,
                                reads=[kp, "pb_s"], writes=[km])
                        self.tt("pool", y[:, b, half * 512:(half + 1) * 512], m[:], x[:, b, half * 512:(half + 1) * 512],
                                ALU.add, reads=[km, kx], writes=[(ky, b, half)])
                self.dma("pool", self.X[pi][t * T:(t + 1) * T, :].rearrange("(b p) d -> p b d", p=128), y[:],
                         reads=[(ky, b, hf) for b in range(4) for hf in range(2)], writes=[("X%d" % pi, pi, t)])
            self.phase_end()

    def ret_tables(self, st, want_q=True, want_m=True):
        Tb = {}
        raw = self.load_c(st, "rt_raw", "decay_rep", [128, 8], self.c["decay_rep"][:, :])
        dmask = self.load_c(st, "rt_dm", "dmask", [128, 256], self.c["dmask"][:, :])
        ind = self.load_c(st, "rt_ind", "ind", [128, 256], self.c["ind"][:, :])
        e12 = self.load_c(st, "rt_e12", "e12", [128, 1024], self.c["e12"][:, :]) if want_q else None
        pcol = self.load_c(st, "rt_pc", "pcol", [128, 2], self.c["pcol"][:, :])
        lg = self.sb(st, "rt_lg", [128, 8], F32)
        tmp = self.sb(st, "rt_tmp", [128, 8], F32)
        self.act(tmp[:], raw[:], AF.Exp, reads=["rt_raw"], writes=["rt_tmp"])
        self.act(tmp[:], tmp[:], AF.Ln, reads=["rt_tmp"], writes=["rt_tmp"], bias=1.0)
        self.ts("dve", lg[:], tmp[:], -1.0, None, ALU.mult, None, reads=["rt_tmp"], writes=["rt_lg"])
        mask = self.sb(st, "rt_mask", [128, 2, 512], F32) if want_m else None
        mtmp = self.sb(st, "rt_mtmp", [128, 2, 512], F32) if want_m else None
        qdec = self.sb(st, "rt_qdec", [128, 8, 512], F32) if want_q else None
        kdec = self.sb(st, "rt_kdec", [128, 8], F32)
        cdec = self.sb(st, "rt_cdec", [128, 8], F32)
        for d in range(2):
            for h in range(4):
                col = d * 4 + h
                if want_m:
                    self.act(mtmp[:, d, h * 128:(h + 1) * 128], dmask[:, d * 128:(d + 1) * 128], AF.Exp,
                             reads=["rt_dm", "rt_lg"], writes=[("rt_mtmp", col)], scale=lg[:, col:col + 1])
                    self.tt("dve", mask[:, d, h * 128:(h + 1) * 128], mtmp[:, d, h * 128:(h + 1) * 128],
                            ind[:, d * 128:(d + 1) * 128], ALU.mult, reads=[("rt_mtmp", col), "rt_ind"],
                            writes=[("rt_mask", col)])
                if want_q:
                    self.act(qdec[:, col, :], e12[:, d * 512:(d + 1) * 512], AF.Exp, reads=["rt_e12", "rt_lg"],
                             writes=[("rt_qdec", col)], scale=lg[:, col:col + 1])
                self.act(kdec[:, col:col + 1], pcol[:, d:d + 1], AF.Exp, reads=["rt_pc", "rt_lg"],
                         writes=[("rt_kdec0", col)], scale=lg[:, col:col + 1])
        self.ts("dve", kdec[:], kdec[:], 1.0 / 16.0, None, ALU.mult, None,
                reads=[("rt_kdec0", c) for c in range(8)], writes=["rt_kdec"])
        self.act(cdec[:], lg[:], AF.Exp, reads=["rt_lg"], writes=["rt_cdec"], scale=128.0)
        Tb.update(mask=mask, qdec=qdec, kdec=kdec, cdec=cdec)
        Tb["mask_keys"] = [("rt_mask", c) for c in range(8)]
        return Tb

    def ret_layer(self, li, pi, src, sname):
        L = self.piece_lens[pi]
        nt = L // T
        RQ = self.scratch("r_q%d" % pi, [8, 128, L])
        RQF = self.scratch("r_qf%d" % pi, [8, 128, L])
        RQB = self.scratch("r_qb%d" % pi, [8, 128, L])
        RKT = self.scratch("r_kt%d" % pi, [8, 128, L])
        RKF = self.scratch("r_kf%d" % pi, [L, 1024])
        RKB = self.scratch("r_kb%d" % pi, [L, 1024])
        RV = self.scratch("r_v%d" % pi, [L, 2048])
        RG = self.scratch("r_g%d" % pi, [L, 2048])
        GO = self.scratch("r_go%d" % pi, [L, 2048])
        OF = self.scratch("r_of%d" % pi, [L, 2048], F32)
        tok = lambda A, t: A[t * T:(t + 1) * T, :].rearrange("(b p) d -> p b d", p=128)

        st = self.phase_begin()
        with st:
            R = self.front_alloc(st)
            Tb = self.ret_tables(st, want_m=False)
            wq = self.load_w(st, "ra_wq", "r_wq", 8, 1024)
            wk = self.load_w(st, "ra_wk", "r_wk", 8, 1024)
            gcol = self.load_c(st, "ra_gcol", "nmix_col", [128, 8], self.c["nmix_col"][:, li * 8:(li + 1) * 8])
            hT = Ring("ra_hT", [self.sb(st, "ra_hT%d" % i, [128, 8, T], BF16) for i in range(2)])
            cs = Ring("ra_cs", [self.sb(st, "ra_cs%d" % i, [128, T], F32) for i in range(2)])
            sn = Ring("ra_sn", [self.sb(st, "ra_sn%d" % i, [128, T], F32) for i in range(2)])
            tA = Ring("ra_tA", [self.sb(st, "ra_tA%d" % i, [128, T], F32) for i in range(4)])
            tB = Ring("ra_tB", [self.sb(st, "ra_tB%d" % i, [128, T], F32) for i in range(4)])
            oo = Ring("ra_oo", [self.sb(st, "ra_oo%d" % i, [128, T], F32) for i in range(4)])
            ob = Ring("ra_ob", [self.sb(st, "ra_ob%d" % i, [128, T], BF16) for i in range(8)])
            kTt = self.sb(st, "ra_kTt", [128, 8, T], BF16)
            ktm = Ring("ra_ktm", [self.sb(st, "ra_ktm%d" % i, [128, 4, 1024], BF16) for i in range(2)])
            pp = Ring("ra_pp", [self.ps(st, "ra_pp%d" % i, [128, 512], F32) for i in range(4)])
            ptk = Ring("ra_ptk", [self.ps(st, "ra_ptk%d" % i, [128, 1024], BF16) for i in range(2)])
            for t in range(nt):
                h, hk = hT.next()
                self.front(R, src, sname, pi, t, gcol, "ra_gcol", h, hk)
                hreads = [(hk, c) for c in range(8)]
                c_, kc_ = cs.next()
                s_, ks_ = sn.next()
                self.dma("sp", c_[:], self.c["cosr"][:, t * T:(t + 1) * T], reads=[], writes=[kc_])
                self.dma("sp", s_[:], self.c["sinr"][:, t * T:(t + 1) * T], reads=[], writes=[ks_])
                for which in range(2):
                    w = wq if which == 0 else wk
                    wkey = "ra_wq" if which == 0 else "ra_wk"
                    for hh in range(4):
                        P12 = []
                        for half in range(2):
                            p, kp = pp.next()
                            oc = hh * 2 + half
                            for k in range(8):
                                self.mm(p[:], w[:, k, oc * 128:(oc + 1) * 128], h[:, k, :], k == 0, k == 7,
                                        reads=hreads + [wkey], writes=[kp])
                            P12.append((p, kp))
                        (p1, k1), (p2, k2) = P12
                        a1, ka1 = tA.next()
                        b1, kb1 = tB.next()
                        a2, ka2 = tA.next()
                        b2, kb2 = tB.next()
                        self.tt("dve", a1[:], p1[:], c_[:], ALU.mult, reads=[k1, kc_], writes=[ka1])
                        self.tt("dve", b1[:], p2[:], s_[:], ALU.mult, reads=[k2, ks_], writes=[kb1])
                        self.tt("dve", a2[:], p1[:], s_[:], ALU.mult, reads=[k1, ks_], writes=[ka2])
                        self.tt("dve", b2[:], p2[:], c_[:], ALU.mult, reads=[k2, kc_], writes=[kb2])
                        o1, ko1 = oo.next()
                        o2, ko2 = oo.next()
                        self.tt("pool", o1[:], a1[:], b1[:], ALU.subtract, reads=[ka1, kb1], writes=[ko1])
                        self.tt("pool", o2[:], a2[:], b2[:], ALU.add, reads=[ka2, kb2], writes=[ko2])
                        for half, (o, ko) in enumerate(((o1, ko1), (o2, ko2))):
                            oc = hh * 2 + half
                            if which == 0:
                                pb_, kpb = ob.next()
                                self.cp("act", pb_[:], o[:], reads=[ko], writes=[kpb])
                                self.dma("pool", RQ[oc, :, t * T:(t + 1) * T], pb_[:], reads=[kpb],
                                         writes=[("r_q", pi, oc, t)])
                                for d, RQD, nm in ((0, RQF, "r_qf"), (1, RQB, "r_qb")):
                                    pd_, kpd = ob.next()
                                    self.tt("pool", pd_[:], o[:], Tb["qdec"][:, d * 4 + hh, :], ALU.mult,
                                            reads=[ko, ("rt_qdec", d * 4 + hh)], writes=[kpd])
                                    self.dma("pool", RQD[oc, :, t * T:(t + 1) * T], pd_[:], reads=[kpd],
                                             writes=[(nm, pi, oc, t)])
                            else:
                                self.cp("act", kTt[:, oc, :], o[:], reads=[ko], writes=[("ra_kTt", oc)])
                                self.dma("pool", RKT[oc, :, t * T:(t + 1) * T], kTt[:, oc, :], reads=[("ra_kTt", oc)],
                                         writes=[("r_kt", pi, oc, t)])
                kf, kkf = ktm.next()
                kb, kkb = ktm.next()
                for b in range(4):
                    pt, kpt = ptk.next()
                    for oc in range(8):
                        self.tr(pt[:, oc * 128:(oc + 1) * 128], kTt[:, oc, b * 128:(b + 1) * 128],
                                reads=[("ra_kTt", oc)], writes=[kpt])
                    for hh in range(4):
                        self.act(kf[:, b, hh * 256:(hh + 1) * 256], pt[:, hh * 256:(hh + 1) * 256], AF.Copy,
                                 reads=[kpt, "rt_kdec"], writes=[(kkf, b, hh)], scale=Tb["kdec"][:, hh:hh + 1])
                        self.act(kb[:, b, hh * 256:(hh + 1) * 256], pt[:, hh * 256:(hh + 1) * 256], AF.Copy,
                                 reads=[kpt, "rt_kdec"], writes=[(kkb, b, hh)], scale=Tb["kdec"][:, 4 + hh:5 + hh])
                self.dma("pool", tok(RKF, t), kf[:], reads=[(kkf, b, hh) for b in range(4) for hh in range(4)],
                         writes=[("r_kf", pi, t)])
                self.dma("pool", tok(RKB, t), kb[:], reads=[(kkb, b, hh) for b in range(4) for hh in range(4)],
                         writes=[("r_kb", pi, t)])
            self.phase_end()

        st = self.phase_begin()
        with st:
            R = self.front_alloc(st)
            wv = self.load_w(st, "rb_wv", "r_wv", 8, 2048)
            wg = self.load_w(st, "rb_wg", "r_wg", 8, 2048)
            gcol = self.load_c(st, "rb_gcol", "nmix_col", [128, 8], self.c["nmix_col"][:, li * 8:(li + 1) * 8])
            hT = Ring("rb_hT", [self.sb(st, "rb_hT%d" % i, [128, 8, T], BF16) for i in range(2)])
            vt = Ring("rb_vt", [self.sb(st, "rb_vt%d" % i, [128, 4, 2048], BF16) for i in range(2)])
            gt = Ring("rb_gt", [self.sb(st, "rb_gt%d" % i, [128, 4, 2048], BF16) for i in range(2)])
            pp = Ring("rb_pp", [self.ps(st, "rb_pp%d" % i, [128, 512], F32) for i in range(6)])
            flip = 0
            for t in range(nt):
                h, hk = hT.next()
                self.front(R, src, sname, pi, t, gcol, "rb_gcol", h, hk)
                hreads = [(hk, c) for c in range(8)]
                v, kv = vt.next()
                g, kg = gt.next()
                for b in range(4):
                    for cc in range(4):
                        p, kp = pp.next()
                        for k in range(8):
                            self.mm(p[:], h[:, k, b * 128:(b + 1) * 128], wv[:, k, cc * 512:(cc + 1) * 512], k == 0, k == 7,
                                    reads=hreads + ["rb_wv"], writes=[kp])
                        eng = "dve" if flip % 2 == 0 else "act"
                        flip += 1
                        self.cp(eng, v[:, b, cc * 512:(cc + 1) * 512], p[:], reads=[kp], writes=[(kv, b, cc)])
                        p, kp = pp.next()
                        for k in range(8):
                            self.mm(p[:], h[:, k, b * 128:(b + 1) * 128], wg[:, k, cc * 512:(cc + 1) * 512], k == 0, k == 7,
                                    reads=hreads + ["rb_wg"], writes=[kp])
                        self.act(g[:, b, cc * 512:(cc + 1) * 512], p[:], AF.Silu, reads=[kp], writes=[(kg, b, cc)])
                self.dma("pool", tok(RV, t), v[:], reads=[(kv, b, cc) for b in range(4) for cc in range(4)],
                         writes=[("r_v", pi, t)])
                self.dma("pool", tok(RG, t), g[:], reads=[(kg, b, cc) for b in range(4) for cc in range(4)],
                         writes=[("r_g", pi, t)])
            self.phase_end()

        for d in range(2):
            st = self.phase_begin()
            with st:
                Tb = self.ret_tables(st, want_q=False)
                RQD = RQF if d == 0 else RQB
                RKD = RKF if d == 0 else RKB
                nqd = "r_qf" if d == 0 else "r_qb"
                nkd = "r_kf" if d == 0 else "r_kb"
                qt = Ring("rs_qt", [self.sb(st, "rs_qt%d" % i, [128, 8, T], BF16) for i in range(2)])
                kt = Ring("rs_kt", [self.sb(st, "rs_kt%d" % i, [128, 8, T], BF16) for i in range(2)])
                qd = Ring("rs_qd", [self.sb(st, "rs_qd%d" % i, [128, 8, T], BF16) for i in range(2)])
                km = Ring("rs_km", [self.sb(st, "rs_km%d" % i, [128, 4, 1024], BF16) for i in range(2)])
                vv = Ring("rs_vv", [self.sb(st, "rs_vv%d" % i, [128, 4, 2048], BF16) for i in range(2)])
                Sf = self.sb(st, "rs_Sf", [128, 8, 512], F32)
                Sb = self.sb(st, "rs_Sb", [128, 8, 512], BF16)
                am = Ring("rs_am", [self.sb(st, "rs_am%d" % i, [128, 512], BF16) for i in range(2)])
                of = Ring("rs_of", [self.sb(st, "rs_of%d" % i, [128, 2048], F32) for i in range(2)])
                patt = Ring("rs_patt", [self.ps(st, "rs_patt%d" % i, [128, 512], F32) for i in range(2)])
                po = Ring("rs_po", [self.ps(st, "rs_po%d" % i, [128, 512], F32) for i in range(2)])
                pd = Ring("rs_pd", [self.ps(st, "rs_pd%d" % i, [128, 512], F32) for i in range(4)])
                if d == 1:
                    ofl = Ring("rs_ofl", [self.sb(st, "rs_ofl%d" % i, [128, 2048], F32) for i in range(2)])
                    gl = Ring("rs_gl", [self.sb(st, "rs_gl%d" % i, [128, 2048], BF16) for i in range(2)])
                    on = Ring("rs_on", [self.sb(st, "rs_on%d" % i, [128, 2048], F32) for i in range(1)])
                    go = Ring("rs_go", [self.sb(st, "rs_go%d" % i, [128, 2048], BF16) for i in range(2)])
                    s1 = Ring("rs_s1", [self.sb(st, "rs_s1%d" % i, [128, 4], F32) for i in range(2)])
                    s2 = Ring("rs_s2", [self.sb(st, "rs_s2%d" % i, [128, 4], F32) for i in range(2)])
                    mu = Ring("rs_mu", [self.sb(st, "rs_mu%d" % i, [128, 4], F32) for i in range(2)])
                    vr = Ring("rs_vr", [self.sb(st, "rs_vr%d" % i, [128, 4], F32) for i in range(2)])
                    junk = self.sb(st, "rs_junk", [128, 512], BF16)
                self.S.op("dve", lambda e, Sf=Sf: e.memset(Sf[:], 0.0), [], [("rs_Sf", c) for c in range(8)])
                self.S.op("pool", lambda e, Sb=Sb: e.memset(Sb[:], 0.0), [], [("rs_Sb", c) for c in range(8)])
                flip = 0
                trange = range(nt) if d == 0 else range(nt - 1, -1, -1)
                for t in trange:
                    q_, kq = qt.next()
                    k_, kk = kt.next()
                    qd_, kqd = qd.next()
                    km_, kkm = km.next()
                    v_, kv = vv.next()
                    self.dma("sp", q_[:], RQ[:, :, t * T:(t + 1) * T].rearrange("c p n -> p c n"),
                             reads=[("r_q", pi, oc, t) for oc in range(8)], writes=[kq])
                    self.dma("sp", k_[:], RKT[:, :, t * T:(t + 1) * T].rearrange("c p n -> p c n"),
                             reads=[("r_kt", pi, oc, t) for oc in range(8)], writes=[kk])
                    self.dma("sp", qd_[:], RQD[:, :, t * T:(t + 1) * T].rearrange("c p n -> p c n"),
                             reads=[(nqd, pi, oc, t) for oc in range(8)], writes=[kqd])
                    self.dma("sp", km_[:], tok(RKD, t), reads=[(nkd, pi, t)], writes=[kkm])
                    self.dma("sp", v_[:], tok(RV, t), reads=[("r_v", pi, t)], writes=[kv])
                    brange = range(4) if d == 0 else range(3, -1, -1)
                    for b in brange:
                        n = t * 4 + b
                        sl = slice(b * 128, (b + 1) * 128)
                        pa, kpa = patt.next()
                        for hh in range(4):
                            for dc in range(2):
                                self.mm(pa[:, hh * 128:(hh + 1) * 128], k_[:, hh * 2 + dc, sl], q_[:, hh * 2 + dc, sl],
                                        dc == 0, dc == 1, reads=[kk, kq], writes=[kpa])
                        a_, ka = am.next()
                        self.tt("dve", a_[:], pa[:], Tb["mask"][:, d, :], ALU.mult, reads=[kpa] + Tb["mask_keys"], writes=[ka])
                        if d == 0:
                            o_, ko = of.next()
                        else:
                            ol, kol = ofl.next()
                            self.dma("sp", ol[:], OF[n * 128:(n + 1) * 128, :], reads=[("r_of", pi, n)], writes=[kol])
                            g_, kg = gl.next()
                            self.dma("sp", g_[:], RG[n * 128:(n + 1) * 128, :], reads=[("r_g", pi, t)], writes=[kg])
                            o_, ko = of.next()
                            s1_, ks1 = s1.next()
                            s2_, ks2 = s2.next()
                        for hh in range(4):
                            p, kp = po.next()
                            self.mm(p[:], a_[:, hh * 128:(hh + 1) * 128], v_[:, b, hh * 512:(hh + 1) * 512], True, False,
                                    reads=[ka, kv], writes=[kp])
                            for dc in range(2):
                                self.mm(p[:], qd_[:, hh * 2 + dc, sl], Sb[:, hh * 2 + dc, :], False, dc == 1,
                                        reads=[kqd, ("rs_Sb", hh * 2 + dc)], writes=[kp])
                            if d == 0:
                                self.cp("act", o_[:, hh * 512:(hh + 1) * 512], p[:], reads=[kp], writes=[(ko, hh)])
                            else:
                                self.tt("dve", o_[:, hh * 512:(hh + 1) * 512], p[:], ol[:, hh * 512:(hh + 1) * 512], ALU.add,
                                        reads=[kp, kol], writes=[(ko, hh)])
                                self.act(junk[:], o_[:, hh * 512:(hh + 1) * 512], AF.Copy, reads=[(ko, hh)],
                                         writes=[(ks1, hh)], accum=s1_[:, hh:hh + 1])
                                self.act(junk[:], o_[:, hh * 512:(hh + 1) * 512], AF.Square, reads=[(ko, hh)],
                                         writes=[(ks2, hh)], accum=s2_[:, hh:hh + 1])
                            for dc in range(2):
                                pq, kpq = pd.next()
                                self.mm(pq[:], km_[:, b, hh * 256 + dc * 128:hh * 256 + (dc + 1) * 128],
                                        v_[:, b, hh * 512:(hh + 1) * 512], True, True, reads=[kkm, kv], writes=[kpq])
                                c8 = hh * 2 + dc
                                self.stt(Sf[:, c8, :], Sf[:, c8, :], Tb["cdec"][:, d * 4 + hh:d * 4 + hh + 1], pq[:],
                                         ALU.mult, ALU.add, reads=[kpq, ("rs_Sf", c8), "rt_cdec"], writes=[("rs_Sf", c8)])
                                self.cp("act", Sb[:, c8, :], Sf[:, c8, :], reads=[("rs_Sf", c8)], writes=[("rs_Sb", c8)])
                        if d == 0:
                            self.dma("pool", OF[n * 128:(n + 1) * 128, :], o_[:], reads=[(ko, hh) for hh in range(4)],
                                     writes=[("r_of", pi, n)])
                        else:
                            mu_, kmu = mu.next()
                            vr_, kvr = vr.next()
                            self.ts("dve", mu_[:], s1_[:], 1.0 / 512, None, ALU.mult, None,
                                    reads=[(ks1, hh) for hh in range(4)], writes=[kmu])
                            self.tt("dve", vr_[:], mu_[:], mu_[:], ALU.mult, reads=[kmu], writes=[kvr])
                            self.stt(vr_[:], s2_[:], 1.0 / 512, vr_[:], ALU.mult, ALU.subtract,
                                     reads=[(ks2, hh) for hh in range(4)] + [kvr], writes=[kvr])
                            self.act(vr_[:], vr_[:], AF.Ln, reads=[kvr], writes=[kvr], bias=EPS)
                            self.act(vr_[:], vr_[:], AF.Exp, reads=[kvr], writes=[kvr], scale=-0.5)
                            on_, kon = on.next()
                            go_, kgo = go.next()
                            for hh in range(4):
                                self.ts("dve", on_[:, hh * 512:(hh + 1) * 512], o_[:, hh * 512:(hh + 1) * 512],
                                        mu_[:, hh:hh + 1], vr_[:, hh:hh + 1], ALU.subtract, ALU.mult,
                                        reads=[(ko, hh), kmu, kvr], writes=[(kon, hh)])
                                self.tt("pool", go_[:, hh * 512:(hh + 1) * 512], on_[:, hh * 512:(hh + 1) * 512],
                                        g_[:, hh * 512:(hh + 1) * 512], ALU.mult, reads=[(kon, hh), kg], writes=[(kgo, hh)])
                            self.dma("pool", GO[n * 128:(n + 1) * 128, :], go_[:], reads=[(kgo, hh) for hh in range(4)],
                                     writes=[("r_go", pi, n)])
                self.phase_end()

        st = self.phase_begin()
        with st:
            wo = self.load_w(st, "rd_wo", "r_wo", 16, 1024)
            gin = Ring("rd_gin", [self.sb(st, "rd_gin%d" % i, [128, 4, 2048], BF16) for i in range(2)])
            goT = Ring("rd_goT", [self.sb(st, "rd_goT%d" % i, [128, 16, T], BF16) for i in range(2)])
            xr = Ring("rd_xr", [self.sb(st, "rd_xr%d" % i, [128, 4, D], F32) for i in range(2)])
            xo = Ring("rd_xo", [self.sb(st, "rd_xo%d" % i, [128, 4, D], F32) for i in range(2)])
            ptr = Ring("rd_ptr", [self.ps(st, "rd_ptr%d" % i, [128, 1024], BF16) for i in range(3)])
            pp = Ring("rd_pp", [self.ps(st, "rd_pp%d" % i, [128, 512], F32) for i in range(4)])
            for t in range(nt):
                gi, kgi = gin.next()
                self.dma("sp", gi[:], tok(GO, t), reads=[("r_go", pi, t * 4 + b) for b in range(4)], writes=[kgi])
                x, kx = xr.next()
                self.dma("sp", x[:], tok(src, t), reads=[(sname, pi, t)], writes=[kx])
                gT, kgT = goT.next()
                for cp_ in range(8):
                    pt, kpt = ptr.next()
                    for cc in range(2):
                        c = cp_ * 2 + cc
                        for b in range(4):
                            self.tr(pt[:, cc * 512 + b * 128:cc * 512 + (b + 1) * 128], gi[:, b, c * 128:(c + 1) * 128],
                                    reads=[kgi], writes=[kpt])
                    self.cp("act", gT[:, cp_ * 2:cp_ * 2 + 2, :], pt[:].rearrange("p (c n) -> p c n", c=2),
                            reads=[kpt], writes=[(kgT, cp_)])
                y, ky = xo.next()
                for b in range(4):
                    for half in range(2):
                        pq, kpq = pp.next()
                        for c in range(16):
                            self.mm(pq[:], gT[:, c, b * 128:(b + 1) * 128], wo[:, c, half * 512:(half + 1) * 512],
                                    c == 0, c == 15, reads=[(kgT, c // 2), "rd_wo"], writes=[kpq])
                        self.tt("dve", y[:, b, half * 512:(half + 1) * 512], pq[:], x[:, b, half * 512:(half + 1) * 512],
                                ALU.add, reads=[kpq, kx], writes=[(ky, b, half)])
                self.dma("pool", tok(self.X[pi], t), y[:],
                         reads=[(ky, b, hf) for b in range(4) for hf in range(2)], writes=[("X%d" % pi, pi, t)])
            self.phase_end()


def rope_tables(lmax):
    pos = np.arange(lmax, dtype=np.float32)
    inv = (1.0 / (np.float32(10000.0) ** (np.arange(32, dtype=np.float32) * np.float32(2.0) / np.float32(64)))).astype(np.float32)
    ang = (pos[:, None] * inv[None, :]).astype(np.float32)
    cos, sin = np.cos(ang).astype(np.float32), np.sin(ang).astype(np.float32)
    ccm = np.zeros((128, lmax), np.float32)
    ssm = np.zeros((128, lmax), np.float32)
    for r in range(128):
        f = r % 32
        ccm[r] = cos[:, f]
        ssm[r] = sin[:, f] * (-1.0 if (r % 64) < 32 else 1.0)
    inv2 = (1.0 / (np.float32(10000.0) ** (np.arange(128, dtype=np.float32) * np.float32(2.0) / np.float32(256)))).astype(np.float32)
    ang2 = (pos[:, None] * inv2[None, :]).astype(np.float32)
    cosr = np.ascontiguousarray(np.cos(ang2).astype(np.float32).T)
    sinr = np.ascontiguousarray(np.sin(ang2).astype(np.float32).T)
    return ccm, ssm, cosr, sinr


def pool_tables():
    wins = (2, 4, 8, 16)
    bt = np.zeros((3, 4, 6, 128, 512), np.float32)
    rc = np.zeros((3, 4, 128, 512), np.float32)
    for var in range(3):
        for g, w in enumerate(wins):
            for t in range(512):
                lo, hi = t - w // 2, t + w // 2 - 1
                if var == 0:
                    lo = max(lo, 0)
                if var == 2:
                    hi = min(hi, 511)
                cnt = hi - lo + 1
                rc[var, g, :, t] = 1.0 / cnt
                for jj in range(lo, hi + 1):
                    ja = jj + 128
                    bt[var, g, ja // 128, ja % 128, t] = 1.0
                ja = t + 128
                bt[var, g, ja // 128, ja % 128, t] = 1.0 - cnt
    return bt.reshape(3 * 4 * 6 * 128, 512), rc.reshape(3 * 4 * 128, 512)


def col_layout(v):
    v = np.asarray(v, np.float32)
    return np.ascontiguousarray(v.reshape(-1, 128).T)


def rep128(v):
    v = np.asarray(v, np.float32).reshape(1, -1)
    return np.ascontiguousarray(np.broadcast_to(v, (128, v.shape[1])))


def host_shared(inp, lmax):
    f = lambda a: np.ascontiguousarray(np.asarray(a, np.float32))
    sh = {}
    for j in range(2):
        wuq = f(inp["mla_w_uq"][j]).reshape(384, 8, 192)
        nope = wuq[:, :, :128].reshape(384, 1024)
        ra = wuq[:, :, 128:].reshape(384, 512)
        rb = np.concatenate([wuq[:, :, 160:192], wuq[:, :, 128:160]], axis=2).reshape(384, 512)
        sh["m%d_wq" % j] = np.ascontiguousarray(np.concatenate([nope, ra, rb], axis=1))
        wdkv = f(inp["mla_w_dkv"][j])
        sh["m%d_wdq" % j] = f(inp["mla_w_dq"][j])
        sh["m%d_wdkvc" % j] = np.ascontiguousarray(wdkv[:, :256])
        sh["m%d_wdkvr" % j] = np.ascontiguousarray(
            np.concatenate([wdkv[:, 256:320], wdkv[:, 288:320], wdkv[:, 256:288]], axis=1))
        wukv = f(inp["mla_w_ukv"][j]).reshape(256, 8, 256)
        sh["m%d_wukv" % j] = np.ascontiguousarray(
            np.concatenate([wukv[:, :, :128].reshape(256, 1024), wukv[:, :, 128:].reshape(256, 1024)], axis=1))
        sh["m%d_wo" % j] = f(inp["mla_w_o"][j])
    sh["r_wq"] = f(inp["ret_w_q"][0])
    sh["r_wk"] = f(inp["ret_w_k"][0])
    sh["r_wv"] = f(inp["ret_w_v"][0])
    sh["r_wg"] = f(inp["ret_w_g"][0])
    sh["r_wo"] = f(inp["ret_w_o"][0])
    sh["p_w"] = f(inp["pool_w"][0]).reshape(1024, 256)
    for i in range(4):
        sh["w1_%d" % i] = f(inp["mlp_w1"][i])
        sh["w2_%d" % i] = f(inp["mlp_w2"][i])
    sh["ident"] = np.eye(128, dtype=np.float32)
    sh["ones"] = np.ones((128, 128), np.float32)
    bt, rc = pool_tables()
    sh["pool_bt"] = bt
    sh["pool_rc"] = rc
    sh["nmix_col"] = np.concatenate([col_layout(inp["norm_mix"][i]) for i in range(4)], axis=1)
    sh["nffn_col"] = np.concatenate([col_layout(inp["norm_ffn"][i]) for i in range(4)], axis=1)
    sh["nmix2_rep"] = rep128(inp["norm_mix"][2])
    sh["final_rep"] = rep128(inp["final_norm"])
    sh["pscale_rep"] = rep128(inp["pool_scale"][0])
    sh["qn_col"] = np.concatenate([col_layout(inp["mla_q_norm"][j]) for j in range(2)], axis=1)
    sh["kvn_col"] = np.concatenate([col_layout(inp["mla_kv_norm"][j]) for j in range(2)], axis=1)
    sh["decay_rep"] = rep128(np.concatenate([np.asarray(inp["ret_decay_fwd"][0]), np.asarray(inp["ret_decay_bwd"][0])]))
    ccm, ssm, cosr, sinr = rope_tables(lmax)
    sh["ccm"], sh["ssm"], sh["cosr"], sh["sinr"] = ccm, ssm, cosr, sinr
    ii = np.arange(128, dtype=np.float32)
    df = np.maximum(ii[None, :] - ii[:, None], 0.0)
    db = np.maximum(ii[:, None] - ii[None, :], 0.0)
    sh["dmask"] = np.ascontiguousarray(np.concatenate([df, db], axis=1))
    indf = (ii[None, :] >= ii[:, None]).astype(np.float32) / 16.0
    indb = (ii[:, None] >= ii[None, :]).astype(np.float32) / 16.0
    sh["ind"] = np.ascontiguousarray(np.concatenate([indf, indb], axis=1))
    il = (np.arange(512) % 128).astype(np.float32)
    sh["e12"] = rep128(np.concatenate([il + 1.0, 128.0 - il]))
    sh["pcol"] = np.ascontiguousarray(np.stack([127.0 - ii, ii], axis=1).astype(np.float32))
    return sh


_CACHE = {}


def get_prog(piece_lens, layers, final, lmax):
    key = (tuple(piece_lens), tuple(layers), final, lmax)
    if key not in _CACHE:
        p = Prog(piece_lens, layers, final, lmax)
        p.build()
        _CACHE[key] = p
    return _CACHE[key]


def run_pieces(inp, core_pieces, layers=(0, 1, 2, 3), final=True):
    piece_lens = [a.shape[0] for a in core_pieces[0]]
    lmax = max(piece_lens)
    prog = get_prog(piece_lens, layers, final, lmax)
    sh = host_shared(inp, lmax)
    in_maps = []
    for cp in core_pieces:
        m = dict(sh)
        for pi, a in enumerate(cp):
            m["x%d" % pi] = np.ascontiguousarray(a, dtype=np.float32)
        m = {k: m[k] for k in prog.inputs}
        in_maps.append(m)
    res = run_bass_kernel_spmd(prog.nc, in_maps, core_ids=list(range(len(core_pieces))))
    return [[np.asarray(r["y%d" % pi]) for pi in range(len(piece_lens))] for r in res.results]


def kernel(**inp):
    xp = np.asarray(inp["x_prompt"], np.float32)
    xs = np.asarray(inp["x_sample"], np.float32)
    zero = np.zeros_like(xs[0])
    core_pieces = []
    for c in range(8):
        core_pieces.append([xs[c] if c < 2 else zero, xp[2 * c], xp[2 * c + 1]])
    outs = run_pieces(inp, core_pieces)
    y_prompt = np.stack([outs[c][1 + k] for c in range(8) for k in range(2)], axis=0)
    y_sample = np.stack([outs[0][0], outs[1][0]], axis=0)
    return (y_prompt.astype(np.float32), y_sample.astype(np.float32))
```

```python
import contextlib
import numpy as np
import ml_dtypes
import concourse.bass as bass
import concourse.mybir as mybir
from concourse.bass_utils import run_bass_kernel_spmd

F32 = mybir.dt.float32
BF16 = mybir.dt.bfloat16
AF = mybir.ActivationFunctionType
ALU = mybir.AluOpType

D = 1024
T = 512
EPS = 1e-6
NH = 8
MLA_SCALE = 192 ** -0.5
KCH = 2048


class Sched:
    ENG = ("sp", "act", "dve", "pool", "pe")

    def __init__(self, nc, stack):
        self.nc = nc
        self.eobj = dict(pe=nc.tensor, act=nc.scalar, dve=nc.vector, pool=nc.gpsimd, sp=nc.sync)
        self.lanes = {"sp": ["sp%d" % i for i in range(8)], "pool": ["pl%d" % i for i in range(6)],
                      "act": ["aq%d" % i for i in range(4)]}
        self.sems = {}
        self.count = {}
        for n in list(self.ENG) + sum(self.lanes.values(), []):
            self.sems[n] = stack.enter_context(nc.semaphore("s_" + n))
            self.count[n] = 0
        self.lane_rr = {k: 0 for k in self.lanes}
        self.pending = {e: [] for e in self.ENG}
        self.seen = {e: {} for e in self.ENG}
        self.lw = {}
        self.rd = {}
        self.nops = 0

    def op(self, e, fn, reads=(), writes=(), dma=False):
        deps = {}

        def add(ev):
            s, v = ev
            if deps.get(s, 0) < v:
                deps[s] = v

        for k in reads:
            ev = self.lw.get(k)
            if ev is not None:
                add(ev)
        for k in writes:
            ev = self.lw.get(k)
            if ev is not None:
                add(ev)
            r = self.rd.get(k)
            if r:
                for s, v in r.items():
                    add((s, v))
        if dma:
            lanes = self.lanes[e]
            ln = lanes[self.lane_rr[e] % len(lanes)]
            self.lane_rr[e] += 1
            if self.count[ln] > 0:
                add((ln, self.count[ln]))
            self.count[ln] += 16
            ev = (ln, self.count[ln])
            inc = (ln, 16)
        else:
            self.count[e] += 1
            ev = (e, self.count[e])
            inc = (e, 1)
        waits = []
        seen = self.seen[e]
        for s, v in deps.items():
            if s == e and e == "pe":
                continue
            if seen.get(s, 0) >= v:
                continue
            seen[s] = v
            waits.append((s, v))
        self.pending[e].append((fn, waits, inc))
        for k in reads:
            r = self.rd.setdefault(k, {})
            if r.get(ev[0], 0) < ev[1]:
                r[ev[0]] = ev[1]
        for k in writes:
            self.lw[k] = ev
            self.rd[k] = {}
        self.nops += 1
        return ev

    def barrier(self):
        for e in self.ENG:
            waits = []
            for s, v in self.count.items():
                if v > 0 and self.seen[e].get(s, 0) < v:
                    self.seen[e][s] = v
                    waits.append((s, v))
            if waits:
                self.pending[e].append((None, waits, None))

    def flush(self):
        with self.nc.Block() as blk:
            for e, deco in (("sp", blk.sync), ("act", blk.scalar), ("dve", blk.vector),
                            ("pool", blk.gpsimd), ("pe", blk.tensor)):
                ops = self.pending[e]
                self.pending[e] = []

                def body(eng, ops=ops):
                    for fn, waits, inc in ops:
                        for s, v in waits:
                            eng.wait_ge(self.sems[s], v)
                        if fn is not None:
                            ins = fn(eng)
                            ins.then_inc(self.sems[inc[0]], inc[1])

                deco(body)


class Ring:
    def __init__(self, name, aps):
        self.name = name
        self.aps = aps
        self.i = 0

    def next(self):
        j = self.i % len(self.aps)
        self.i += 1
        return self.aps[j], (self.name, j)


class Prog:
    def __init__(self, piece_lens, layers=(0, 1, 2, 3), final=True, lmax=16384):
        self.piece_lens = list(piece_lens)
        self.layers = list(layers)
        self.final = final
        self.lmax = lmax
        self.inputs = {}
        self.nc = bass.Bass("TRN2", target_bir_lowering=False)
        self.stack = contextlib.ExitStack()
        self.S = None

    def din(self, name, shape, dt=F32):
        self.inputs[name] = (tuple(shape), np.float32 if dt == F32 else ml_dtypes.bfloat16)
        return self.nc.dram_tensor(name, list(shape), dt, kind="ExternalInput").ap()

    def dout(self, name, shape):
        return self.nc.dram_tensor(name, list(shape), F32, kind="ExternalOutput").ap()

    def dscr(self, name, shape, dt=BF16):
        return self.nc.dram_tensor(name, list(shape), dt, kind="Internal").ap()

    def uniq(self, name):
        self._u = getattr(self, "_u", 0) + 1
        return "%s_u%d" % (name, self._u)

    def sb(self, st, name, shape, dt):
        return st.enter_context(self.nc.sbuf_tensor(self.uniq(name), list(shape), dt))

    def ps(self, st, name, shape, dt):
        return st.enter_context(self.nc.psum_tensor(self.uniq(name), list(shape), dt))

    def dma(self, q, out, in_, reads, writes, **kw):
        self.S.op(q, lambda e: e.dma_start(out=out, in_=in_, **kw), reads, writes, dma=True)

    def mm(self, out, lhsT, rhs, start, stop, reads, writes):
        self.S.op("pe", lambda e: e.matmul(out, lhsT=lhsT, rhs=rhs, start=start, stop=stop), reads, writes)

    def tr(self, out, in_, reads, writes):
        ident = self.ident
        self.S.op("pe", lambda e: e.transpose(out, in_, ident), list(reads) + ["ident"], writes)

    def act(self, out, in_, func, reads, writes, bias=None, scale=None, accum=None):
        kw = {}
        if bias is not None:
            kw["bias"] = bias
        if scale is not None:
            kw["scale"] = scale
        if accum is not None:
            kw["accum_out"] = accum
        self.S.op("act", lambda e: e.activation(out=out, in_=in_, func=func, **kw), reads, writes)

    def tt(self, eng, out, a, b, op, reads, writes):
        self.S.op(eng, lambda e: e.tensor_tensor(out=out, in0=a, in1=b, op=op), reads, writes)

    def ts(self, eng, out, a, s1, s2, op0, op1, reads, writes):
        if op1 is None:
            self.S.op(eng, lambda e: e.tensor_scalar(out=out, in0=a, scalar1=s1, scalar2=None, op0=op0),
                      reads, writes)
        else:
            self.S.op(eng, lambda e: e.tensor_scalar(out=out, in0=a, scalar1=s1, scalar2=s2, op0=op0, op1=op1),
                      reads, writes)

    def stt(self, out, a, s, b, op0, op1, reads, writes):
        self.S.op("dve", lambda e: e.scalar_tensor_tensor(out=out, in0=a, scalar=s, in1=b, op0=op0, op1=op1),
                  reads, writes)

    def cp(self, eng, out, in_, reads, writes):
        if eng == "act":
            self.act(out, in_, AF.Copy, reads, writes)
        else:
            self.S.op(eng, lambda e: e.tensor_copy(out=out, in_=in_), reads, writes)

    def declare(self):
        nc = self.nc
        LM = self.lmax
        self.xin = []
        self.yout = []
        self.X = []
        for pi, L in enumerate(self.piece_lens):
            self.xin.append(self.din("x%d" % pi, [L, D]))
            self.yout.append(self.dout("y%d" % pi, [L, D]))
            self.X.append(self.dscr("X%d" % pi, [L, D], F32))
        self.wspec = {}
        for j in range(2):
            self.wspec["m%d_wdq" % j] = (1024, 384)
            self.wspec["m%d_wdkvc" % j] = (1024, 256)
            self.wspec["m%d_wdkvr" % j] = (1024, 128)
            self.wspec["m%d_wq" % j] = (384, 2048)
            self.wspec["m%d_wukv" % j] = (256, 2048)
            self.wspec["m%d_wo" % j] = (1024, 1024)
        self.wspec["r_wq"] = (1024, 1024)
        self.wspec["r_wk"] = (1024, 1024)
        self.wspec["r_wv"] = (1024, 2048)
        self.wspec["r_wg"] = (1024, 2048)
        self.wspec["r_wo"] = (2048, 1024)
        self.wspec["p_w"] = (1024, 256)
        for i in range(4):
            self.wspec["w1_%d" % i] = (1024, 4096)
            self.wspec["w2_%d" % i] = (4096, 1024)
        self.wspec["ident"] = (128, 128)
        self.wspec["ones"] = (128, 128)
        self.wspec["pool_bt"] = (3 * 4 * 6 * 128, 512)
        self.wf = {}
        self.wb = {}
        for n, shp in self.wspec.items():
            self.wf[n] = self.din(n, shp)
            self.wb[n] = self.dscr("b_" + n, shp, BF16)
        self.c = {}
        for n, shp in [("nmix_col", (128, 32)), ("nffn_col", (128, 32)), ("nmix2_rep", (128, 1024)),
                       ("final_rep", (128, 1024)), ("pscale_rep", (128, 1024)),
                       ("qn_col", (128, 6)), ("kvn_col", (128, 4)), ("decay_rep", (128, 8)),
                       ("ccm", (128, LM)), ("ssm", (128, LM)), ("cosr", (128, LM)), ("sinr", (128, LM)),
                       ("dmask", (128, 256)), ("ind", (128, 256)), ("e12", (128, 1024)), ("pcol", (128, 2)),
                       ("pool_rc", (3 * 4 * 128, 512))]:
            self.c[n] = self.din(n, shp)

    def build(self):
        self.declare()
        with self.stack:
            self.S = Sched(self.nc, self.stack)
            self.identt = self.sb(self.stack, "identt", [128, 128], BF16)
            self.ident = self.identt[:]
            self.onest = self.sb(self.stack, "onest", [128, 128], BF16)
            self.prologue()
            for li in self.layers:
                kind, j = li % 3, li // 3
                first = (li == self.layers[0])
                for pi in range(len(self.piece_lens)):
                    src = self.xin[pi] if first else self.X[pi]
                    sname = ("xin%d" % pi) if first else ("X%d" % pi)
                    if kind == 0:
                        self.mla_layer(li, j, pi, src, sname)
                    elif kind == 1:
                        self.ret_layer(li, pi, src, sname)
                    else:
                        self.pool_layer(li, pi, src, sname)
                for pi in range(len(self.piece_lens)):
                    self.mlp_layer(li, pi)
            if self.final:
                for pi in range(len(self.piece_lens)):
                    self.final_norm(pi)
            else:
                for pi in range(len(self.piece_lens)):
                    for t in range(self.piece_lens[pi] // T):
                        self.dma("sp", self.yout[pi][t * T:(t + 1) * T, :], self.X[pi][t * T:(t + 1) * T, :],
                                 reads=[("X%d" % pi, pi, t)], writes=[("yout", pi, t)])
            self.S.barrier()
            self.S.flush()
        return self.nc

    def prologue(self):
        for n, shp in self.wspec.items():
            K, N = shp
            rows = 512 if N * 512 * 4 <= (8 << 20) else 128
            rows = min(rows, K)
            for r0 in range(0, K, rows):
                self.dma("pool", self.wb[n][r0:r0 + rows, :], self.wf[n][r0:r0 + rows, :],
                         reads=[], writes=[("wb", n)], max_dma_last_dim=4096)
        self.dma("sp", self.identt[:], self.wb["ident"][:, :], reads=[("wb", "ident")], writes=["ident"])
        self.dma("sp", self.onest[:], self.wb["ones"][:, :], reads=[("wb", "ones")], writes=["ones"])

    def phase_begin(self):
        self.S.barrier()
        return contextlib.ExitStack()

    def phase_end(self):
        self.S.barrier()
        self.S.flush()

    def load_w(self, st, name, wname, kchunks, ncols, col0=0, q="sp"):
        t = self.sb(st, name, [128, kchunks, ncols], BF16)
        src = self.wb[wname][0:kchunks * 128, col0:col0 + ncols].rearrange("(c p) n -> p c n", p=128)
        self.dma(q, t[:], src, reads=[("wb", wname)], writes=[name])
        return t

    def load_c(self, st, name, cname, shape, src_ap, q="sp"):
        t = self.sb(st, name, list(shape), F32)
        self.dma(q, t[:], src_ap, reads=[], writes=[name])
        return t

    def front_alloc(self, st, nbuf=2):
        R = {}
        R["xin"] = Ring("f_xin", [self.sb(st, "f_xin%d" % i, [128, 4, D], F32) for i in range(nbuf)])
        R["xn"] = Ring("f_xn", [self.sb(st, "f_xn%d" % i, [128, 4, D], BF16) for i in range(nbuf)])
        R["ss"] = Ring("f_ss", [self.sb(st, "f_ss%d" % i, [128, 4], F32) for i in range(nbuf)])
        R["rs"] = Ring("f_rs", [self.sb(st, "f_rs%d" % i, [128, 4], F32) for i in range(nbuf)])
        R["junk"] = self.sb(st, "f_junk", [128, D], BF16)
        pts = [self.ps(st, "f_pt%d" % i, [128, 1024], BF16) for i in range(2)]
        R["pt"] = Ring("f_pt", [pts[0], pts[1]])
        R["flip"] = 0
        return R

    def front_stats(self, R, src, sname, pi, t, want_xn=True):
        xin, kx = R["xin"].next()
        self.dma("sp", xin[:], src[t * T:(t + 1) * T, :].rearrange("(b p) d -> p b d", p=128),
                 reads=[(sname, pi, t)], writes=[kx])
        ss, kss = R["ss"].next()
        rs, krs = R["rs"].next()
        junk = R["junk"]
        for b in range(4):
            self.act(junk[:], xin[:, b, :], AF.Square, reads=[kx], writes=[(kss, b)], accum=ss[:, b:b + 1])
        self.act(rs[:], ss[:], AF.Ln, reads=[(kss, b) for b in range(4)], writes=[krs], scale=1.0 / D, bias=EPS)
        self.act(rs[:], rs[:], AF.Exp, reads=[krs], writes=[krs], scale=-0.5)
        xn = kxn = None
        if want_xn:
            xn, kxn = R["xn"].next()
            for b in range(4):
                self.ts("dve", xn[:, b, :], xin[:, b, :], rs[:, b:b + 1], None, ALU.mult, None,
                        reads=[kx, krs], writes=[(kxn, b)])
        return xin, kx, rs, krs, xn, kxn

    def front(self, R, src, sname, pi, t, gcol, gkey, hT, hkey):
        xin, kx, rs, krs, xn, kxn = self.front_stats(R, src, sname, pi, t)
        import os
        FD = int(os.environ.get("FDBG", "9"))
        for cpair in range(4):
            if FD < 2:
                break
            pt, kpt = R["pt"].next()
            for cc in range(2):
                c = cpair * 2 + cc
                for b in range(4):
                    self.tr(pt[:, cc * 512 + b * 128:cc * 512 + (b + 1) * 128], xn[:, b, c * 128:(c + 1) * 128],
                            reads=[(kxn, b)], writes=[kpt])
            for cc in range(2):
                if FD < 3:
                    break
                c = cpair * 2 + cc
                if True:
                    self.act(hT[:, c, :], pt[:, cc * 512:(cc + 1) * 512], AF.Copy, reads=[kpt, gkey],
                             writes=[(hkey, c)], scale=gcol[:, c:c + 1])
                else:
                    self.ts("dve", hT[:, c, :], pt[:, cc * 512:(cc + 1) * 512], gcol[:, c:c + 1], None, ALU.mult, None,
                            reads=[kpt, gkey], writes=[(hkey, c)])
        return xin, kx

    def scratch(self, name, shape, dt=BF16):
        if not hasattr(self, "_scr"):
            self._scr = {}
        if name not in self._scr:
            self._scr[name] = self.dscr(name, shape, dt)
        return self._scr[name]

    def mla_layer(self, li, j, pi, src, sname):
        L = self.piece_lens[pi]
        nt = L // T
        nblk = L // 128
        QTn = self.scratch("a_qtn%d" % pi, [NH, 128, L])
        QTr = self.scratch("a_qtr%d" % pi, [NH * 64, L])
        KTn = self.scratch("a_ktn%d" % pi, [NH, 128, L])
        KPE = self.scratch("a_kpe%d" % pi, [64, L])
        VP = self.scratch("a_vp%d" % pi, [NH, 128, nblk, 128])
        OT = self.scratch("a_ot%d" % pi, [NH, 128, L])
        pre = "m%d_" % j

        st = self.phase_begin()
        with st:
            R = self.front_alloc(st)
            wdq = self.load_w(st, "a_wdq", pre + "wdq", 8, 384)
            wdkvc = self.load_w(st, "a_wdkvc", pre + "wdkvc", 8, 256)
            wdkvr = self.load_w(st, "a_wdkvr", pre + "wdkvr", 8, 128)
            wq = self.load_w(st, "a_wq", pre + "wq", 3, 2048)
            wukv = self.load_w(st, "a_wukv", pre + "wukv", 2, 2048)
            gcol = self.load_c(st, "a_gcol", "nmix_col", [128, 8], self.c["nmix_col"][:, li * 8:(li + 1) * 8])
            qncol = self.load_c(st, "a_qn", "qn_col", [128, 3], self.c["qn_col"][:, j * 3:(j + 1) * 3])
            kvncol = self.load_c(st, "a_kvn", "kvn_col", [128, 2], self.c["kvn_col"][:, j * 2:(j + 1) * 2])
            hT = Ring("a_hT", [self.sb(st, "a_hT%d" % i, [128, 8, T], BF16) for i in range(2)])
            cqf = self.sb(st, "a_cqf", [128, 4, 640], F32)
            cqn = self.sb(st, "a_cqn", [128, 4, 640], BF16)
            ss2 = self.sb(st, "a_ss2", [128, 8], F32)
            rs2 = self.sb(st, "a_rs2", [128, 8], F32)
            cT = Ring("a_cT", [self.sb(st, "a_cT%d" % i, [128, 5, T], BF16) for i in range(2)])
            cc = Ring("a_cc", [self.sb(st, "a_cc%d" % i, [128, T], F32) for i in range(2)])
            sn = Ring("a_sn", [self.sb(st, "a_sn%d" % i, [128, T], F32) for i in range(2)])
            ob = Ring("a_ob", [self.sb(st, "a_ob%d" % i, [128, T], BF16) for i in range(6)])
            t1 = Ring("a_t1", [self.sb(st, "a_t1%d" % i, [128, T], F32) for i in range(2)])
            t2 = Ring("a_t2", [self.sb(st, "a_t2%d" % i, [128, T], F32) for i in range(2)])
            vt = Ring("a_vt", [self.sb(st, "a_vt%d" % i, [128, 4, 1024], BF16) for i in range(2)])
            pp = Ring("a_pp", [self.ps(st, "a_pp%d" % i, [128, 512], F32) for i in range(6)])
            junk = R["junk"]
            flip = 0
            for t in range(nt):
                h, hk = hT.next()
                self.front(R, src, sname, pi, t, gcol, "a_gcol", h, hk)
                hreads = [(hk, c) for c in range(8)]
                for b in range(4):
                    pa, kpa = pp.next()
                    for k in range(8):
                        self.mm(pa[:, 0:384], h[:, k, b * 128:(b + 1) * 128], wdq[:, k, :], k == 0, k == 7,
                                reads=hreads + ["a_wdq"], writes=[kpa])
                    self.cp("dve", cqf[:, b, 0:384], pa[:, 0:384], reads=[kpa], writes=[("a_cqf", b, 0)])
                    pb, kpb = pp.next()
                    for k in range(8):
                        self.mm(pb[:, 0:256], h[:, k, b * 128:(b + 1) * 128], wdkvc[:, k, :], k == 0, k == 7,
                                reads=hreads + ["a_wdkvc"], writes=[kpb])
                    self.cp("dve", cqf[:, b, 384:640], pb[:, 0:256], reads=[kpb], writes=[("a_cqf", b, 1)])
                for b in range(4):
                    self.act(junk[:, 0:384], cqf[:, b, 0:384], AF.Square, reads=[("a_cqf", b, 0)],
                             writes=[("a_ss2", b)], accum=ss2[:, b:b + 1])
                    self.act(junk[:, 0:256], cqf[:, b, 384:640], AF.Square, reads=[("a_cqf", b, 1)],
                             writes=[("a_ss2", 4 + b)], accum=ss2[:, 4 + b:5 + b])
                self.act(rs2[:, 0:4], ss2[:, 0:4], AF.Ln, reads=[("a_ss2", b) for b in range(4)],
                         writes=[("a_rs2", 0)], scale=1.0 / 384, bias=EPS)
                self.act(rs2[:, 4:8], ss2[:, 4:8], AF.Ln, reads=[("a_ss2", 4 + b) for b in range(4)],
                         writes=[("a_rs2", 1)], scale=1.0 / 256, bias=EPS)
                self.act(rs2[:], rs2[:], AF.Exp, reads=[("a_rs2", 0), ("a_rs2", 1)], writes=[("a_rs2", 0), ("a_rs2", 1)],
                         scale=-0.5)
                for b in range(4):
                    self.ts("dve", cqn[:, b, 0:384], cqf[:, b, 0:384], rs2[:, b:b + 1], None, ALU.mult, None,
                            reads=[("a_cqf", b, 0), ("a_rs2", 0)], writes=[("a_cqn", b, 0)])
                    self.ts("dve", cqn[:, b, 384:640], cqf[:, b, 384:640], rs2[:, 4 + b:5 + b], None, ALU.mult, None,
                            reads=[("a_cqf", b, 1), ("a_rs2", 1)], writes=[("a_cqn", b, 1)])
                ct, kct = cT.next()
                for cpair in range(3):
                    pt, kpt = R["pt"].next()
                    cs = [c for c in (2 * cpair, 2 * cpair + 1) if c < 5]
                    for c in cs:
                        for b in range(4):
                            self.tr(pt[:, (c % 2) * 512 + b * 128:(c % 2) * 512 + (b + 1) * 128],
                                    cqn[:, b, c * 128:(c + 1) * 128],
                                    reads=[("a_cqn", b, 0 if c < 3 else 1)], writes=[kpt])
                    for c in cs:
                        gc = qncol[:, c:c + 1] if c < 3 else kvncol[:, c - 3:c - 2]
                        gk = "a_qn" if c < 3 else "a_kvn"
                        ptc = pt[:, (c % 2) * 512:(c % 2 + 1) * 512]
                        if True:
                            self.act(ct[:, c, :], ptc, AF.Copy, reads=[kpt, gk], writes=[(kct, c)], scale=gc)
                        else:
                            self.ts("dve", ct[:, c, :], ptc, gc, None, ALU.mult, None, reads=[kpt, gk], writes=[(kct, c)])
                creads_q = [(kct, c) for c in range(3)]
                creads_kv = [(kct, 3), (kct, 4)]
                cct, kcc = cc.next()
                snt, ksn = sn.next()
                self.dma("sp", cct[:], self.c["ccm"][:, t * T:(t + 1) * T], reads=[], writes=[kcc])
                self.dma("sp", snt[:], self.c["ssm"][:, t * T:(t + 1) * T], reads=[], writes=[ksn])
                for hh in range(NH):
                    pq, kpq = pp.next()
                    for k in range(3):
                        self.mm(pq[:], wq[:, k, hh * 128:(hh + 1) * 128], ct[:, k, :], k == 0, k == 2,
                                reads=creads_q + ["a_wq"], writes=[kpq])
                    o, ko = ob.next()
                    eng = "act" if flip % 2 == 0 else "dve"
                    flip += 1
                    self.cp(eng, o[:], pq[:], reads=[kpq], writes=[ko])
                    self.dma("pool", QTn[hh, :, t * T:(t + 1) * T], o[:], reads=[ko], writes=[("a_qtn", pi, hh, t)])
                for hp in range(4):
                    pA, kpA = pp.next()
                    for k in range(3):
                        self.mm(pA[:], wq[:, k, 1024 + hp * 128:1024 + (hp + 1) * 128], ct[:, k, :], k == 0, k == 2,
                                reads=creads_q + ["a_wq"], writes=[kpA])
                    pB, kpB = pp.next()
                    for k in range(3):
                        self.mm(pB[:], wq[:, k, 1536 + hp * 128:1536 + (hp + 1) * 128], ct[:, k, :], k == 0, k == 2,
                                reads=creads_q + ["a_wq"], writes=[kpB])
                    a1, k1 = t1.next()
                    a2, k2 = t2.next()
                    self.tt("dve", a1[:], pA[:], cct[:], ALU.mult, reads=[kpA, kcc], writes=[k1])
                    self.tt("dve", a2[:], pB[:], snt[:], ALU.mult, reads=[kpB, ksn], writes=[k2])
                    o, ko = ob.next()
                    self.tt("pool", o[:], a1[:], a2[:], ALU.add, reads=[k1, k2], writes=[ko])
                    self.dma("pool", QTr[hp * 128:(hp + 1) * 128, t * T:(t + 1) * T], o[:], reads=[ko],
                             writes=[("a_qtr", pi, hp, t)])
                pA, kpA = pp.next()
                for k in range(8):
                    self.mm(pA[0:64, :], wdkvr[:, k, 0:64], h[:, k, :], k == 0, k == 7,
                            reads=hreads + ["a_wdkvr"], writes=[kpA])
                pB, kpB = pp.next()
                for k in range(8):
                    self.mm(pB[0:64, :], wdkvr[:, k, 64:128], h[:, k, :], k == 0, k == 7,
                            reads=hreads + ["a_wdkvr"], writes=[kpB])
                a1, k1 = t1.next()
                a2, k2 = t2.next()
                self.tt("dve", a1[0:64, :], pA[0:64, :], cct[0:64, :], ALU.mult, reads=[kpA, kcc], writes=[k1])
                self.tt("dve", a2[0:64, :], pB[0:64, :], snt[0:64, :], ALU.mult, reads=[kpB, ksn], writes=[k2])
                o, ko = ob.next()
                self.tt("pool", o[0:64, :], a1[0:64, :], a2[0:64, :], ALU.add, reads=[k1, k2], writes=[ko])
                self.dma("pool", KPE[:, t * T:(t + 1) * T], o[0:64, :], reads=[ko], writes=[("a_kpe", pi, t)])
                for hh in range(NH):
                    pq, kpq = pp.next()
                    for k in range(2):
                        self.mm(pq[:], wukv[:, k, hh * 128:(hh + 1) * 128], ct[:, 3 + k, :], k == 0, k == 1,
                                reads=creads_kv + ["a_wukv"], writes=[kpq])
                    o, ko = ob.next()
                    eng = "act" if flip % 2 == 0 else "dve"
                    flip += 1
                    self.cp(eng, o[:], pq[:], reads=[kpq], writes=[ko])
                    self.dma("pool", KTn[hh, :, t * T:(t + 1) * T], o[:], reads=[ko], writes=[("a_ktn", pi, hh, t)])
                v, kv = vt.next()
                for b in range(4):
                    for half in range(2):
                        pq, kpq = pp.next()
                        for k in range(2):
                            self.mm(pq[:], ct[:, 3 + k, b * 128:(b + 1) * 128],
                                    wukv[:, k, 1024 + half * 512:1024 + (half + 1) * 512], k == 0, k == 1,
                                    reads=creads_kv + ["a_wukv"], writes=[kpq])
                        eng = "act" if flip % 2 == 0 else "dve"
                        flip += 1
                        self.cp(eng, v[:, b, half * 512:(half + 1) * 512], pq[:], reads=[kpq], writes=[(kv, b, half)])
                for hh in range(NH):
                    self.dma("pool", VP[hh, :, t * 4:(t + 1) * 4, :], v[:, :, hh * 128:(hh + 1) * 128],
                             reads=[(kv, b, hh // 4) for b in range(4)], writes=[("a_vp", pi, hh, t)])
            self.phase_end()

        st = self.phase_begin()
        with st:
            nkc = max(1, L // KCH)
            kch = min(KCH, L)
            bpc = kch // 128
            qn = Ring("b_qn", [self.sb(st, "b_qn%d" % i, [128, T], BF16) for i in range(2)])
            qr = Ring("b_qr", [self.sb(st, "b_qr%d" % i, [128, T], BF16) for i in range(2)])
            kn = Ring("b_kn", [self.sb(st, "b_kn%d" % i, [128, kch], BF16) for i in range(3)])
            kr = Ring("b_kr", [self.sb(st, "b_kr%d" % i, [128, kch], BF16) for i in range(3)])
            for i_, ap_ in enumerate(qr.aps):
                self.S.op("dve", lambda e, a=ap_: e.memset(a[:], 0.0), [], [("b_qr", i_)])
            for i_, ap_ in enumerate(kr.aps):
                self.S.op("dve", lambda e, a=ap_: e.memset(a[:], 0.0), [], [("b_kr", i_)])
            vv = Ring("b_vv", [self.sb(st, "b_vv%d" % i, [128, bpc, 128], BF16) for i in range(3)])
            pT = Ring("b_pT", [self.sb(st, "b_pT%d" % i, [128, T], BF16) for i in range(6)])
            ps2 = Ring("b_ps2", [self.sb(st, "b_ps2%d" % i, [128, T], BF16) for i in range(8)])
            rsum = Ring("b_rs", [self.sb(st, "b_rs%d" % i, [128, T], F32) for i in range(2)])
            oo = Ring("b_oo", [self.sb(st, "b_oo%d" % i, [128, T], BF16) for i in range(2)])
            pss = Ring("b_pss", [self.ps(st, "b_pss%d" % i, [128, 512], F32) for i in range(4)])
            pso = Ring("b_pso", [self.ps(st, "b_pso%d" % i, [128, 512], F32) for i in range(2)])
            psm = Ring("b_psm", [self.ps(st, "b_psm%d" % i, [128, 512], F32) for i in range(2)])
            ones = self.onest
            for hh in range(NH):
                for qt in range(nt):
                    q1, kq1 = qn.next()
                    q2, kq2 = qr.next()
                    self.dma("sp", q1[:], QTn[hh, :, qt * T:(qt + 1) * T], reads=[("a_qtn", pi, hh, qt)], writes=[kq1])
                    self.dma("sp", q2[0:64, :], QTr[hh * 64:(hh + 1) * 64, qt * T:(qt + 1) * T],
                             reads=[("a_qtr", pi, hh // 2, qt)], writes=[kq2])
                    po, kpo = pso.next()
                    pm, kpm = psm.next()
                    nblk_tot = nkc * bpc
                    chunks = {}

                    def get_chunk(kc, hh=hh):
                        if kc not in chunks:
                            k1, kk1 = kn.next()
                            k2, kk2 = kr.next()
                            v1, kv1 = vv.next()
                            tl = [kc * (kch // T) + x for x in range(max(1, kch // T))]
                            self.dma("sp", k1[:], KTn[hh, :, kc * kch:(kc + 1) * kch],
                                     reads=[("a_ktn", pi, hh, x) for x in tl], writes=[kk1])
                            self.dma("sp", k2[0:64, :], KPE[:, kc * kch:(kc + 1) * kch],
                                     reads=[("a_kpe", pi, x) for x in tl], writes=[kk2])
                            self.dma("sp", v1[:], VP[hh, :, kc * bpc:(kc + 1) * bpc, :],
                                     reads=[("a_vp", pi, hh, x) for x in tl], writes=[kv1])
                            chunks[kc] = (k1, kk1, k2, kk2, v1, kv1)
                        return chunks[kc]

                    def emit_qk(i):
                        kc, b = divmod(i, bpc)
                        k1, kk1, k2, kk2, v1, kv1 = get_chunk(kc)
                        s, ks = pss.next()
                        self.mm(s[:], k1[:, b * 128:(b + 1) * 128], q1[:], True, False, reads=[kk1, kq1], writes=[ks])
                        self.mm(s[:], k2[:, b * 128:(b + 1) * 128], q2[:, :], False, True,
                                reads=[kk2, kq2], writes=[ks])
                        return s, ks, v1[:, b, :], kv1

                    LOOK = 3
                    pend_sum = None
                    stackp = []
                    nsum = 0
                    SG = 8
                    Sq = {}
                    for i in range(min(LOOK, nblk_tot)):
                        Sq[i] = emit_qk(i)
                    for jb in range(nblk_tot):
                        if jb + LOOK < nblk_tot:
                            Sq[jb + LOOK] = emit_qk(jb + LOOK)
                        s, ks, vap, kv1 = Sq.pop(jb)
                        p, kp = pT.next()
                        self.act(p[:], s[:], AF.Exp, reads=[ks], writes=[kp], scale=MLA_SCALE)
                        self.mm(po[:], vap, p[:], jb == 0, jb == nblk_tot - 1, reads=[kv1, kp], writes=[kpo])
                        if pend_sum is not None:
                            s2_, ks2_, first_ = pend_sum
                            self.mm(pm[:], ones[:], s2_[:], first_, False, reads=["ones", ks2_], writes=[kpm])
                            pend_sum = None
                        stackp.append((p, kp, 1))
                        while len(stackp) >= 2 and stackp[-1][2] == stackp[-2][2]:
                            b_, kb_, lv = stackp.pop()
                            a_, ka_, _ = stackp.pop()
                            s2, ks2 = ps2.next()
                            self.tt("dve", s2[:], a_[:], b_[:], ALU.add, reads=[ka_, kb_], writes=[ks2])
                            stackp.append((s2, ks2, lv * 2))
                        if stackp[-1][2] == SG:
                            s2, ks2, _ = stackp.pop()
                            pend_sum = (s2, ks2, nsum == 0)
                            nsum += 1
                    assert not stackp
                    s2_, ks2_, first_ = pend_sum
                    self.mm(pm[:], ones[:], s2_[:], first_, True, reads=["ones", ks2_], writes=[kpm])
                    r, kr_ = rsum.next()
                    self.act(r[:], pm[:], AF.Ln, reads=[kpm], writes=[kr_])
                    self.act(r[:], r[:], AF.Exp, reads=[kr_], writes=[kr_], scale=-1.0)
                    o, ko = oo.next()
                    self.tt("dve", o[:], po[:], r[:], ALU.mult, reads=[kpo, kr_], writes=[ko])
                    self.dma("pool", OT[hh, :, qt * T:(qt + 1) * T], o[:], reads=[ko], writes=[("a_ot", pi, hh, qt)])
            self.phase_end()

        st = self.phase_begin()
        with st:
            wo = self.load_w(st, "c_wo", pre + "wo", 8, 1024)
            ot = Ring("c_ot", [self.sb(st, "c_ot%d" % i, [128, 8, T], BF16) for i in range(2)])
            xr = Ring("c_xr", [self.sb(st, "c_xr%d" % i, [128, 4, D], F32) for i in range(2)])
            xo = Ring("c_xo", [self.sb(st, "c_xo%d" % i, [128, 4, D], F32) for i in range(2)])
            pp = Ring("c_pp", [self.ps(st, "c_pp%d" % i, [128, 512], F32) for i in range(4)])
            for t in range(nt):
                o, ko = ot.next()
                for hh in range(NH):
                    self.dma("sp", o[:, hh, :], OT[hh, :, t * T:(t + 1) * T], reads=[("a_ot", pi, hh, t)],
                             writes=[(ko, hh)])
                x, kx = xr.next()
                self.dma("sp", x[:], src[t * T:(t + 1) * T, :].rearrange("(b p) d -> p b d", p=128),
                         reads=[(sname, pi, t)], writes=[kx])
                y, ky = xo.next()
                for b in range(4):
                    for half in range(2):
                        pq, kpq = pp.next()
                        for hh in range(NH):
                            self.mm(pq[:], o[:, hh, b * 128:(b + 1) * 128], wo[:, hh, half * 512:(half + 1) * 512],
                                    hh == 0, hh == NH - 1, reads=[(ko, hh), "c_wo"], writes=[kpq])
                        self.tt("dve", y[:, b, half * 512:(half + 1) * 512], pq[:], x[:, b, half * 512:(half + 1) * 512],
                                ALU.add, reads=[kpq, kx], writes=[(ky, b, half)])
                self.dma("pool", self.X[pi][t * T:(t + 1) * T, :].rearrange("(b p) d -> p b d", p=128), y[:],
                         reads=[(ky, b, hf) for b in range(4) for hf in range(2)], writes=[("X%d" % pi, pi, t)])
            self.phase_end()

    def _pv(self, pend, po, kpo, pm, kpm, ones):
        p, kp, v, kv, first, last = pend
        self.mm(po[:], v, p[:], first, last, reads=[kv, kp], writes=[kpo])
        self.mm(pm[:], ones[:], p[:], first, last, reads=["ones", kp], writes=[kpm])

    def mlp_layer(self, li, pi):
        L = self.piece_lens[pi]
        TS = 1024 if L % 1024 == 0 else 512
        nsup = L // TS
        nsub = TS // T
        nb = TS // 128
        Xp = self.X[pi]
        sname = "X%d" % pi
        st = self.phase_begin()
        with st:
            R = self.front_alloc(st)
            gcol = self.load_c(st, "m_gcol", "nffn_col", [128, 8], self.c["nffn_col"][:, li * 8:(li + 1) * 8])
            hT = self.sb(st, "m_hT", [128, 8, TS], BF16)
            w1 = Ring("m_w1", [self.sb(st, "m_w1%d" % i, [128, 8, 512], BF16) for i in range(3)])
            w2 = Ring("m_w2", [self.sb(st, "m_w2%d" % i, [128, 4, 1024], BF16) for i in range(3)])
            aT = Ring("m_aT", [self.sb(st, "m_aT%d" % i, [128, 4, TS], BF16) for i in range(3)])
            rl = Ring("m_rl", [self.sb(st, "m_rl%d" % i, [128, T], BF16) for i in range(3)])
            yacc = self.sb(st, "m_yacc", [128, nb, D], F32)
            xr = Ring("m_xr", [self.sb(st, "m_xr%d" % i, [128, D], F32) for i in range(2)])
            pa = Ring("m_pa", [self.ps(st, "m_pa%d" % i, [128, 512], F32) for i in range(3)])
            py = Ring("m_py", [self.ps(st, "m_py%d" % i, [128, 512], F32) for i in range(3)])
            w1n, w2n = "w1_%d" % li, "w2_%d" % li
            for s in range(nsup):
                for u in range(nsub):
                    t = s * nsub + u
                    self.front(R, Xp, sname, pi, t, gcol, "m_gcol", hT[:, :, u * T:(u + 1) * T], ("m_hT", u))
                import os
                DBG = int(os.environ.get("KDBG", "9"))
                def emit_w2(g, a2, ka2, at, kat):
                    for b in range(nb):
                        if g == 0:
                            x, kx = xr.next()
                            self.dma("sp", x[:], Xp[s * TS + b * 128:s * TS + (b + 1) * 128, :],
                                     reads=[(sname, pi, (s * TS + b * 128) // T)], writes=[kx])
                        for half in range(2):
                            p, kp = py.next()
                            for jc in range(4):
                                self.mm(p[:], at[:, jc, b * 128:(b + 1) * 128], a2[:, jc, half * 512:(half + 1) * 512],
                                        jc == 0, jc == 3, reads=[ka2, (kat, jc, b // 4)], writes=[kp])
                            ysl = yacc[:, b, half * 512:(half + 1) * 512]
                            if g == 0:
                                self.tt("dve", ysl, p[:], x[:, half * 512:(half + 1) * 512], ALU.add,
                                        reads=[kp, kx], writes=[("m_yacc", b, half)])
                            else:
                                self.tt("dve", ysl, p[:], ysl, ALU.add, reads=[kp, ("m_yacc", b, half)],
                                        writes=[("m_yacc", b, half)])

                prev = None
                for g in range(8):
                    a1, ka1 = w1.next()
                    a2, ka2 = w2.next()
                    self.dma("sp", a1[:], self.wb[w1n][:, g * 512:(g + 1) * 512].rearrange("(c p) n -> p c n", p=128),
                             reads=[("wb", w1n)], writes=[ka1])
                    self.dma("sp", a2[:], self.wb[w2n][g * 512:(g + 1) * 512, :].rearrange("(c p) n -> p c n", p=128),
                             reads=[("wb", w2n)], writes=[ka2])
                    at, kat = aT.next()
                    for jc in range(4):
                        for u in range(nsub):
                            p, kp = pa.next()
                            for k in range(8):
                                self.mm(p[:], a1[:, k, jc * 128:(jc + 1) * 128], hT[:, k, u * T:(u + 1) * T],
                                        k == 0, k == 7, reads=[ka1] + [(("m_hT", u), c) for c in range(8)], writes=[kp])
                            r, kr = rl.next()
                            self.act(r[:], p[:], AF.Relu, reads=[kp], writes=[kr])
                            self.tt("pool", at[:, jc, u * T:(u + 1) * T], r[:], r[:], ALU.mult, reads=[kr],
                                    writes=[(kat, jc, u)])
                    if prev is not None:
                        emit_w2(*prev)
                    prev = (g, a2, ka2, at, kat)
                emit_w2(*prev)
                for u in range(nsub):
                    t = s * nsub + u
                    self.dma("pool", Xp[t * T:(t + 1) * T, :].rearrange("(b p) d -> p b d", p=128),
                             yacc[:, u * 4:(u + 1) * 4, :],
                             reads=[("m_yacc", b, hf) for b in range(u * 4, u * 4 + 4) for hf in range(2)],
                             writes=[(sname, pi, t)])
            self.phase_end()

    def final_norm(self, pi):
        L = self.piece_lens[pi]
        nt = L // T
        st = self.phase_begin()
        with st:
            R = self.front_alloc(st)
            grep = self.load_c(st, "fn_g", "final_rep", [128, D], self.c["final_rep"][:, :])
            yo = Ring("fn_y", [self.sb(st, "fn_y%d" % i, [128, 4, D], F32) for i in range(2)])
            for t in range(nt):
                xin, kx, rs, krs, _, _ = self.front_stats(R, self.X[pi], "X%d" % pi, pi, t, want_xn=False)
                y, ky = yo.next()
                for b in range(4):
                    self.stt(y[:, b, :], xin[:, b, :], rs[:, b:b + 1], grep[:], ALU.mult, ALU.mult,
                             reads=[kx, krs, "fn_g"], writes=[(ky, b)])
                self.dma("pool", self.yout[pi][t * T:(t + 1) * T, :].rearrange("(b p) d -> p b d", p=128), y[:],
                         reads=[(ky, b) for b in range(4)], writes=[("yout", pi, t)])
            self.phase_end()

    def pool_layer(self, li, pi, src, sname):
        L = self.piece_lens[pi]
        nt = L // T
        nblk = L // 128
        H = self.scratch("p_h%d" % pi, [L, D])
        st = self.phase_begin()
        with st:
            R = self.front_alloc(st)
            grep = self.load_c(st, "pa_g", "nmix2_rep", [128, D], self.c["nmix2_rep"][:, :])
            ho = Ring("pa_h", [self.sb(st, "pa_h%d" % i, [128, 4, D], BF16) for i in range(2)])
            for t in range(nt):
                xin, kx, rs, krs, _, _ = self.front_stats(R, src, sname, pi, t, want_xn=False)
                y, ky = ho.next()
                for b in range(4):
                    self.stt(y[:, b, :], xin[:, b, :], rs[:, b:b + 1], grep[:], ALU.mult, ALU.mult,
                             reads=[kx, krs, "pa_g"], writes=[(ky, b)])
                self.dma("pool", H[t * T:(t + 1) * T, :].rearrange("(b p) d -> p b d", p=128), y[:],
                         reads=[(ky, b) for b in range(4)], writes=[("p_h", pi, t)])
            self.phase_end()
        st = self.phase_begin()
        with st:
            wp = self.load_w(st, "pb_w", "p_w", 8, 256)
            srep = self.load_c(st, "pb_s", "pscale_rep", [128, D], self.c["pscale_rep"][:, :])
            hb = Ring("pb_hb", [self.sb(st, "pb_hb%d" % i, [128, 6, D], BF16) for i in range(2)])
            bt = Ring("pb_bt", [self.sb(st, "pb_bt%d" % i, [128, 24, 512], BF16) for i in range(2)])
            rc = Ring("pb_rc", [self.sb(st, "pb_rc%d" % i, [128, 4, 512], F32) for i in range(2)])
            dT = Ring("pb_dT", [self.sb(st, "pb_dT%d" % i, [128, 8, T], BF16) for i in range(2)])
            xr = Ring("pb_xr", [self.sb(st, "pb_xr%d" % i, [128, 4, D], F32) for i in range(2)])
            tm = Ring("pb_tm", [self.sb(st, "pb_tm%d" % i, [128, 512], F32) for i in range(2)])
            xo = Ring("pb_xo", [self.sb(st, "pb_xo%d" % i, [128, 4, D], F32) for i in range(2)])
            pd = Ring("pb_pd", [self.ps(st, "pb_pd%d" % i, [128, 512], F32) for i in range(3)])
            pm = Ring("pb_pm", [self.ps(st, "pb_pm%d" % i, [128, 512], F32) for i in range(3)])
            for t in range(nt):
                var = 0 if t == 0 else (2 if t == nt - 1 else 1)
                if nt == 1:
                    raise NotImplementedError
                h6, kh = hb.next()
                blks = [r for r in range(6) if 0 <= t * 4 - 1 + r < nblk]
                r0, r1 = blks[0], blks[-1] + 1
                g0 = t * 4 - 1 + r0
                tiles = sorted(set((g0 + i) // 4 for i in range(r1 - r0)))
                self.dma("sp", h6[:, r0:r1, :],
                         H[g0 * 128:(g0 + r1 - r0) * 128, :].rearrange("(b p) d -> p b d", p=128),
                         reads=[("p_h", pi, x) for x in tiles], writes=[kh])
                b_, kb = bt.next()
                self.dma("sp", b_[:], self.wb["pool_bt"][var * 3072:(var + 1) * 3072, :].rearrange("(q p) n -> p q n", p=128),
                         reads=[("wb", "pool_bt")], writes=[kb])
                rc_, krc = rc.next()
                self.dma("sp", rc_[:], self.c["pool_rc"][var * 512:(var + 1) * 512, :].rearrange("(g p) n -> p g n", p=128),
                         reads=[], writes=[krc])
                x, kx = xr.next()
                self.dma("sp", x[:], src[t * T:(t + 1) * T, :].rearrange("(b p) d -> p b d", p=128),
                         reads=[(sname, pi, t)], writes=[kx])
                d, kd = dT.next()
                for c in range(8):
                    g = c // 2
                    p, kp = pd.next()
                    for i, r in enumerate(blks):
                        self.mm(p[:], h6[:, r, c * 128:(c + 1) * 128], b_[:, g * 6 + r, :], i == 0, i == len(blks) - 1,
                                reads=[kh, kb], writes=[kp])
                    self.tt("dve", d[:, c, :], p[:], rc_[:, g, :], ALU.mult, reads=[kp, krc], writes=[(kd, c)])
                y, ky = xo.next()
                for b in range(4):
                    for half in range(2):
                        p, kp = pm.next()
                        for gg in range(2):
                            g = half * 2 + gg
                            for cc_ in range(2):
                                self.mm(p[:, gg * 256:(gg + 1) * 256], d[:, 2 * g + cc_, b * 128:(b + 1) * 128],
                                        wp[:, 2 * g + cc_, :], cc_ == 0, cc_ == 1,
                                        reads=[(kd, 2 * g + cc_), "pb_w"], writes=[kp])
                        m, km = tm.next()
                        self.tt("dve", m[:], p[:], srep[:, half * 512:(half + 1) * 512], ALU.mult,
                                reads=[kp, "pb_s"], writes=[km])
                        self.tt("pool", y[:, b, half * 512:(half + 1) * 512], m[:], x[:, b, half * 512:(half + 1) * 512],
                                ALU.add, reads=[km, kx], writes=[(ky, b, half)])
                self.dma("pool", self.X[pi][t * T:(t + 1) * T, :].rearrange("(b p) d -> p b d", p=128), y[:],
                         reads=[(ky, b, hf) for b in range(4) for hf in range(2)], writes=[("X%d" % pi, pi, t)])
            self.phase_end()

    def ret_tables(self, st, want_q=True, want_m=True):
        Tb = {}
        raw = self.load_c(st, "rt_raw", "decay_rep", [128, 8], self.c["decay_rep"][:, :])
        dmask = self.load_c(st, "rt_dm", "dmask", [128, 256], self.c["dmask"][:, :])
        ind = self.load_c(st, "rt_ind", "ind", [128, 256], self.c["ind"][:, :])
        e12 = self.load_c(st, "rt_e12", "e12", [128, 1024], self.c["e12"][:, :]) if want_q else None
        pcol = self.load_c(st, "rt_pc", "pcol", [128, 2], self.c["pcol"][:, :])
        lg = self.sb(st, "rt_lg", [128, 8], F32)
        tmp = self.sb(st, "rt_tmp", [128, 8], F32)
        self.act(tmp[:], raw[:], AF.Exp, reads=["rt_raw"], writes=["rt_tmp"])
        self.act(tmp[:], tmp[:], AF.Ln, reads=["rt_tmp"], writes=["rt_tmp"], bias=1.0)
        self.ts("dve", lg[:], tmp[:], -1.0, None, ALU.mult, None, reads=["rt_tmp"], writes=["rt_lg"])
        mask = self.sb(st, "rt_mask", [128, 2, 512], F32) if want_m else None
        mtmp = self.sb(st, "rt_mtmp", [128, 2, 512], F32) if want_m else None
        qdec = self.sb(st, "rt_qdec", [128, 8, 512], F32) if want_q else None
        kdec = self.sb(st, "rt_kdec", [128, 8], F32)
        cdec = self.sb(st, "rt_cdec", [128, 8], F32)
        for d in range(2):
            for h in range(4):
                col = d * 4 + h
                if want_m:
                    self.act(mtmp[:, d, h * 128:(h + 1) * 128], dmask[:, d * 128:(d + 1) * 128], AF.Exp,
                             reads=["rt_dm", "rt_lg"], writes=[("rt_mtmp", col)], scale=lg[:, col:col + 1])
                    self.tt("dve", mask[:, d, h * 128:(h + 1) * 128], mtmp[:, d, h * 128:(h + 1) * 128],
                            ind[:, d * 128:(d + 1) * 128], ALU.mult, reads=[("rt_mtmp", col), "rt_ind"],
                            writes=[("rt_mask", col)])
                if want_q:
                    self.act(qdec[:, col, :], e12[:, d * 512:(d + 1) * 512], AF.Exp, reads=["rt_e12", "rt_lg"],
                             writes=[("rt_qdec", col)], scale=lg[:, col:col + 1])
                self.act(kdec[:, col:col + 1], pcol[:, d:d + 1], AF.Exp, reads=["rt_pc", "rt_lg"],
                         writes=[("rt_kdec0", col)], scale=lg[:, col:col + 1])
        self.ts("dve", kdec[:], kdec[:], 1.0 / 16.0, None, ALU.mult, None,
                reads=[("rt_kdec0", c) for c in range(8)], writes=["rt_kdec"])
        self.act(cdec[:], lg[:], AF.Exp, reads=["rt_lg"], writes=["rt_cdec"], scale=128.0)
        Tb.update(mask=mask, qdec=qdec, kdec=kdec, cdec=cdec)
        Tb["mask_keys"] = [("rt_mask", c) for c in range(8)]
        return Tb

    def ret_layer(self, li, pi, src, sname):
        L = self.piece_lens[pi]
        nt = L // T
        RQ = self.scratch("r_q%d" % pi, [8, 128, L])
        RQF = self.scratch("r_qf%d" % pi, [8, 128, L])
        RQB = self.scratch("r_qb%d" % pi, [8, 128, L])
        RKT = self.scratch("r_kt%d" % pi, [8, 128, L])
        RKF = self.scratch("r_kf%d" % pi, [L, 1024])
        RKB = self.scratch("r_kb%d" % pi, [L, 1024])
        RV = self.scratch("r_v%d" % pi, [L, 2048])
        RG = self.scratch("r_g%d" % pi, [L, 2048])
        GO = self.scratch("r_go%d" % pi, [L, 2048])
        OF = self.scratch("r_of%d" % pi, [L, 2048], F32)
        tok = lambda A, t: A[t * T:(t + 1) * T, :].rearrange("(b p) d -> p b d", p=128)

        st = self.phase_begin()
        with st:
            R = self.front_alloc(st)
            Tb = self.ret_tables(st, want_m=False)
            wq = self.load_w(st, "ra_wq", "r_wq", 8, 1024)
            wk = self.load_w(st, "ra_wk", "r_wk", 8, 1024)
            gcol = self.load_c(st, "ra_gcol", "nmix_col", [128, 8], self.c["nmix_col"][:, li * 8:(li + 1) * 8])
            hT = Ring("ra_hT", [self.sb(st, "ra_hT%d" % i, [128, 8, T], BF16) for i in range(2)])
            cs = Ring("ra_cs", [self.sb(st, "ra_cs%d" % i, [128, T], F32) for i in range(2)])
            sn = Ring("ra_sn", [self.sb(st, "ra_sn%d" % i, [128, T], F32) for i in range(2)])
            tA = Ring("ra_tA", [self.sb(st, "ra_tA%d" % i, [128, T], F32) for i in range(4)])
            tB = Ring("ra_tB", [self.sb(st, "ra_tB%d" % i, [128, T], F32) for i in range(4)])
            oo = Ring("ra_oo", [self.sb(st, "ra_oo%d" % i, [128, T], F32) for i in range(4)])
            ob = Ring("ra_ob", [self.sb(st, "ra_ob%d" % i, [128, T], BF16) for i in range(8)])
            kTt = self.sb(st, "ra_kTt", [128, 8, T], BF16)
            ktm = Ring("ra_ktm", [self.sb(st, "ra_ktm%d" % i, [128, 4, 1024], BF16) for i in range(2)])
            pp = Ring("ra_pp", [self.ps(st, "ra_pp%d" % i, [128, 512], F32) for i in range(4)])
            ptk = Ring("ra_ptk", [self.ps(st, "ra_ptk%d" % i, [128, 1024], BF16) for i in range(2)])
            for t in range(nt):
                h, hk = hT.next()
                self.front(R, src, sname, pi, t, gcol, "ra_gcol", h, hk)
                hreads = [(hk, c) for c in range(8)]
                c_, kc_ = cs.next()
                s_, ks_ = sn.next()
                self.dma("sp", c_[:], self.c["cosr"][:, t * T:(t + 1) * T], reads=[], writes=[kc_])
                self.dma("sp", s_[:], self.c["sinr"][:, t * T:(t + 1) * T], reads=[], writes=[ks_])
                for which in range(2):
                    w = wq if which == 0 else wk
                    wkey = "ra_wq" if which == 0 else "ra_wk"
                    for hh in range(4):
                        P12 = []
                        for half in range(2):
                            p, kp = pp.next()
                            oc = hh * 2 + half
                            for k in range(8):
                                self.mm(p[:], w[:, k, oc * 128:(oc + 1) * 128], h[:, k, :], k == 0, k == 7,
                                        reads=hreads + [wkey], writes=[kp])
                            P12.append((p, kp))
                        (p1, k1), (p2, k2) = P12
                        a1, ka1 = tA.next()
                        b1, kb1 = tB.next()
                        a2, ka2 = tA.next()
                        b2, kb2 = tB.next()
                        self.tt("dve", a1[:], p1[:], c_[:], ALU.mult, reads=[k1, kc_], writes=[ka1])
                        self.tt("dve", b1[:], p2[:], s_[:], ALU.mult, reads=[k2, ks_], writes=[kb1])
                        self.tt("dve", a2[:], p1[:], s_[:], ALU.mult, reads=[k1, ks_], writes=[ka2])
                        self.tt("dve", b2[:], p2[:], c_[:], ALU.mult, reads=[k2, kc_], writes=[kb2])
                        o1, ko1 = oo.next()
                        o2, ko2 = oo.next()
                        self.tt("pool", o1[:], a1[:], b1[:], ALU.subtract, reads=[ka1, kb1], writes=[ko1])
                        self.tt("pool", o2[:], a2[:], b2[:], ALU.add, reads=[ka2, kb2], writes=[ko2])
                        for half, (o, ko) in enumerate(((o1, ko1), (o2, ko2))):
                            oc = hh * 2 + half
                            if which == 0:
                                pb_, kpb = ob.next()
                                self.cp("act", pb_[:], o[:], reads=[ko], writes=[kpb])
                                self.dma("pool", RQ[oc, :, t * T:(t + 1) * T], pb_[:], reads=[kpb],
                                         writes=[("r_q", pi, oc, t)])
                                for d, RQD, nm in ((0, RQF, "r_qf"), (1, RQB, "r_qb")):
                                    pd_, kpd = ob.next()
                                    self.tt("pool", pd_[:], o[:], Tb["qdec"][:, d * 4 + hh, :], ALU.mult,
                                            reads=[ko, ("rt_qdec", d * 4 + hh)], writes=[kpd])
                                    self.dma("pool", RQD[oc, :, t * T:(t + 1) * T], pd_[:], reads=[kpd],
                                             writes=[(nm, pi, oc, t)])
                            else:
                                self.cp("act", kTt[:, oc, :], o[:], reads=[ko], writes=[("ra_kTt", oc)])
                                self.dma("pool", RKT[oc, :, t * T:(t + 1) * T], kTt[:, oc, :], reads=[("ra_kTt", oc)],
                                         writes=[("r_kt", pi, oc, t)])
                kf, kkf = ktm.next()
                kb, kkb = ktm.next()
                for b in range(4):
                    pt, kpt = ptk.next()
                    for oc in range(8):
                        self.tr(pt[:, oc * 128:(oc + 1) * 128], kTt[:, oc, b * 128:(b + 1) * 128],
                                reads=[("ra_kTt", oc)], writes=[kpt])
                    for hh in range(4):
                        self.act(kf[:, b, hh * 256:(hh + 1) * 256], pt[:, hh * 256:(hh + 1) * 256], AF.Copy,
                                 reads=[kpt, "rt_kdec"], writes=[(kkf, b, hh)], scale=Tb["kdec"][:, hh:hh + 1])
                        self.act(kb[:, b, hh * 256:(hh + 1) * 256], pt[:, hh * 256:(hh + 1) * 256], AF.Copy,
                                 reads=[kpt, "rt_kdec"], writes=[(kkb, b, hh)], scale=Tb["kdec"][:, 4 + hh:5 + hh])
                self.dma("pool", tok(RKF, t), kf[:], reads=[(kkf, b, hh) for b in range(4) for hh in range(4)],
                         writes=[("r_kf", pi, t)])
                self.dma("pool", tok(RKB, t), kb[:], reads=[(kkb, b, hh) for b in range(4) for hh in range(4)],
                         writes=[("r_kb", pi, t)])
            self.phase_end()

        st = self.phase_begin()
        with st:
            R = self.front_alloc(st)
            wv = self.load_w(st, "rb_wv", "r_wv", 8, 2048)
            wg = self.load_w(st, "rb_wg", "r_wg", 8, 2048)
            gcol = self.load_c(st, "rb_gcol", "nmix_col", [128, 8], self.c["nmix_col"][:, li * 8:(li + 1) * 8])
            hT = Ring("rb_hT", [self.sb(st, "rb_hT%d" % i, [128, 8, T], BF16) for i in range(2)])
            vt = Ring("rb_vt", [self.sb(st, "rb_vt%d" % i, [128, 4, 2048], BF16) for i in range(2)])
            gt = Ring("rb_gt", [self.sb(st, "rb_gt%d" % i, [128, 4, 2048], BF16) for i in range(2)])
            pp = Ring("rb_pp", [self.ps(st, "rb_pp%d" % i, [128, 512], F32) for i in range(6)])
            flip = 0
            for t in range(nt):
                h, hk = hT.next()
                self.front(R, src, sname, pi, t, gcol, "rb_gcol", h, hk)
                hreads = [(hk, c) for c in range(8)]
                v, kv = vt.next()
                g, kg = gt.next()
                for b in range(4):
                    for cc in range(4):
                        p, kp = pp.next()
                        for k in range(8):
                            self.mm(p[:], h[:, k, b * 128:(b + 1) * 128], wv[:, k, cc * 512:(cc + 1) * 512], k == 0, k == 7,
                                    reads=hreads + ["rb_wv"], writes=[kp])
                        eng = "dve" if flip % 2 == 0 else "act"
                        flip += 1
                        self.cp(eng, v[:, b, cc * 512:(cc + 1) * 512], p[:], reads=[kp], writes=[(kv, b, cc)])
                        p, kp = pp.next()
                        for k in range(8):
                            self.mm(p[:], h[:, k, b * 128:(b + 1) * 128], wg[:, k, cc * 512:(cc + 1) * 512], k == 0, k == 7,
                                    reads=hreads + ["rb_wg"], writes=[kp])
                        self.act(g[:, b, cc * 512:(cc + 1) * 512], p[:], AF.Silu, reads=[kp], writes=[(kg, b, cc)])
                self.dma("pool", tok(RV, t), v[:], reads=[(kv, b, cc) for b in range(4) for cc in range(4)],
                         writes=[("r_v", pi, t)])
                self.dma("pool", tok(RG, t), g[:], reads=[(kg, b, cc) for b in range(4) for cc in range(4)],
                         writes=[("r_g", pi, t)])
            self.phase_end()

        for d in range(2):
            st = self.phase_begin()
            with st:
                Tb = self.ret_tables(st, want_q=False)
                RQD = RQF if d == 0 else RQB
                RKD = RKF if d == 0 else RKB
                nqd = "r_qf" if d == 0 else "r_qb"
                nkd = "r_kf" if d == 0 else "r_kb"
                qt = Ring("rs_qt", [self.sb(st, "rs_qt%d" % i, [128, 8, T], BF16) for i in range(2)])
                kt = Ring("rs_kt", [self.sb(st, "rs_kt%d" % i, [128, 8, T], BF16) for i in range(2)])
                qd = Ring("rs_qd", [self.sb(st, "rs_qd%d" % i, [128, 8, T], BF16) for i in range(2)])
                km = Ring("rs_km", [self.sb(st, "rs_km%d" % i, [128, 4, 1024], BF16) for i in range(2)])
                vv = Ring("rs_vv", [self.sb(st, "rs_vv%d" % i, [128, 4, 2048], BF16) for i in range(2)])
                Sf = self.sb(st, "rs_Sf", [128, 8, 512], F32)
                Sb = self.sb(st, "rs_Sb", [128, 8, 512], BF16)
                am = Ring("rs_am", [self.sb(st, "rs_am%d" % i, [128, 512], BF16) for i in range(2)])
                of = Ring("rs_of", [self.sb(st, "rs_of%d" % i, [128, 2048], F32) for i in range(2)])
                patt = Ring("rs_patt", [self.ps(st, "rs_patt%d" % i, [128, 512], F32) for i in range(2)])
                po = Ring("rs_po", [self.ps(st, "rs_po%d" % i, [128, 512], F32) for i in range(2)])
                pd = Ring("rs_pd", [self.ps(st, "rs_pd%d" % i, [128, 512], F32) for i in range(4)])
                if d == 1:
                    ofl = Ring("rs_ofl", [self.sb(st, "rs_ofl%d" % i, [128, 2048], F32) for i in range(2)])
                    gl = Ring("rs_gl", [self.sb(st, "rs_gl%d" % i, [128, 2048], BF16) for i in range(2)])
                    on = Ring("rs_on", [self.sb(st, "rs_on%d" % i, [128, 2048], F32) for i in range(1)])
                    go = Ring("rs_go", [self.sb(st, "rs_go%d" % i, [128, 2048], BF16) for i in range(2)])
                    s1 = Ring("rs_s1", [self.sb(st, "rs_s1%d" % i, [128, 4], F32) for i in range(2)])
                    s2 = Ring("rs_s2", [self.sb(st, "rs_s2%d" % i, [128, 4], F32) for i in range(2)])
                    mu = Ring("rs_mu", [self.sb(st, "rs_mu%d" % i, [128, 4], F32) for i in range(2)])
                    vr = Ring("rs_vr", [self.sb(st, "rs_vr%d" % i, [128, 4], F32) for i in range(2)])
                    junk = self.sb(st, "rs_junk", [128, 512], BF16)
                self.S.op("dve", lambda e, Sf=Sf: e.memset(Sf[:], 0.0), [], [("rs_Sf", c) for c in range(8)])
                self.S.op("pool", lambda e, Sb=Sb: e.memset(Sb[:], 0.0), [], [("rs_Sb", c) for c in range(8)])
                flip = 0
                trange = range(nt) if d == 0 else range(nt - 1, -1, -1)
                for t in trange:
                    q_, kq = qt.next()
                    k_, kk = kt.next()
                    qd_, kqd = qd.next()
                    km_, kkm = km.next()
                    v_, kv = vv.next()
                    self.dma("sp", q_[:], RQ[:, :, t * T:(t + 1) * T].rearrange("c p n -> p c n"),
                             reads=[("r_q", pi, oc, t) for oc in range(8)], writes=[kq])
                    self.dma("sp", k_[:], RKT[:, :, t * T:(t + 1) * T].rearrange("c p n -> p c n"),
                             reads=[("r_kt", pi, oc, t) for oc in range(8)], writes=[kk])
                    self.dma("sp", qd_[:], RQD[:, :, t * T:(t + 1) * T].rearrange("c p n -> p c n"),
                             reads=[(nqd, pi, oc, t) for oc in range(8)], writes=[kqd])
                    self.dma("sp", km_[:], tok(RKD, t), reads=[(nkd, pi, t)], writes=[kkm])
                    self.dma("sp", v_[:], tok(RV, t), reads=[("r_v", pi, t)], writes=[kv])
                    brange = range(4) if d == 0 else range(3, -1, -1)
                    for b in brange:
                        n = t * 4 + b
                        sl = slice(b * 128, (b + 1) * 128)
                        pa, kpa = patt.next()
                        for hh in range(4):
                            for dc in range(2):
                                self.mm(pa[:, hh * 128:(hh + 1) * 128], k_[:, hh * 2 + dc, sl], q_[:, hh * 2 + dc, sl],
                                        dc == 0, dc == 1, reads=[kk, kq], writes=[kpa])
                        a_, ka = am.next()
                        self.tt("dve", a_[:], pa[:], Tb["mask"][:, d, :], ALU.mult, reads=[kpa] + Tb["mask_keys"], writes=[ka])
                        if d == 0:
                            o_, ko = of.next()
                        else:
                            ol, kol = ofl.next()
                            self.dma("sp", ol[:], OF[n * 128:(n + 1) * 128, :], reads=[("r_of", pi, n)], writes=[kol])
                            g_, kg = gl.next()
                            self.dma("sp", g_[:], RG[n * 128:(n + 1) * 128, :], reads=[("r_g", pi, t)], writes=[kg])
                            o_, ko = of.next()
                            s1_, ks1 = s1.next()
                            s2_, ks2 = s2.next()
                        for hh in range(4):
                            p, kp = po.next()
                            self.mm(p[:], a_[:, hh * 128:(hh + 1) * 128], v_[:, b, hh * 512:(hh + 1) * 512], True, False,
                                    reads=[ka, kv], writes=[kp])
                            for dc in range(2):
                                self.mm(p[:], qd_[:, hh * 2 + dc, sl], Sb[:, hh * 2 + dc, :], False, dc == 1,
                                        reads=[kqd, ("rs_Sb", hh * 2 + dc)], writes=[kp])
                            if d == 0:
                                self.cp("act", o_[:, hh * 512:(hh + 1) * 512], p[:], reads=[kp], writes=[(ko, hh)])
                            else:
                                self.tt("dve", o_[:, hh * 512:(hh + 1) * 512], p[:], ol[:, hh * 512:(hh + 1) * 512], ALU.add,
                                        reads=[kp, kol], writes=[(ko, hh)])
                                self.act(junk[:], o_[:, hh * 512:(hh + 1) * 512], AF.Copy, reads=[(ko, hh)],
                                         writes=[(ks1, hh)], accum=s1_[:, hh:hh + 1])
                                self.act(junk[:], o_[:, hh * 512:(hh + 1) * 512], AF.Square, reads=[(ko, hh)],
                                         writes=[(ks2, hh)], accum=s2_[:, hh:hh + 1])
                            for dc in range(2):
                                pq, kpq = pd.next()
                                self.mm(pq[:], km_[:, b, hh * 256 + dc * 128:hh * 256 + (dc + 1) * 128],
                                        v_[:, b, hh * 512:(hh + 1) * 512], True, True, reads=[kkm, kv], writes=[kpq])
                                c8 = hh * 2 + dc
                                self.stt(Sf[:, c8, :], Sf[:, c8, :], Tb["cdec"][:, d * 4 + hh:d * 4 + hh + 1], pq[:],
                                         ALU.mult, ALU.add, reads=[kpq, ("rs_Sf", c8), "rt_cdec"], writes=[("rs_Sf", c8)])
                                self.cp("act", Sb[:, c8, :], Sf[:, c8, :], reads=[("rs_Sf", c8)], writes=[("rs_Sb", c8)])
                        if d == 0:
                            self.dma("pool", OF[n * 128:(n + 1) * 128, :], o_[:], reads=[(ko, hh) for hh in range(4)],
                                     writes=[("r_of", pi, n)])
                        else:
                            mu_, kmu = mu.next()
                            vr_, kvr = vr.next()
                            self.ts("dve", mu_[:], s1_[:], 1.0 / 512, None, ALU.mult, None,
                                    reads=[(ks1, hh) for hh in range(4)], writes=[kmu])
                            self.tt("dve", vr_[:], mu_[:], mu_[:], ALU.mult, reads=[kmu], writes=[kvr])
                            self.stt(vr_[:], s2_[:], 1.0 / 512, vr_[:], ALU.mult, ALU.subtract,
                                     reads=[(ks2, hh) for hh in range(4)] + [kvr], writes=[kvr])
                            self.act(vr_[:], vr_[:], AF.Ln, reads=[kvr], writes=[kvr], bias=EPS)
                            self.act(vr_[:], vr_[:], AF.Exp, reads=[kvr], writes=[kvr], scale=-0.5)
                            on_, kon = on.next()
                            go_, kgo = go.next()
                            for hh in range(4):
                                self.ts("dve", on_[:, hh * 512:(hh + 1) * 512], o_[:, hh * 512:(hh + 1) * 512],
                                        mu_[:, hh:hh + 1], vr_[:, hh:hh + 1], ALU.subtract, ALU.mult,
                                        reads=[(ko, hh), kmu, kvr], writes=[(kon, hh)])
                                self.tt("pool", go_[:, hh * 512:(hh + 1) * 512], on_[:, hh * 512:(hh + 1) * 512],
                                        g_[:, hh * 512:(hh + 1) * 512], ALU.mult, reads=[(kon, hh), kg], writes=[(kgo, hh)])
                            self.dma("pool", GO[n * 128:(n + 1) * 128, :], go_[:], reads=[(kgo, hh) for hh in range(4)],
                                     writes=[("r_go", pi, n)])
                self.phase_end()

        st = self.phase_begin()
        with st:
            wo = self.load_w(st, "rd_wo", "r_wo", 16, 1024)
            gin = Ring("rd_gin", [self.sb(st, "rd_gin%d" % i, [128, 4, 2048], BF16) for i in range(2)])
            goT = Ring("rd_goT", [self.sb(st, "rd_goT%d" % i, [128, 16, T], BF16) for i in range(2)])
            xr = Ring("rd_xr", [self.sb(st, "rd_xr%d" % i, [128, 4, D], F32) for i in range(2)])
            xo = Ring("rd_xo", [self.sb(st, "rd_xo%d" % i, [128, 4, D], F32) for i in range(2)])
            ptr = Ring("rd_ptr", [self.ps(st, "rd_ptr%d" % i, [128, 1024], BF16) for i in range(3)])
            pp = Ring("rd_pp", [self.ps(st, "rd_pp%d" % i, [128, 512], F32) for i in range(4)])
            for t in range(nt):
                gi, kgi = gin.next()
                self.dma("sp", gi[:], tok(GO, t), reads=[("r_go", pi, t * 4 + b) for b in range(4)], writes=[kgi])
                x, kx = xr.next()
                self.dma("sp", x[:], tok(src, t), reads=[(sname, pi, t)], writes=[kx])
                gT, kgT = goT.next()
                for cp_ in range(8):
                    pt, kpt = ptr.next()
                    for cc in range(2):
                        c = cp_ * 2 + cc
                        for b in range(4):
                            self.tr(pt[:, cc * 512 + b * 128:cc * 512 + (b + 1) * 128], gi[:, b, c * 128:(c + 1) * 128],
                                    reads=[kgi], writes=[kpt])
                    self.cp("act", gT[:, cp_ * 2:cp_ * 2 + 2, :], pt[:].rearrange("p (c n) -> p c n", c=2),
                            reads=[kpt], writes=[(kgT, cp_)])
                y, ky = xo.next()
                for b in range(4):
                    for half in range(2):
                        pq, kpq = pp.next()
                        for c in range(16):
                            self.mm(pq[:], gT[:, c, b * 128:(b + 1) * 128], wo[:, c, half * 512:(half + 1) * 512],
                                    c == 0, c == 15, reads=[(kgT, c // 2), "rd_wo"], writes=[kpq])
                        self.tt("dve", y[:, b, half * 512:(half + 1) * 512], pq[:], x[:, b, half * 512:(half + 1) * 512],
                                ALU.add, reads=[kpq, kx], writes=[(ky, b, half)])
                self.dma("pool", tok(self.X[pi], t), y[:],
                         reads=[(ky, b, hf) for b in range(4) for hf in range(2)], writes=[("X%d" % pi, pi, t)])
            self.phase_end()


def rope_tables(lmax):
    pos = np.arange(lmax, dtype=np.float32)
    inv = (1.0 / (np.float32(10000.0) ** (np.arange(32, dtype=np.float32) * np.float32(2.0) / np.float32(64)))).astype(np.float32)
    ang = (pos[:, None] * inv[None, :]).astype(np.float32)
    cos, sin = np.cos(ang).astype(np.float32), np.sin(ang).astype(np.float32)
    ccm = np.zeros((128, lmax), np.float32)
    ssm = np.zeros((128, lmax), np.float32)
    for r in range(128):
        f = r % 32
        ccm[r] = cos[:, f]
        ssm[r] = sin[:, f] * (-1.0 if (r % 64) < 32 else 1.0)
    inv2 = (1.0 / (np.float32(10000.0) ** (np.arange(128, dtype=np.float32) * np.float32(2.0) / np.float32(256)))).astype(np.float32)
    ang2 = (pos[:, None] * inv2[None, :]).astype(np.float32)
    cosr = np.ascontiguousarray(np.cos(ang2).astype(np.float32).T)
    sinr = np.ascontiguousarray(np.sin(ang2).astype(np.float32).T)
    return ccm, ssm, cosr, sinr


def pool_tables():
    wins = (2, 4, 8, 16)
    bt = np.zeros((3, 4, 6, 128, 512), np.float32)
    rc = np.zeros((3, 4, 128, 512), np.float32)
    for var in range(3):
        for g, w in enumerate(wins):
            for t in range(512):
                lo, hi = t - w // 2, t + w // 2 - 1
                if var == 0:
                    lo = max(lo, 0)
                if var == 2:
                    hi = min(hi, 511)
                cnt = hi - lo + 1
                rc[var, g, :, t] = 1.0 / cnt
                for jj in range(lo, hi + 1):
                    ja = jj + 128
                    bt[var, g, ja // 128, ja % 128, t] = 1.0
                ja = t + 128
                bt[var, g, ja // 128, ja % 128, t] = 1.0 - cnt
    return bt.reshape(3 * 4 * 6 * 128, 512), rc.reshape(3 * 4 * 128, 512)


def col_layout(v):
    v = np.asarray(v, np.float32)
    return np.ascontiguousarray(v.reshape(-1, 128).T)


def rep128(v):
    v = np.asarray(v, np.float32).reshape(1, -1)
    return np.ascontiguousarray(np.broadcast_to(v, (128, v.shape[1])))


def host_shared(inp, lmax):
    f = lambda a: np.ascontiguousarray(np.asarray(a, np.float32))
    sh = {}
    for j in range(2):
        wuq = f(inp["mla_w_uq"][j]).reshape(384, 8, 192)
        nope = wuq[:, :, :128].reshape(384, 1024)
        ra = wuq[:, :, 128:].reshape(384, 512)
        rb = np.concatenate([wuq[:, :, 160:192], wuq[:, :, 128:160]], axis=2).reshape(384, 512)
        sh["m%d_wq" % j] = np.ascontiguousarray(np.concatenate([nope, ra, rb], axis=1))
        wdkv = f(inp["mla_w_dkv"][j])
        sh["m%d_wdq" % j] = f(inp["mla_w_dq"][j])
        sh["m%d_wdkvc" % j] = np.ascontiguousarray(wdkv[:, :256])
        sh["m%d_wdkvr" % j] = np.ascontiguousarray(
            np.concatenate([wdkv[:, 256:320], wdkv[:, 288:320], wdkv[:, 256:288]], axis=1))
        wukv = f(inp["mla_w_ukv"][j]).reshape(256, 8, 256)
        sh["m%d_wukv" % j] = np.ascontiguousarray(
            np.concatenate([wukv[:, :, :128].reshape(256, 1024), wukv[:, :, 128:].reshape(256, 1024)], axis=1))
        sh["m%d_wo" % j] = f(inp["mla_w_o"][j])
    sh["r_wq"] = f(inp["ret_w_q"][0])
    sh["r_wk"] = f(inp["ret_w_k"][0])
    sh["r_wv"] = f(inp["ret_w_v"][0])
    sh["r_wg"] = f(inp["ret_w_g"][0])
    sh["r_wo"] = f(inp["ret_w_o"][0])
    sh["p_w"] = f(inp["pool_w"][0]).reshape(1024, 256)
    for i in range(4):
        sh["w1_%d" % i] = f(inp["mlp_w1"][i])
        sh["w2_%d" % i] = f(inp["mlp_w2"][i])
    sh["ident"] = np.eye(128, dtype=np.float32)
    sh["ones"] = np.ones((128, 128), np.float32)
    bt, rc = pool_tables()
    sh["pool_bt"] = bt
    sh["pool_rc"] = rc
    sh["nmix_col"] = np.concatenate([col_layout(inp["norm_mix"][i]) for i in range(4)], axis=1)
    sh["nffn_col"] = np.concatenate([col_layout(inp["norm_ffn"][i]) for i in range(4)], axis=1)
    sh["nmix2_rep"] = rep128(inp["norm_mix"][2])
    sh["final_rep"] = rep128(inp["final_norm"])
    sh["pscale_rep"] = rep128(inp["pool_scale"][0])
    sh["qn_col"] = np.concatenate([col_layout(inp["mla_q_norm"][j]) for j in range(2)], axis=1)
    sh["kvn_col"] = np.concatenate([col_layout(inp["mla_kv_norm"][j]) for j in range(2)], axis=1)
    sh["decay_rep"] = rep128(np.concatenate([np.asarray(inp["ret_decay_fwd"][0]), np.asarray(inp["ret_decay_bwd"][0])]))
    ccm, ssm, cosr, sinr = rope_tables(lmax)
    sh["ccm"], sh["ssm"], sh["cosr"], sh["sinr"] = ccm, ssm, cosr, sinr
    ii = np.arange(128, dtype=np.float32)
    df = np.maximum(ii[None, :] - ii[:, None], 0.0)
    db = np.maximum(ii[:, None] - ii[None, :], 0.0)
    sh["dmask"] = np.ascontiguousarray(np.concatenate([df, db], axis=1))
    indf = (ii[None, :] >= ii[:, None]).astype(np.float32) / 16.0
    indb = (ii[:, None] >= ii[None, :]).astype(np.float32) / 16.0
    sh["ind"] = np.ascontiguousarray(np.concatenate([indf, indb], axis=1))
    il = (np.arange(512) % 128).astype(np.float32)
    sh["e12"] = rep128(np.concatenate([il + 1.0, 128.0 - il]))
    sh["pcol"] = np.ascontiguousarray(np.stack([127.0 - ii, ii], axis=1).astype(np.float32))
    return sh


_CACHE = {}


def get_prog(piece_lens, layers, final, lmax):
    key = (tuple(piece_lens), tuple(layers), final, lmax)
    if key not in _CACHE:
        p = Prog(piece_lens, layers, final, lmax)
        p.build()
        _CACHE[key] = p
    return _CACHE[key]


def run_pieces(inp, core_pieces, layers=(0, 1, 2, 3), final=True):
    piece_lens = [a.shape[0] for a in core_pieces[0]]
    lmax = max(piece_lens)
    prog = get_prog(piece_lens, layers, final, lmax)
    sh = host_shared(inp, lmax)
    in_maps = []
    for cp in core_pieces:
        m = dict(sh)
        for pi, a in enumerate(cp):
            m["x%d" % pi] = np.ascontiguousarray(a, dtype=np.float32)
        m = {k: m[k] for k in prog.inputs}
        in_maps.append(m)
    res = run_bass_kernel_spmd(prog.nc, in_maps, core_ids=list(range(len(core_pieces))))
    return [[np.asarray(r["y%d" % pi]) for pi in range(len(piece_lens))] for r in res.results]


def kernel(**inp):
    xp = np.asarray(inp["x_prompt"], np.float32)
    xs = np.asarray(inp["x_sample"], np.float32)
    zero = np.zeros_like(xs[0])
    core_pieces = []
    for c in range(8):
        core_pieces.append([xs[c] if c < 2 else zero, xp[2 * c], xp[2 * c + 1]])
    outs = run_pieces(inp, core_pieces)
    y_prompt = np.stack([outs[c][1 + k] for c in range(8) for k in range(2)], axis=0)
    y_sample = np.stack([outs[0][0], outs[1][0]], axis=0)
    return (y_prompt.astype(np.float32), y_sample.astype(np.float32))
```
